# Optimizing a Trainium2 kernel written in Bass

```python
import jax, jax.numpy as jnp
from jax import lax
import numpy as np

D_MODEL = 1024
BATCH = 16
SEQ = 256
DEPTH = 2
DEC_BATCH = 4
DEC_SEQ = 2048
PAST_LEN = 256

GRID_W = 64
N_MIXERS = 4
GROUP_W = D_MODEL // N_MIXERS
HEAD_DIM = 64
N_HEADS = GROUP_W // HEAD_DIM
NA_WIN_H = 8
NA_WIN_W = 16
Q_BLOCK = 128
CONV_W = 31
RET_CHUNK = 128
RWKV_LORA_W = 64
RWKV_LORA_A = 64
RWKV_LORA_G = 128
RWKV_SHIFT_W = 3
D_FF = 2816
FFN_CONV_W = 3
ROPE_BASE = 10000.0
EPS = 1e-6
RWKV_GN_EPS = 64e-5
NEG_INF = -1e30
SPLIT_SIZES = (GROUP_W,) * 12 + (RWKV_LORA_W, RWKV_LORA_A, RWKV_LORA_G)
D_IN = sum(SPLIT_SIZES)
SPLIT_POINTS = tuple(int(s) for s in np.cumsum(SPLIT_SIZES)[:-1])

kernel_name = "hybrid_na_conformer_retnet_rwkv7_denoise_step"

F32 = jnp.float32


def rms_norm(x, g):
    xf = x.astype(F32)
    y = xf * lax.rsqrt(jnp.mean(xf * xf, axis=-1, keepdims=True) + EPS)
    return (y * g.astype(F32)).astype(x.dtype)


def layer_norm(x, g, b):
    xf = x.astype(F32)
    xc = xf - jnp.mean(xf, axis=-1, keepdims=True)
    return xc * lax.rsqrt(jnp.mean(xc * xc, axis=-1, keepdims=True) + EPS) * g.astype(F32) + b.astype(F32)


def head_norm(x, g, eps):
    xf = x.astype(F32)
    xc = xf - jnp.mean(xf, axis=-1, keepdims=True)
    y = xc * lax.rsqrt(jnp.mean(xc * xc, axis=-1, keepdims=True) + eps)
    return y.reshape(x.shape[0], x.shape[1], -1) * g.astype(F32)


def dwconv(x, w):
    k, ch = w.shape
    return lax.conv_general_dilated(x, w[:, None, :].astype(x.dtype), window_strides=(1,),
                                    padding=[(k // 2, k // 2)], dimension_numbers=("NWC", "WIO", "NWC"),
                                    feature_group_count=ch)


def rev(t):
    return jnp.flip(t, axis=1)


def axial_rope(length):
    t = jnp.arange(length)
    n_freq = HEAD_DIM // 4
    inv = ROPE_BASE ** (-jnp.arange(n_freq, dtype=F32) / n_freq)
    ang = jnp.concatenate([(t // GRID_W).astype(F32)[:, None] * inv,
                           (t % GRID_W).astype(F32)[:, None] * inv], axis=-1)
    return jnp.cos(ang), jnp.sin(ang)


def apply_rope(x, cos, sin):
    xf = x.astype(F32)
    x1, x2 = jnp.split(xf, 2, axis=-1)
    c, s = cos[None, :, None, :], sin[None, :, None, :]
    return jnp.concatenate([x1 * c - x2 * s, x1 * s + x2 * c], axis=-1).astype(x.dtype)


def context_attention(q, k, v):
    B, L, H, dh = q.shape
    qb = q.reshape(B, L // Q_BLOCK, Q_BLOCK, H, dh).transpose(1, 0, 2, 3, 4)

    def block(qi):
        s = jnp.einsum("bqhd,bkhd->bhqk", qi, k).astype(F32)
        p = jax.nn.softmax(s, axis=-1).astype(v.dtype)
        return jnp.einsum("bhqk,bkhd->bqhd", p, v)

    o = lax.map(block, qb)
    return o.transpose(1, 0, 2, 3, 4).reshape(B, L, H, dh)


def neighbourhood_attention(q, k, v, ctx_k, ctx_v, rpb):
    B, L, H, dh = q.shape
    rows = L // GRID_W
    kh = min(NA_WIN_H, rows)
    r = jnp.arange(rows)
    row_idx = jnp.clip(r - kh // 2, 0, rows - kh)[:, None] + jnp.arange(kh)[None, :]
    col = jnp.arange(GRID_W)
    col_start = jnp.clip(col - NA_WIN_W // 2, 0, GRID_W - NA_WIN_W)
    col_in = (col[None, :] >= col_start[:, None]) & (col[None, :] < col_start[:, None] + NA_WIN_W)
    d_row = row_idx - r[:, None] + (NA_WIN_H - 1)
    d_col = jnp.clip(col[None, :] - col[:, None], -(NA_WIN_W - 1), NA_WIN_W - 1) + (NA_WIN_W - 1)
    bias = rpb[:, d_row[:, None, :, None], d_col[None, :, None, :]].astype(F32)
    bias = bias.transpose(1, 0, 2, 3, 4)
    qg = q.reshape(B, rows, GRID_W, H, dh)
    kg = k.reshape(B, rows, GRID_W, H, dh)[:, row_idx]
    vg = v.reshape(B, rows, GRID_W, H, dh)[:, row_idx]
    s_loc = jnp.einsum("brqhd,brikhd->brhqik", qg, kg).astype(F32) + bias[None]
    s_loc = jnp.where(col_in[:, None, :], s_loc, NEG_INF)
    s_ctx = jnp.einsum("brqhd,bkhd->brhqk", qg, ctx_k).astype(F32)
    n_loc = kh * GRID_W
    p = jax.nn.softmax(jnp.concatenate([s_loc.reshape(B, rows, H, GRID_W, n_loc), s_ctx], axis=-1),
                       axis=-1).astype(v.dtype)
    o = (jnp.einsum("brhqik,brikhd->brqhd", p[..., :n_loc].reshape(B, rows, H, GRID_W, kh, GRID_W), vg)
         + jnp.einsum("brhqk,bkhd->brqhd", p[..., n_loc:], ctx_v))
    return o.reshape(B, L, H, dh)


def conformer_conv(a, b, w_dw, ln_g, ln_b):
    h = a * jax.nn.sigmoid(b)
    h = dwconv(h, w_dw)
    return jax.nn.silu(layer_norm(h, ln_g, ln_b))


def retention_scan(q, k, v, log_g, s0):
    B, L, H, dh = q.shape
    n = L // RET_CHUNK

    def chunks(t):
        return t.astype(F32).reshape(B, n, RET_CHUNK, H, dh).transpose(1, 0, 3, 2, 4)

    pos = jnp.arange(RET_CHUNK, dtype=F32)
    diff = pos[:, None] - pos[None, :]
    decay_in = jnp.where(diff >= 0, jnp.exp(jnp.maximum(diff, 0.0) * log_g[:, None, None]), 0.0)
    decay_q = jnp.exp((pos + 1.0) * log_g[:, None])[None, :, :, None]
    decay_k = jnp.exp((RET_CHUNK - 1.0 - pos) * log_g[:, None])[None, :, :, None]
    decay_c = jnp.exp(RET_CHUNK * log_g)[None, :, None, None]

    def step(s, inp):
        qc, kc, vc = inp
        inner = jnp.einsum("bhid,bhjd->bhij", qc, kc) * decay_in
        o = jnp.einsum("bhij,bhjd->bhid", inner, vc) + jnp.einsum("bhid,bhde->bhie", qc, s) * decay_q
        s = s * decay_c + jnp.einsum("bhjd,bhje->bhde", kc * decay_k, vc)
        return s, o

    s, o = lax.scan(step, s0.astype(F32), (chunks(q), chunks(k), chunks(v)))
    return o.transpose(1, 0, 3, 2, 4).reshape(B, L, H, dh), s


def rwkv7_scan(r, decay, k, v, kk, a, s0):
    def step(s, inp):
        r_t, w_t, k_t, v_t, kk_t, a_t = inp
        sa = jnp.einsum("bhvk,bhk->bhv", s, -kk_t)
        s = (s * w_t[:, :, None, :] + sa[..., None] * (kk_t * a_t)[:, :, None, :]
             + v_t[..., None] * k_t[:, :, None, :])
        return s, jnp.einsum("bhvk,bhk->bhv", s, r_t)

    xs = tuple(jnp.swapaxes(t, 0, 1) for t in (r, decay, k, v, kk, a))
    s, o = lax.scan(step, s0.astype(F32), xs)
    return jnp.swapaxes(o, 0, 1), s


def token_mixers(h, p, cache):
    B, L, _ = h.shape
    dt = h.dtype
    latent = cache is not None
    (na_q, na_k, na_v, cv_a, cv_b, rt_q, rt_k, rt_v, rt_g,
     rw_r, rw_k, rw_v, rw_w1, rw_a1, rw_g1) = jnp.split(h @ p["w_in"], SPLIT_POINTS, axis=-1)

    def heads(t):
        return t.reshape(B, L, N_HEADS, HEAD_DIM)

    scale = HEAD_DIM ** -0.5

    a_q, a_k, a_v = heads(na_q) * scale, heads(na_k), heads(na_v)
    if latent:
        o_a = neighbourhood_attention(a_q, a_k, a_v, cache["na_k"], cache["na_v"], p["na_rpb"])
    else:
        o_a = context_attention(a_q, a_k, a_v)
    o_a = o_a.reshape(B, L, GROUP_W)

    o_b = conformer_conv(cv_a, cv_b, p["conv_dw"], p["conv_ln_g"], p["conv_ln_b"])

    c_q, c_k, c_v = heads(rt_q), heads(rt_k) * scale, heads(rt_v)
    if latent:
        cos, sin = axial_rope(L)
        c_q, c_k = apply_rope(c_q, cos, sin), apply_rope(c_k, cos, sin)
        ret0 = cache["ret"].astype(F32)
    else:
        ret0 = jnp.zeros((B, 2, N_HEADS, HEAD_DIM, HEAD_DIM), F32)
    log_g = jax.nn.log_sigmoid(p["ret_decay"].astype(F32))
    o_cf, s_cf = retention_scan(c_q, c_k, c_v, log_g[0], ret0[:, 0])
    o_cb, s_cb = retention_scan(rev(c_q), rev(c_k), rev(c_v), log_g[1], ret0[:, 1])
    o_c = head_norm(o_cf + rev(o_cb), p["ret_gn"], EPS) * jax.nn.silu(rt_g.astype(F32))
    ret_state = jnp.stack([s_cf, s_cb], axis=1)

    d_r, d_k, d_v = jnp.split(dwconv(jnp.concatenate([rw_r, rw_k, rw_v], axis=-1), p["rwkv_shift"]).astype(F32), 3, axis=-1)
    w_low = jnp.tanh(rw_w1.astype(F32))
    a_low = rw_a1.astype(F32)
    gate = jax.nn.sigmoid(rw_g1.astype(F32)) @ p["rwkv_g2"].astype(F32)
    kk = heads(d_k * p["rwkv_kk"])
    kk = kk * lax.rsqrt(jnp.sum(kk * kk, axis=-1, keepdims=True) + 1e-12)
    r_h, v_h, rk_h = heads(d_r), heads(d_v), p["rwkv_rk"].astype(F32).reshape(N_HEADS, HEAD_DIM)
    rwkv0 = cache["rwkv"].astype(F32) if latent else jnp.zeros((B, 2, N_HEADS, HEAD_DIM, HEAD_DIM), F32)
    outs, bonuses, states = [], [], []
    for d in range(2):
        z_w = p["rwkv_w0"][d] + w_low @ p["rwkv_w2"][d]
        decay = heads(jnp.exp(-jnp.exp(-jax.nn.softplus(-z_w) - 0.5)))
        a = jax.nn.sigmoid(p["rwkv_a0"][d] + a_low @ p["rwkv_a2"][d])
        k_d = heads(d_k * (1.0 + (a - 1.0) * p["rwkv_ka"]))
        seqs = (r_h, decay, k_d, v_h, kk, heads(a))
        if d == 1:
            seqs = tuple(rev(t) for t in seqs)
        o, s = rwkv7_scan(*seqs, rwkv0[:, d])
        outs.append(rev(o) if d == 1 else o)
        bonuses.append(jnp.sum(r_h * k_d * rk_h, axis=-1, keepdims=True) * v_h)
        states.append(s)
    o_d = (head_norm(outs[0] + outs[1], p["rwkv_gn"], RWKV_GN_EPS)
           + (bonuses[0] + bonuses[1]).reshape(B, L, GROUP_W)) * gate
    rwkv_state = jnp.stack(states, axis=1)

    mixed = jnp.concatenate([o_a.astype(dt), o_b.astype(dt), o_c.astype(dt), o_d.astype(dt)], axis=-1) @ p["w_out"]
    if latent:
        return mixed, None
    return mixed, (a_k, a_v, ret_state, rwkv_state)


def conv_ffn(h, w_up, w_conv, w_down):
    u = dwconv(h @ w_up, w_conv)
    g, val = jnp.split(u, 2, axis=-1)
    return (jax.nn.silu(g) * val) @ w_down


def trunk_layer(x, mod, p, cache):
    shift1, scale1, gate1, shift2, scale2, gate2 = jnp.split(mod[:, None, :].astype(x.dtype), 6, axis=-1)
    h = rms_norm(x, p["norm_g"][0]) * (1.0 + scale1) + shift1
    m, state = token_mixers(h, p, cache)
    x = x + gate1 * rms_norm(m, p["norm_g"][1])
    h = rms_norm(x, p["norm_g"][2]) * (1.0 + scale2) + shift2
    f = conv_ffn(h, p["ffn_up"], p["ffn_conv"], p["ffn_down"])
    x = x + gate2 * rms_norm(f, p["norm_g"][3])
    return x, state


def setup_inputs(seed: int = 0) -> dict:
    key = jax.random.key(seed)
    ks = iter(jax.random.split(key, 48))

    def nrm(shape, s):
        return s * jax.random.normal(next(ks), shape, F32)

    G, H, dh = GROUP_W, N_HEADS, HEAD_DIM
    ret_base = jnp.log(2.0 ** (5.0 + jnp.arange(H, dtype=F32)) - 1.0)
    center3 = jnp.array([0.0, 1.0, 0.0], F32)[None, :, None]
    return {
        "x_prompt": nrm((BATCH, SEQ, D_MODEL), 1.0),
        "x_sample": nrm((DEC_BATCH, DEC_SEQ, D_MODEL), 1.0),
        "cache_na_k": nrm((DEC_BATCH, DEPTH, PAST_LEN, H, dh), 1.0),
        "cache_na_v": nrm((DEC_BATCH, DEPTH, PAST_LEN, H, dh), 1.0),
        "state_retention": nrm((DEC_BATCH, DEPTH, 2, H, dh, dh), 0.5),
        "state_rwkv": nrm((DEC_BATCH, DEPTH, 2, H, dh, dh), 0.5),
        "c": nrm((DEC_BATCH, D_MODEL), 1.0),
        "c_ctx": nrm((D_MODEL,), 1.0),
        "ada_w": nrm((DEPTH, D_MODEL, 6 * D_MODEL), 0.5 * D_MODEL ** -0.5),
        "ada_b": nrm((DEPTH, 6 * D_MODEL), 0.02),
        "norm_g": 1.0 + nrm((DEPTH, 4, D_MODEL), 0.05),
        "w_in": nrm((DEPTH, D_MODEL, D_IN), D_MODEL ** -0.5),
        "w_out": nrm((DEPTH, N_MIXERS * G, D_MODEL), (N_MIXERS * G) ** -0.5),
        "na_rpb": nrm((DEPTH, H, 2 * NA_WIN_H - 1, 2 * NA_WIN_W - 1), 0.1),
        "conv_dw": nrm((DEPTH, CONV_W, G), CONV_W ** -0.5),
        "conv_ln_g": 1.0 + nrm((DEPTH, G), 0.05),
        "conv_ln_b": nrm((DEPTH, G), 0.02),
        "ret_decay": ret_base + nrm((DEPTH, 2, H), 0.1),
        "ret_gn": 1.0 + nrm((DEPTH, G), 0.05),
        "rwkv_shift": center3 + nrm((DEPTH, RWKV_SHIFT_W, 3 * G), 0.2),
        "rwkv_w0": jnp.linspace(-5.5, -0.5, G, dtype=F32) + nrm((DEPTH, 2, G), 0.1),
        "rwkv_w2": nrm((DEPTH, 2, RWKV_LORA_W, G), 0.1 * RWKV_LORA_W ** -0.5),
        "rwkv_a0": nrm((DEPTH, 2, G), 0.1),
        "rwkv_a2": nrm((DEPTH, 2, RWKV_LORA_A, G), 0.1 * RWKV_LORA_A ** -0.5),
        "rwkv_g2": nrm((DEPTH, RWKV_LORA_G, G), RWKV_LORA_G ** -0.5),
        "rwkv_kk": 0.85 + nrm((DEPTH, G), 0.05),
        "rwkv_ka": 1.0 + nrm((DEPTH, G), 0.05),
        "rwkv_rk": nrm((DEPTH, G), 0.1),
        "rwkv_gn": 1.0 + nrm((DEPTH, G), 0.05),
        "ffn_up": nrm((DEPTH, D_MODEL, 2 * D_FF), D_MODEL ** -0.5),
        "ffn_conv": center3 + nrm((DEPTH, FFN_CONV_W, 2 * D_FF), 0.2),
        "ffn_down": nrm((DEPTH, D_FF, D_MODEL), D_FF ** -0.5),
    }


def reference(x_prompt, x_sample, cache_na_k, cache_na_v, state_retention, state_rwkv, c, c_ctx,
              ada_w, ada_b, norm_g, w_in, w_out, na_rpb, conv_dw, conv_ln_g, conv_ln_b, ret_decay, ret_gn,
              rwkv_shift, rwkv_w0, rwkv_w2, rwkv_a0, rwkv_a2, rwkv_g2, rwkv_kk, rwkv_ka, rwkv_rk, rwkv_gn,
              ffn_up, ffn_conv, ffn_down):
    y_prompt, y_sample = x_prompt, x_sample
    new_k, new_v, new_ret, new_rwkv = [], [], [], []
    for l in range(DEPTH):
        p = {"norm_g": norm_g[l], "w_in": w_in[l], "w_out": w_out[l], "na_rpb": na_rpb[l],
             "conv_dw": conv_dw[l], "conv_ln_g": conv_ln_g[l], "conv_ln_b": conv_ln_b[l],
             "ret_decay": ret_decay[l], "ret_gn": ret_gn[l], "rwkv_shift": rwkv_shift[l],
             "rwkv_w0": rwkv_w0[l], "rwkv_w2": rwkv_w2[l], "rwkv_a0": rwkv_a0[l], "rwkv_a2": rwkv_a2[l],
             "rwkv_g2": rwkv_g2[l], "rwkv_kk": rwkv_kk[l], "rwkv_ka": rwkv_ka[l], "rwkv_rk": rwkv_rk[l],
             "rwkv_gn": rwkv_gn[l], "ffn_up": ffn_up[l], "ffn_conv": ffn_conv[l], "ffn_down": ffn_down[l]}
        mod_ctx = (jax.nn.silu(c_ctx) @ ada_w[l] + ada_b[l])[None]
        y_prompt, (k_l, v_l, ret_l, rwkv_l) = trunk_layer(y_prompt, mod_ctx, p, None)
        new_k.append(k_l)
        new_v.append(v_l)
        new_ret.append(ret_l)
        new_rwkv.append(rwkv_l)
        mod_lat = jax.nn.silu(c) @ ada_w[l] + ada_b[l]
        cache_l = {"na_k": cache_na_k[:, l], "na_v": cache_na_v[:, l],
                   "ret": state_retention[:, l], "rwkv": state_rwkv[:, l]}
        y_sample, _ = trunk_layer(y_sample, mod_lat, p, cache_l)
    new_na_k = jnp.stack(new_k, axis=1)
    new_na_v = jnp.stack(new_v, axis=1)
    new_state_retention = jnp.stack(new_ret, axis=1)
    new_state_rwkv = jnp.stack(new_rwkv, axis=1)
    return (y_prompt, y_sample, new_na_k, new_na_v, new_state_retention, new_state_rwkv)
```

```python
import numpy as np
from contextlib import ExitStack
import concourse.bass as bass
import concourse.mybir as mybir
from concourse.bass_utils import run_bass_kernel_spmd

F32 = mybir.dt.float32
BF16 = mybir.dt.bfloat16
ALU = mybir.AluOpType
AF = mybir.ActivationFunctionType

D = 1024
KC = 8
T = 2560
DFF = 2816
NJ = 22
DIN = 3328
DEPTH = 2
EPS = 1e-6
SEQS = [(0, 1, 256), (256, 258, 256), (512, 515, 2048)]
TP = 2564
import os as _os
SAME_ENG_SYNC = _os.environ.get('SES', '1') == '1'
DEBUG = False


def tok2col(t):
    for t0, c0, ln in SEQS:
        if t0 <= t < t0 + ln:
            return c0 + (t - t0)
    raise ValueError(t)


def vec_layout():
    lay = {}
    off = 0

    def add(name, n):
        nonlocal off
        lay[name] = (off, n)
        off += n

    for l in range(DEPTH):
        add(f"ada_b{l}", 48)
        for i in range(4):
            add(f"ng{i}_{l}", 8)
        for nm in ("cln_g", "cln_b", "ret_gn", "kk", "ka", "rk", "rw_gn", "w0_0", "w0_1", "a0_0", "a0_1"):
            add(f"{nm}_{l}", 2)
        for tp in range(3):
            add(f"shift{tp}_{l}", 6)
        for tp in range(3):
            add(f"fc{tp}_{l}", 44)
        for tp in range(31):
            add(f"dw{tp}_{l}", 2)
    return lay, off


VLAY, NV = vec_layout()


def pack_vecs(inp):
    v = np.zeros((128, NV), np.float32)

    def put(name, arr):
        off, n = VLAY[name]
        v[:, off:off + n] = np.asarray(arr, np.float32).reshape(n, 128).T

    for l in range(DEPTH):
        put(f"ada_b{l}", inp["ada_b"][l])
        for i in range(4):
            put(f"ng{i}_{l}", inp["norm_g"][l, i])
        put(f"cln_g_{l}", inp["conv_ln_g"][l])
        put(f"cln_b_{l}", inp["conv_ln_b"][l])
        put(f"ret_gn_{l}", inp["ret_gn"][l])
        put(f"kk_{l}", inp["rwkv_kk"][l])
        put(f"ka_{l}", inp["rwkv_ka"][l])
        put(f"rk_{l}", inp["rwkv_rk"][l])
        put(f"rw_gn_{l}", inp["rwkv_gn"][l])
        for d in range(2):
            put(f"w0_{d}_{l}", inp["rwkv_w0"][l, d])
            put(f"a0_{d}_{l}", inp["rwkv_a0"][l, d])
        for tp in range(3):
            put(f"shift{tp}_{l}", inp["rwkv_shift"][l, tp])
            put(f"fc{tp}_{l}", inp["ffn_conv"][l, tp])
        for tp in range(31):
            put(f"dw{tp}_{l}", inp["conv_dw"][l, tp])
    return v


class _Eng:
    def __init__(self, name, sem):
        self.name = name
        self.sem = sem
        self.count = 0
        self.waited = {}
        self.items = []


class Prog:
    def __init__(self, nc, n_dma_sems=(40, 30, 4)):
        self.nc = nc
        self.engs = {}
        for name in ("pe", "act", "dve", "pool", "sp"):
            self.engs[name] = _Eng(name, nc.alloc_semaphore("s_" + name))
        self.last_write = {}
        self.reads = {}
        self.dma_pools = {}
        for q, n in zip(("sp", "pool", "act"), n_dma_sems):
            self.dma_pools[q] = dict(sems=[nc.alloc_semaphore(f"d_{q}{i}") for i in range(n)],
                                     tot=[0] * n, nxt=0)
        self.n_ops = 0
        self.marks = []

    def _need(self, e, tok):
        if tok is None:
            return
        sem, val, src = tok
        if src == e.name and (e.name == "pe" or not SAME_ENG_SYNC):
            return
        k = id(sem)
        if e.waited.get(k, 0) >= val:
            return
        e.waited[k] = val
        e.items.append(("wait", sem, val))

    def _deps(self, e, reads, writes):
        for k in reads:
            self._need(e, self.last_write.get(k))
        for k in writes:
            self._need(e, self.last_write.get(k))
            for tok in self.reads.get(k, {}).values():
                self._need(e, tok)

    def _commit(self, tok, reads, writes):
        for k in writes:
            self.last_write[k] = tok
            self.reads[k] = {}
        for k in reads:
            self.reads.setdefault(k, {})[tok[2]] = tok

    def op(self, eng, fn, reads=(), writes=()):
        psk = [k for k in reads if isinstance(k, tuple) and k and k[0] == "ps"]
        if psk:
            writes = list(writes) + [k for k in psk if k not in writes]
            reads = [k for k in reads if k not in psk]
        e = self.engs[eng]
        self._deps(e, reads, writes)
        e.count += 1
        e.items.append(("op", fn, e.sem))
        self._commit((e.sem, e.count, eng), reads, writes)
        self.n_ops += 1

    def dma(self, q, fn, reads=(), writes=()):
        e = self.engs[q]
        self._deps(e, reads, writes)
        p = self.dma_pools[q]
        i = p["nxt"]
        p["nxt"] = (i + 1) % len(p["sems"])
        sem = p["sems"][i]
        if p["tot"][i] > 0:
            self._need(e, (sem, p["tot"][i], "dma"))
        p["tot"][i] += 16
        e.items.append(("dma", fn, sem))
        tok = (sem, p["tot"][i], "dma:" + q + str(i))
        self._commit(tok, reads, writes)
        self.n_ops += 1
        return tok

    def wait_all(self, eng, toks):
        e = self.engs[eng]
        for t in toks:
            self._need(e, t)

    def barrier(self):
        self.marks.append((self.engs["pe"].count, self.engs["act"].count, self.engs["dve"].count, self.engs["pool"].count))
        toks = [(e.sem, e.count, name) for name, e in self.engs.items() if e.count > 0]
        for q, p in self.dma_pools.items():
            for s, t in zip(p["sems"], p["tot"]):
                if t > 0:
                    toks.append((s, t, "dma"))
        for name, e in self.engs.items():
            for t in toks:
                self._need(e, t)
        self.last_write = {}
        self.reads = {}

    def emit(self):
        nc = self.nc
        handles = {"pe": "tensor", "act": "scalar", "dve": "vector", "pool": "gpsimd", "sp": "sync"}
        with nc.Block() as block:
            for name, attr in handles.items():
                e = self.engs[name]

                def body(h, e=e):
                    for it in e.items:
                        if it[0] == "wait":
                            h.wait_ge(it[1], it[2])
                        elif it[0] == "op":
                            it[1](h).then_inc(it[2], 1)
                        else:
                            it[1](h).then_inc(it[2], 16)

                getattr(block, attr)(body)


class Builder:
    def __init__(self, dbg_names=()):
        self.nc = bass.Bass("TRN2", target_bir_lowering=False)
        self.P = Prog(self.nc)
        self.dbg_names = set(dbg_names)
        self.dbg_out = {}
        self.out_toks = []
        self._bank = 0

    def mm(self, out, lhsT, rhs, start, stop, reads, writes):
        self.P.op("pe", lambda h: h.matmul(out, lhsT=lhsT, rhs=rhs, start=start, stop=stop), reads, writes)

    def tr(self, out, in_, ident, reads, writes):
        self.P.op("pe", lambda h: h.transpose(out=out, in_=in_, identity=ident), reads, writes)

    def act(self, out, in_, func, reads, writes, scale=1.0, bias=None, accum=None):
        def f(h):
            kw = {}
            if bias is not None:
                kw["bias"] = bias
            if accum is not None:
                kw["accum_out"] = accum
            return h.activation(out=out, in_=in_, func=func, scale=scale, **kw)
        self.P.op("act", f, reads, writes)

    def ts(self, eng, out, in0, s1, s2, op0, op1, reads, writes):
        if s2 is None:
            self.P.op(eng, lambda h: h.tensor_scalar(out=out, in0=in0, scalar1=s1, scalar2=None, op0=op0), reads, writes)
        else:
            self.P.op(eng, lambda h: h.tensor_scalar(out=out, in0=in0, scalar1=s1, scalar2=s2, op0=op0, op1=op1), reads, writes)

    def tt(self, eng, out, in0, in1, op, reads, writes):
        self.P.op(eng, lambda h: h.tensor_tensor(out=out, in0=in0, in1=in1, op=op), reads, writes)

    def stt(self, eng, out, in0, scalar, in1, op0, op1, reads, writes):
        self.P.op(eng, lambda h: h.scalar_tensor_tensor(out=out, in0=in0, scalar=scalar, in1=in1, op0=op0, op1=op1), reads, writes)

    def cp(self, eng, out, in_, reads, writes):
        if eng == "act":
            self.act(out, in_, AF.Copy, reads, writes)
        else:
            self.P.op(eng, lambda h: h.tensor_copy(out=out, in_=in_), reads, writes)

    def memset(self, eng, ap, val, writes):
        self.P.op(eng, lambda h: h.memset(ap, val), (), writes)

    def recip(self, out, in_, reads, writes):
        self.P.op("dve", lambda h: h.reciprocal(out=out, in_=in_), reads, writes)

    def dma(self, q, out, in_, reads, writes):
        return self.P.dma(q, lambda h: h.dma_start(out=out, in_=in_), reads, writes)

    def sb(self, name, shape, dt):
        self._uid = getattr(self, "_uid", 0) + 1
        return self.nc.sbuf_tensor(f"{name}_{self._uid}", shape, dt)

    def bank(self):
        res = getattr(self, "_reserved", ())
        i = self._bank
        while i in res:
            i = (i + 1) % 8
        self._bank = (i + 1) % 8
        return self.ps[i], ("ps", i)

    def dbg(self, name, ap, shape, key):
        if name not in self.dbg_names:
            return
        dt = ap.dtype
        o = self.nc.dram_tensor("dbg_" + name, list(shape), dt, kind="ExternalOutput").ap()
        self.dbg_out[name] = o
        self.out_toks.append(self.dma("sp", o, ap, reads=key, writes=()))

    def build(self, skip_mixers=False, phases=None):
        nc, P = self.nc, self.P
        self.x_in = nc.dram_tensor("x_in", [T, D], F32, kind="ExternalInput").ap()
        self.cv_in = nc.dram_tensor("cv", [128, KC, 2], F32, kind="ExternalInput").ap()
        self.vec_in = nc.dram_tensor("vec", [128, NV], F32, kind="ExternalInput").ap()
        self.cst_in = nc.dram_tensor("cst", [128, NCST], F32, kind="ExternalInput").ap()
        self.rope_in = nc.dram_tensor("rope", [128, 2, 2048], F32, kind="ExternalInput").ap()
        self.ada_w = nc.dram_tensor("ada_w", [DEPTH, D, 6 * D], F32, kind="ExternalInput").ap()
        self.w_in = nc.dram_tensor("w_in", [DEPTH, D, DIN], F32, kind="ExternalInput").ap()
        self.w_out = nc.dram_tensor("w_out", [DEPTH, D, D], F32, kind="ExternalInput").ap()
        self.ffn_up = nc.dram_tensor("ffn_up", [DEPTH, D, 2 * DFF], F32, kind="ExternalInput").ap()
        self.ffn_down = nc.dram_tensor("ffn_down", [DEPTH, DFF, D], F32, kind="ExternalInput").ap()
        self.y_out = nc.dram_tensor("y", [T, D], F32, kind="ExternalOutput").ap()
        self.xs = nc.dram_tensor("xs", [128, KC, T], F32).ap()
        if skip_mixers:
            self.mix_in = nc.dram_tensor("mix_in", [DEPTH, 128, KC, T], F32, kind="ExternalInput").ap()
        self.cnk_in = nc.dram_tensor("cnk", [DEPTH, 256, 256], F32, kind="ExternalInput").ap()
        self.cnv_in = nc.dram_tensor("cnv", [DEPTH, 256, 256], F32, kind="ExternalInput").ap()
        self.nab_in = nc.dram_tensor("nab", [DEPTH, 5, 128, 2560], F32, kind="ExternalInput").ap()
        self.sret_in = nc.dram_tensor("sret", [DEPTH, 2, 4, 64, 64], F32, kind="ExternalInput").ap()
        self.srwkv_in = nc.dram_tensor("srwkv", [DEPTH, 2, 4, 64, 64], F32, kind="ExternalInput").ap()
        self.retd_in = nc.dram_tensor("retd", [1, DEPTH * 8], F32, kind="ExternalInput").ap()
        self.lora_in = nc.dram_tensor("lora", [DEPTH, 128, 1280], F32, kind="ExternalInput").ap()
        self.w0row_in = nc.dram_tensor("w0row", [1, DEPTH * 2 * 256], F32, kind="ExternalInput").ap()
        self.bmask_in = nc.dram_tensor("bmask", [128, 640], F32, kind="ExternalInput").ap()
        self.nak_out = nc.dram_tensor("nak", [2, DEPTH, 256, 256], F32, kind="ExternalOutput").ap()
        self.nav_out = nc.dram_tensor("nav", [2, DEPTH, 256, 256], F32, kind="ExternalOutput").ap()
        self.oret_out = nc.dram_tensor("oret", [2, DEPTH, 2, 4, 64, 64], F32, kind="ExternalOutput").ap()
        self.orwkv_out = nc.dram_tensor("orwkv", [2, DEPTH, 2, 4, 64, 64], F32, kind="ExternalOutput").ap()

        self.ps = [nc.alloc_psum_tensor(f"ps{i}", [128, 512], F32) for i in range(8)]

        self.ident = nc.alloc_sbuf_tensor("ident", [128, 128], F32)
        self.identb = nc.alloc_sbuf_tensor("identb", [128, 128], BF16)
        self.onesb = nc.alloc_sbuf_tensor("onesb", [128, 128], BF16)
        self.epsD = nc.alloc_sbuf_tensor("epsD", [128, 1], F32)
        self.vec = nc.alloc_sbuf_tensor("vecs", [128, NV], F32)
        self.modv = nc.alloc_sbuf_tensor("modv", [128, DEPTH, 48, 2], F32)
        self.msc = nc.alloc_sbuf_tensor("msc", [128, DEPTH, 4, KC, 2], F32)
        self.hbuf = nc.alloc_sbuf_tensor("hbuf", [128, KC, TP], BF16)
        self.cvt = nc.alloc_sbuf_tensor("cvt", [128, KC, 2], F32)
        self._bg = []

        self.cstt = nc.alloc_sbuf_tensor("cstt", [128, NCST], F32)
        self.dma("sp", self.cstt[:], self.cst_in, (), ["cstt"])
        self.cp("act", self.ident[:], self.cstt[:, 0:128], ["cstt"], ["ident"])
        self.dma("sp", self.vec[:], self.vec_in, (), ["vec"])
        self.cp("dve", self.identb[:], self.ident[:], ["ident"], ["identb"])
        self.memset("pool", self.onesb[:], 1.0, ["onesb"])
        self.memset("pool", self.epsD[:], D * EPS, ["epsD"])
        self.memset("pool", self.hbuf[:], 0.0, ["hbuf_all"])
        P.barrier()

        self.phase_load_x()
        if phases is None or "mod" in phases:
            self.phase_mod()
        for l in range(DEPTH if (phases is None or "layers" in phases) else 0):
            self.phase_norm1(l)
            P.barrier()
            if phases is not None and "norm1only" in phases:
                continue
            with self.sb("mixed", [128, KC, T], BF16) as mixed:
                self.mixed = mixed
                if skip_mixers:
                    self.phase_fake_mixers(l)
                else:
                    self.phase_mixers(l)
                P.barrier()
                self.dbg(f"mixed{l}", self.mixed[:], [128, KC, T], [])
                if phases is not None and "mixonly" in phases:
                    P.barrier()
                    break
                if phases is None or "wout" in phases:
                    self.phase_wout(l)
                P.barrier()
            if phases is None or "ffn" in phases:
                self.phase_ffn(l)
            P.barrier()
        self.phase_store_y()
        P.wait_all("sp", self.out_toks)
        P.emit()
        return nc

    def V(self, name, j=None):
        off, n = VLAY[name]
        if j is None:
            return self.vec[:, off:off + n]
        return self.vec[:, off + j:off + j + 1]

    def phase_load_x(self):
        nc, P = self.nc, self.P
        with self.sb("xrow", [128, 4, D], F32) as xrow, self.sb("xT", [128, 2, KC, 512], F32) as xT, \
                self.sb("adab", [128, 2, KC, 512], F32) as adab:
            cvt = self.cvt
            def gen_x():
                for r in range(T // 128):
                    b = r % 4
                    t4, r4 = (r // 4) % 2, r % 4
                    self.dma("sp", xrow[:, b, :], self.x_in[r * 128:(r + 1) * 128, :], (), [("xrow", b)])
                    for half in range(2):
                        pst, pk = self.bank()
                        for i in range(4):
                            k = half * 4 + i
                            self.tr(pst[:, i * 128:(i + 1) * 128], xrow[:, b, k * 128:(k + 1) * 128], self.ident[:],
                                    [("xrow", b), "ident"], [pk])
                        eng = "act" if half == 0 else "dve"
                        self.cp(eng, xT[:, t4, half * 4:(half + 1) * 4, r4 * 128:(r4 + 1) * 128],
                                pst[:].rearrange("p (i n) -> p i n", i=4), [pk], [("xT", t4, r4, half)])
                    if r4 == 3:
                        tt = r // 4
                        self.dma("sp", self.xs[:, :, tt * 512:(tt + 1) * 512], xT[:, t4],
                                 [("xT", t4, q_, h_) for q_ in range(4) for h_ in range(2)], [("xs", tt)])
                    yield

            gx, gm = gen_x(), self.gen_mod(cvt, adab, [0])
            alive = [gx, gm]
            while alive:
                for g in list(alive):
                    try:
                        next(g)
                    except StopIteration:
                        alive.remove(g)
            P.barrier()

    def phase_mod(self):
        pass

    def bg_step(self):
        for g in list(self._bg):
            try:
                next(g)
            except StopIteration:
                self._bg.remove(g)

    def gen_mod(self, cvt, adab, layers):
        nc, P = self.nc, self.P
        if 0 in layers:
            self.dma("sp", cvt[:], self.cv_in, (), ["cvt"])
            self.act(cvt[:], cvt[:], AF.Silu, ["cvt"], ["cvt"])
        n = 0
        for l in layers:
            wv = self.ada_w[l].rearrange("(k p) n -> p k n", p=128)
            pst, pk = self.ps[7], ("ps", 7)
            self._reserved = {7}
            for nb in range(12):
                b = n % 2
                n += 1
                self.dma("sp", adab[:, b], wv[:, :, nb * 512:(nb + 1) * 512], (), [("adab", b)])
                for m in range(4):
                    cc = nb * 4 + m
                    for k in range(KC):
                        self.mm(pst[:, cc * 2:cc * 2 + 2], adab[:, b, k, m * 128:(m + 1) * 128], cvt[:, k, :],
                                k == 0, k == KC - 1, [("adab", b), "cvt"], [pk])
                yield
            off, _ = VLAY[f"ada_b{l}"]
            self.tt("dve", self.modv[:, l], pst[:, 0:96].rearrange("p (c g) -> p c g", g=2),
                    self.vec[:, off:off + 48].unsqueeze(2).to_broadcast([128, 48, 2]), ALU.add,
                    [pk, "vec"], [("modv", l)])
            self._reserved = set()
            for which, (src, ng) in enumerate(((8, 0), (16, 1), (32, 2), (40, 3))):
                o = self.msc[:, l, which]
                g = self.V(f"ng{ng}_{l}").unsqueeze(2).to_broadcast([128, KC, 2])
                m_ = self.modv[:, l, src:src + 8, :]
                if which in (0, 2):
                    self.ts("dve", o, m_, 1.0, 32.0, ALU.add, ALU.mult, [("modv", l)], [("msc", l, which)])
                else:
                    self.ts("dve", o, m_, 32.0, None, ALU.mult, None, [("modv", l)], [("msc", l, which)])
                self.tt("dve", o, o, g, ALU.mult, [("msc", l, which), "vec"], [("msc", l, which)])
            yield

    def norm_mod_tile(self, l, xt, xkey, sq, sqkeys, rs, rskey, pieces, grp, which_scale, shift_chunk0, xn=None, xnkey=None):
        n = xt.shape[2]
        if xn is None:
            xn, xnkey = xt, xkey
        self.act(sq, xt, AF.Square, [xkey], list(sqkeys))
        pst, pk = self.bank()
        for k in range(KC):
            self.mm(pst[:, 0:n], self.onesb[:], sq[:, k, :], k == 0, k == KC - 1, [sqkeys[k], "onesb"], [pk])
        self.act(rs, pst[:, 0:n], AF.Sqrt, [pk, "epsD"], [rskey], bias=self.epsD[:, 0:1])
        self.recip(rs, rs, [rskey], [rskey])
        xnkeys = xnkey if isinstance(xnkey, list) else [xnkey]
        self.tt("dve", xn, xt, rs.unsqueeze(1).to_broadcast([128, KC, n]), ALU.mult, [xkey, rskey], xnkeys)
        for k in range(KC):
            sc = self.msc[:, l, which_scale, k, grp:grp + 1]
            sh = self.modv[:, l, shift_chunk0 + k, grp:grp + 1]
            xk = xnkeys[k] if len(xnkeys) > 1 else xnkeys[0]
            for (c0, ln, hc) in pieces:
                eng = ("act", "pool")[k % 2]
                o = self.hbuf[:, k, hc:hc + ln]
                i = xn[:, k, c0:c0 + ln]
                rd = [xk, ("msc", l, which_scale), ("modv", l)]
                if eng == "act":
                    self.act(o, i, AF.Identity, rd, [("h", k, hc)], scale=sc, bias=sh)
                else:
                    self.ts(eng, o, i, sc, sh, ALU.mult, ALU.add, rd, [("h", k, hc)])

    @staticmethod
    def tile_pieces(tt):
        if tt == 0:
            return 0, [(0, 256, 1), (256, 256, 258)]
        return 1, [(0, 512, 515 + (tt - 1) * 512)]

    def phase_norm1(self, l):
        nc, P = self.nc, self.P
        with self.sb("n1x", [128, 2, KC, 512], F32) as xt, self.sb("n1sq", [128, 2, KC, 512], BF16) as sq, \
                self.sb("n1rs", [128, 2, 512], F32) as rs:
            for tt in range(5):
                b = tt % 2
                self.dma("sp", xt[:, b], self.xs[:, :, tt * 512:(tt + 1) * 512], (), [("n1x", b)])
                grp, pieces = self.tile_pieces(tt)
                self.norm_mod_tile(l, xt[:, b], ("n1x", b), sq[:, b], [("n1sq", b, k) for k in range(KC)], rs[:, b],
                                   ("n1rs", b), pieces, grp, 0, 0)
            P.barrier()

    def phase_fake_mixers(self, l):
        nc = self.nc
        with self.sb("mixf", [128, KC, 512], F32) as mf:
            for tt in range(5):
                self.dma("sp", mf[:], self.mix_in[l, :, :, tt * 512:(tt + 1) * 512], (), ["mixf"])
                self.cp("dve", self.mixed[:, :, tt * 512:(tt + 1) * 512], mf[:], ["mixf"], [("mixed", tt)])
            self.P.barrier()

    def phase_mixers(self, l):
        raise NotImplementedError

    def phase_wout(self, l):
        nc, P = self.nc, self.P
        with self.sb("wo", [128, KC, D], BF16) as wo, self.sb("wx", [128, 2, KC, 512], F32) as xt, \
                self.sb("wm", [128, 2, KC, 512], F32) as mt_, self.sb("wsq", [128, 2, KC, 512], BF16) as sq_, \
                self.sb("wrs", [128, 2, 512], F32) as rs_:
            self.dma("pool", wo[:], self.w_out[l].rearrange("(k p) n -> p k n", p=128), (), ["wo"])

            def tile(tt, u):
                b = u
                mt, sq, rs = mt_[:, u], sq_[:, u], rs_[:, u]
                grp, pieces = self.tile_pieces(tt)
                mk = [("wm", u, oc) for oc in range(KC)]
                sk = [("wsq", u, oc) for oc in range(KC)]
                self.dma("sp", xt[:, b], self.xs[:, :, tt * 512:(tt + 1) * 512], (), [("wx", b)])
                for oc in range(KC):
                    pst, pk = self.bank()
                    for k in range(KC):
                        self.mm(pst[:], wo[:, k, oc * 128:(oc + 1) * 128], self.mixed[:, k, tt * 512:(tt + 1) * 512],
                                k == 0, k == KC - 1, ["wo", ("mixed", tt)], [pk])
                    self.act(sq[:, oc, :], pst[:], AF.Square, [pk], [sk[oc]])
                    self.cp("dve", mt[:, oc, :], pst[:], [pk], [mk[oc]])
                    if oc % 2 == 1:
                        yield
                yield from self.post_norm_gen(l, mt, mk, sq, sk, rs, ("wrs", u), xt[:, b], ("wx", b), grp, 1)
                self.dma("sp", self.xs[:, :, tt * 512:(tt + 1) * 512], xt[:, b], [("wx", b)], [("xs2", tt)])
                yield
                yield from self.norm_mod_gen(l, xt[:, b], ("wx", b), sq, sk, rs, ("wrs", u), pieces, grp, 2, 24, mt, mk)

            for a in range(0, 5, 2):
                gens = [tile(tt, u) for u, tt in enumerate(range(a, min(a + 2, 5)))]
                while gens:
                    for g in list(gens):
                        try:
                            next(g)
                        except StopIteration:
                            gens.remove(g)
            P.barrier()

    def post_norm_gen(self, l, mt, mkeys, sq, sqkeys, rs, rskey, xt, xkey, grp, which_gate):
        n = mt.shape[2]
        pst, pk = self.bank()
        for k in range(KC):
            self.mm(pst[:, 0:n], self.onesb[:], sq[:, k, :], k == 0, k == KC - 1, [sqkeys[k], "onesb"], [pk])
        yield
        self.act(rs, pst[:, 0:n], AF.Sqrt, [pk, "epsD"], [rskey], bias=self.epsD[:, 0:1])
        yield
        self.recip(rs, rs, [rskey], [rskey])
        yield
        self.tt("dve", mt, mt, rs.unsqueeze(1).to_broadcast([128, KC, n]), ALU.mult, list(mkeys) + [rskey], list(mkeys))
        yield
        for k in range(KC):
            gg = self.msc[:, l, which_gate, k, grp:grp + 1]
            self.stt("dve", xt[:, k, :], mt[:, k, :], gg, xt[:, k, :], ALU.mult, ALU.add,
                     [mkeys[k], xkey, ("msc", l, which_gate)], [xkey])
            if k % 4 == 3:
                yield

    def norm_mod_gen(self, l, xt, xkey, sq, sqkeys, rs, rskey, pieces, grp, which_scale, shift_chunk0, xn, xnkeys):
        n = xt.shape[2]
        self.act(sq, xt, AF.Square, [xkey], list(sqkeys))
        yield
        pst, pk = self.bank()
        for k in range(KC):
            self.mm(pst[:, 0:n], self.onesb[:], sq[:, k, :], k == 0, k == KC - 1, [sqkeys[k], "onesb"], [pk])
        yield
        self.act(rs, pst[:, 0:n], AF.Sqrt, [pk, "epsD"], [rskey], bias=self.epsD[:, 0:1])
        yield
        self.recip(rs, rs, [rskey], [rskey])
        yield
        self.tt("dve", xn, xt, rs.unsqueeze(1).to_broadcast([128, KC, n]), ALU.mult, [xkey, rskey], list(xnkeys))
        yield
        for k in range(KC):
            sc = self.msc[:, l, which_scale, k, grp:grp + 1]
            sh = self.modv[:, l, shift_chunk0 + k, grp:grp + 1]
            for (c0, ln, hc) in pieces:
                eng = ("act", "pool")[k % 2]
                o = self.hbuf[:, k, hc:hc + ln]
                i = xn[:, k, c0:c0 + ln]
                rd = [xnkeys[k], ("msc", l, which_scale), ("modv", l)]
                if eng == "act":
                    self.act(o, i, AF.Identity, rd, [("h", k, hc)], scale=sc, bias=sh)
                else:
                    self.ts(eng, o, i, sc, sh, ALU.mult, ALU.add, rd, [("h", k, hc)])
            if k % 2 == 1:
                yield

    def post_norm_residual(self, l, mt, mkeys, sq, sqkeys, rs, rskey, xt, xkey, grp, which_gate):
        n = mt.shape[2]
        pst, pk = self.bank()
        for k in range(KC):
            self.mm(pst[:, 0:n], self.onesb[:], sq[:, k, :], k == 0, k == KC - 1, [sqkeys[k], "onesb"], [pk])
        self.act(rs, pst[:, 0:n], AF.Sqrt, [pk, "epsD"], [rskey], bias=self.epsD[:, 0:1])
        self.recip(rs, rs, [rskey], [rskey])
        self.tt("dve", mt, mt, rs.unsqueeze(1).to_broadcast([128, KC, n]), ALU.mult, list(mkeys) + [rskey], list(mkeys))
        for k in range(KC):
            gg = self.msc[:, l, which_gate, k, grp:grp + 1]
            self.stt("dve", xt[:, k, :], mt[:, k, :], gg, xt[:, k, :], ALU.mult, ALU.add,
                     [mkeys[k], xkey, ("msc", l, which_gate)], [xkey])

    def ffn_windows(self):
        w = [(0, 258, 0, 256), (257, 258, 256, 256)]
        sizes = [410, 410, 410, 410, 408]
        t = 512
        for s in sizes:
            w.append((tok2col(t) - 1, s + 2, t, s))
            t += s
        return w

    def phase_ffn(self, l):
        nc, P = self.nc, self.P
        wins = self.ffn_windows()
        passes = [wins[0:3], wins[3:5], wins[5:7]]
        upv = self.ffn_up[l].rearrange("(k p) (two j c) -> p k two j c", p=128, two=2, j=NJ)
        dnv = self.ffn_down[l].rearrange("(j p) n -> p j n", p=128)
        MAXT = 1024

        def run_interleaved(gens):
            gens = list(gens)
            while gens:
                for g in list(gens):
                    try:
                        next(g)
                    except StopIteration:
                        gens.remove(g)

        with self.sb("fact", [128, NJ, MAXT], BF16) as fact:
            for pw in passes:
                p_tok0 = pw[0][2]
                p_ntok = sum(w[3] for w in pw)
                with self.sb("wu", [128, 2, KC, 2, 512], BF16) as wu, self.sb("cg", [128, 2, 512], F32) as cg, \
                        self.sb("vc", [128, 2, 3, 512], F32) as vc, self.sb("cvv", [128, 2, 512], F32) as cvv:

                    upw = self.ffn_up[l].rearrange("(k p) (two n) -> p k two n", p=128, two=2)
                    groups = [(j0, min(4, NJ - j0)) for j0 in range(0, NJ, 4)]

                    def load_wu(gi):
                        j0, gs = groups[gi]
                        wb = gi % 2
                        for half in range(2):
                            self.dma("pool", wu[:, wb, :, half, 0:gs * 128], upw[:, :, half, j0 * 128:(j0 + gs) * 128], (), [("wu", wb, half)])

                    def unit(j, wb, win, cb):
                        (c0, ncol, tok0, nout) = win
                        jo = (j % 4) * 128
                        fcg = [self.V(f"fc{tp}_{l}", j) for tp in range(3)]
                        fcv = [self.V(f"fc{tp}_{l}", NJ + j) for tp in range(3)]
                        pg, pgk = self.bank()
                        pv, pvk = self.bank()
                        for k in range(KC):
                            self.mm(pg[:, 0:ncol], wu[:, wb, k, 0, jo:jo + 128], self.hbuf[:, k, c0:c0 + ncol], k == 0, k == KC - 1, [("wu", wb, 0)], [pgk])
                        for k in range(KC):
                            self.mm(pv[:, 0:ncol], wu[:, wb, k, 1, jo:jo + 128], self.hbuf[:, k, c0:c0 + ncol], k == 0, k == KC - 1, [("wu", wb, 1)], [pvk])
                        yield
                        g_o = cg[:, cb, 0:nout]
                        v_o = cvv[:, cb, 0:nout]
                        self.ts("dve", g_o, pg[:, 1:1 + nout], fcg[1], None, ALU.mult, None, [pgk], [("cg", cb)])
                        self.act(vc[:, cb, 0, 0:nout], pv[:, 0:nout], AF.Identity, [pvk], [("vc", cb, 0)], scale=fcv[0])
                        yield
                        self.stt("dve", g_o, pg[:, 0:nout], fcg[0], g_o, ALU.mult, ALU.add, [pgk, ("cg", cb)], [("cg", cb)])
                        self.act(vc[:, cb, 1, 0:nout], pv[:, 1:1 + nout], AF.Identity, [pvk], [("vc", cb, 1)], scale=fcv[1])
                        yield
                        self.stt("dve", g_o, pg[:, 2:2 + nout], fcg[2], g_o, ALU.mult, ALU.add, [pgk, ("cg", cb)], [("cg", cb)])
                        self.act(vc[:, cb, 2, 0:nout], pv[:, 2:2 + nout], AF.Identity, [pvk], [("vc", cb, 2)], scale=fcv[2])
                        yield
                        self.act(g_o, g_o, AF.Silu, [("cg", cb)], [("cg", cb)])
                        self.tt("pool", v_o, vc[:, cb, 0, 0:nout], vc[:, cb, 1, 0:nout], ALU.add, [("vc", cb, 0), ("vc", cb, 1)], [("cvv", cb)])
                        yield
                        self.tt("pool", v_o, v_o, vc[:, cb, 2, 0:nout], ALU.add, [("vc", cb, 2), ("cvv", cb)], [("cvv", cb)])
                        yield
                        a0 = tok0 - p_tok0
                        self.tt("dve", fact[:, j, a0:a0 + nout], g_o, v_o, ALU.mult, [("cg", cb), ("cvv", cb)], [("fact", j)])

                    load_wu(0)
                    for j in range(NJ):
                        gi = j // 4
                        if j % 4 == 0 and gi + 1 < len(groups):
                            load_wu(gi + 1)
                        wl = list(pw)
                        for a in range(0, len(wl), 2):
                            run_interleaved([unit(j, gi % 2, w_, ci) for ci, w_ in enumerate(wl[a:a + 2])])
                    P.barrier()
                tiles = []
                t = 0
                while t < p_ntok:
                    n = min(512, p_ntok - t)
                    tiles.append((t, n))
                    t += n
                nt = len(tiles)
                with self.sb("wd", [128, 2, NJ, 128], BF16) as wd, self.sb("ff", [128, 2, KC, 512], F32) as ff, \
                        self.sb("fsq", [128, 2, KC, 512], BF16) as fsq, self.sb("frs", [128, 2, 512], F32) as frs, \
                        self.sb("fx", [128, 2, KC, 512], F32) as fx:
                    for ti, (t0, n) in enumerate(tiles):
                        g_tok0 = p_tok0 + t0
                        assert (g_tok0 + n <= 512) or g_tok0 >= 512
                        self.dma("sp", fx[:, ti, :, 0:n], self.xs[:, :, g_tok0:g_tok0 + n], (), [("fx", ti)])
                    self.dma("pool", wd[:, 0], dnv[:, :, 0:128], (), [("wd", 0)])
                    for oc in range(KC):
                        db = oc % 2
                        if oc + 1 < KC:
                            self.dma("pool", wd[:, 1 - db], dnv[:, :, (oc + 1) * 128:(oc + 2) * 128], (), [("wd", 1 - db)])
                        for ti, (t0, n) in enumerate(tiles):
                            pst, pk = self.bank()
                            for j in range(NJ):
                                self.mm(pst[:, 0:n], wd[:, db, j, :], fact[:, j, t0:t0 + n], j == 0, j == NJ - 1,
                                        [("wd", db), ("fact", j)], [pk])
                            self.act(fsq[:, ti, oc, 0:n], pst[:, 0:n], AF.Square, [pk], [("fsq", ti, oc)])
                            self.cp("dve", ff[:, ti, oc, 0:n], pst[:, 0:n], [pk], [("ff", ti, oc)])
                    for ti, (t0, n) in enumerate(tiles):
                        g_tok0 = p_tok0 + t0
                        grp = 0 if g_tok0 < 512 else 1
                        self.post_norm_residual(l, ff[:, ti, :, 0:n], [("ff", ti, oc) for oc in range(KC)], fsq[:, ti, :, 0:n],
                                                [("fsq", ti, oc) for oc in range(KC)], frs[:, ti, 0:n], ("frs", ti), fx[:, ti, :, 0:n],
                                                ("fx", ti), grp, 3)
                        self.dma("sp", self.xs[:, :, g_tok0:g_tok0 + n], fx[:, ti, :, 0:n], [("fx", ti)], [("xs3", g_tok0)])
                    P.barrier()
            P.barrier()

    def phase_store_y(self):
        nc, P = self.nc, self.P
        with self.sb("yx", [128, 2, KC, 512], F32) as yx, self.sb("yrow", [128, 2, D], F32) as yrow:
            for tt in range(5):
                b = tt % 2
                self.dma("sp", yx[:, b], self.xs[:, :, tt * 512:(tt + 1) * 512], (), [("yx", b)])
                for r4 in range(4):
                    rb = (tt * 4 + r4) % 2
                    for half in range(2):
                        pst, pk = self.bank()
                        for i in range(4):
                            k = half * 4 + i
                            self.tr(pst[:, i * 128:(i + 1) * 128], yx[:, b, k, r4 * 128:(r4 + 1) * 128], self.ident[:],
                                    [("yx", b), "ident"], [pk])
                        eng = "act" if half == 0 else "dve"
                        self.cp(eng, yrow[:, rb, half * 512:(half + 1) * 512], pst[:], [pk], [("yrow", rb, half)])
                    r = tt * 4 + r4
                    self.out_toks.append(self.dma("sp", self.y_out[r * 128:(r + 1) * 128, :], yrow[:, rb, :],
                                                  [("yrow", rb, 0), ("yrow", rb, 1)], ()))


def host_inputs(inp, core):
    b = core // 2
    x = np.concatenate([inp["x_prompt"][2 * core].reshape(256, D), inp["x_prompt"][2 * core + 1].reshape(256, D),
                        inp["x_sample"][b].reshape(2048, D)], axis=0)
    cvec = np.stack([inp["c_ctx"], inp["c"][b]], axis=-1)
    cv = cvec.reshape(KC, 128, 2).transpose(1, 0, 2)
    return {"x_in": np.ascontiguousarray(x, np.float32), "cv": np.ascontiguousarray(cv, np.float32),
            "cnk": np.ascontiguousarray(inp["cache_na_k"][b].reshape(DEPTH, 256, 256)),
            "cnv": np.ascontiguousarray(inp["cache_na_v"][b].reshape(DEPTH, 256, 256)),
            "sret": np.ascontiguousarray(inp["state_retention"][b]),
            "srwkv": np.ascontiguousarray(inp["state_rwkv"][b])}


def host_shared(inp):
    lora = np.zeros((DEPTH, 128, 1280), np.float32)
    for l in range(DEPTH):
        for d in range(2):
            lora[l, 0:64, d * 256:(d + 1) * 256] = inp["rwkv_w2"][l, d]
            lora[l, 64:128, 512 + d * 256:512 + (d + 1) * 256] = inp["rwkv_a2"][l, d]
        lora[l, :, 1024:1280] = inp["rwkv_g2"][l]
    return {"vec": pack_vecs(inp), "cst": make_cst(), "rope": make_rope(), "ada_w": inp["ada_w"], "w_in": inp["w_in"],
            "w_out": inp["w_out"], "ffn_up": inp["ffn_up"], "ffn_down": inp["ffn_down"],
            "nab": na_bias_tiles(np.asarray(inp["na_rpb"], np.float32)).reshape(DEPTH, 5, 128, 2560),
            "retd": np.ascontiguousarray(inp["ret_decay"].reshape(1, DEPTH * 8)), "lora": lora,
            "w0row": np.ascontiguousarray(inp["rwkv_w0"].reshape(1, DEPTH * 2 * 256)), "bmask": make_bmask()}


def make_bmask():
    j = np.arange(128)[:, None]
    t = np.arange(128)[None, :]
    same = lambda b: (j // b == t // b).astype(np.float32)
    m = [same(8), same(16) - same(8), same(32) - same(16), same(64) - same(32), 1.0 - same(64)]
    return np.ascontiguousarray(np.concatenate(m, axis=1), np.float32)


C_NAQ, C_NAK, C_NAV, C_CVA, C_CVB, C_RTQ, C_RTK, C_RTV, C_RTG, C_RWR, C_RWK, C_RWV = [256 * i for i in range(12)]
C_WLOW, C_ALOW, C_GLOW = 3072, 3136, 3200
SEQ_WINS = [(0, 0, 256, 1, 0), (1, 0, 256, 258, 256)] + [(2, 512 * i, 512, 515 + 512 * i, 512 + 512 * i) for i in range(4)]
TM_SLICES = [(i, tok2col(128 * i), 128 * i) for i in range(20)]


def _install_mixers():
    B = Builder

    def _issue_win(self, l, col0, ncols):
        i = self._wn % 2
        self._wn += 1
        key = ("winb", i)
        self.dma("pool", self.winb[:, i, :, 0:ncols], self.w_in[l].rearrange("(k p) n -> p k n", p=128)[:, :, col0:col0 + ncols],
                 (), [key])
        return self.winb[:, i], key

    def load_win(self, l, col0, ncols):
        pref = getattr(self, "_wpref", None)
        if pref is not None and pref[0] == (l, col0, ncols):
            cur = pref[1]
        else:
            cur = self._issue_win(l, col0, ncols)
        self._wpref = None
        sched = getattr(self, "_wsched", None)
        if sched:
            if sched and sched[0] == (col0, ncols):
                sched.pop(0)
            if sched:
                nc0, nn = sched[0]
                self._wpref = ((l, nc0, nn), self._issue_win(l, nc0, nn))
        return cur

    def proj_fm(self, l, col0, evac, wins=None, m=128):
        wv, wk = self.load_win(l, col0, m)
        for win in (wins or SEQ_WINS):
            (s, toff, n, hc, gt) = win
            pst, pk = self.bank()
            for k in range(KC):
                self.mm(pst[0:m, 0:n], wv[:, k, 0:m], self.hbuf[:, k, hc:hc + n], k == 0, k == KC - 1, [wk, "hbuf"], [pk])
            evac(win, pst, pk)
            self.bg_step()

    def proj_tm(self, l, col0, evac, slices=None):
        wv, wk = self.load_win(l, col0, 256)
        for sl in (slices or TM_SLICES):
            (i, hc, gt) = sl
            pst, pk = self.bank()
            for k in range(KC):
                self.mm(pst[:, 0:256], self.hbuf[:, k, hc:hc + 128], wv[:, k, 0:256], k == 0, k == KC - 1, [wk, "hbuf"], [pk])
            evac(sl, pst, pk)

    B._issue_win, B.load_win, B.proj_fm, B.proj_tm = _issue_win, load_win, proj_fm, proj_tm

    def mixer_conv(self, l):
        nc, P = self.nc, self.P
        GP = 2620
        gcol = {0: 15, 1: 286, 2: 557}
        with self.sb("glu", [128, 2, GP], BF16) as glu, self.sb("dg", [128, 31, 2, 128], BF16) as dg, \
                self.sb("sig", [128, 2, 512], F32) as sig, self.sb("co", [128, 2, 2, 512], F32) as co, \
                self.sb("cob", [128, 2, 2, 512], BF16) as cob, self.sb("cosq", [128, 2, 2, 512], BF16) as cosq, \
                self.sb("cst", [128, 2, 3, 512], F32) as st, self.sb("epsc", [128, 1], F32) as epsc:
            self.memset("pool", glu[:], 0.0, ["glu_all"])
            self.memset("pool", epsc[:], EPS, ["epsc"])
            P.barrier()
            for j in range(31):
                for c in range(2):
                    eng = ("dve", "pool")[(j + c) % 2]
                    self.ts(eng, dg[:, j, c, :], self.identb[:], self.V(f"dw{j}_{l}", c), None, ALU.mult, None,
                            ["identb", "vec"], [("dg", j, c)])
            P.barrier()
            self._wsched = None
            self._wpref = None
            for c in range(2):
                hold = {}

                def evac_b(win, pst, pk, c=c, hold=hold):
                    (s, toff, n, hc, gt) = win
                    b = (gt // 512) % 2 if False else 0
                    self.act(sig[:, c, 0:n], pst[:, 0:n], AF.Sigmoid, [pk], [("sig", c, gt)])

                wa, wak = self.load_win(l, C_CVA + 128 * c, 128)
                wb, wbk = self.load_win(l, C_CVB + 128 * c, 128)
                for win in SEQ_WINS:
                    (s, toff, n, hc, gt) = win
                    pb, pbk = self.bank()
                    for k in range(KC):
                        self.mm(pb[:, 0:n], wb[:, k, 0:128], self.hbuf[:, k, hc:hc + n], k == 0, k == KC - 1, [wbk, "hbuf"], [pbk])
                    pa, pak = self.bank()
                    for k in range(KC):
                        self.mm(pa[:, 0:n], wa[:, k, 0:128], self.hbuf[:, k, hc:hc + n], k == 0, k == KC - 1, [wak, "hbuf"], [pak])
                    self.act(sig[:, c, 0:n], pb[:, 0:n], AF.Sigmoid, [pbk], [("sig", c)])
                    g0 = gcol[s] + toff
                    self.tt("dve", glu[:, c, g0:g0 + n], pa[:, 0:n], sig[:, c, 0:n], ALU.mult, [pak, ("sig", c)], [("glu", c, gt)])
            P.barrier()
            def convwin(win, u):
                (s, toff, n, hc, gt) = win
                g0 = gcol[s] + toff - 15
                K_ = lambda nm, *a: (nm, u) + a
                cw = lambda t, c: t[:, u, c, 0:n]
                for c in range(2):
                    pc, pck = self.bank()
                    for j in range(31):
                        self.mm(pc[:, 0:n], dg[:, j, c, :], glu[:, c, g0 + j:g0 + j + n], j == 0, j == 30, [], [pck])
                    yield
                    self.cp("dve", cw(co, c), pc[:, 0:n], [pck], [K_("co", c)])
                    yield
                    self.act(cw(cosq, c), cw(co, c), AF.Square, [K_("co", c)], [K_("cosq", c)])
                    self.act(cw(cob, c), cw(co, c), AF.Copy, [K_("co", c)], [K_("cob", c)])
                    yield
                p1, p1k = self.bank()
                for c in range(2):
                    self.mm(p1[:, 0:n], self.onesb[:], cw(cob, c), c == 0, c == 1, [K_("cob", c)], [p1k])
                p2, p2k = self.bank()
                for c in range(2):
                    self.mm(p2[:, 0:n], self.onesb[:], cw(cosq, c), c == 0, c == 1, [K_("cosq", c)], [p2k])
                yield
                mean, msq, var = st[:, u, 0, 0:n], st[:, u, 1, 0:n], st[:, u, 2, 0:n]
                self.ts("dve", mean, p1[:, 0:n], 1.0 / 256, None, ALU.mult, None, [p1k], [K_("cmean")])
                yield
                self.tt("dve", msq, mean, mean, ALU.mult, [K_("cmean")], [K_("cmsq")])
                yield
                self.stt("dve", var, p2[:, 0:n], 1.0 / 256, msq, ALU.mult, ALU.subtract, [p2k, K_("cmsq")], [K_("cvar")])
                yield
                self.act(var, var, AF.Sqrt, [K_("cvar")], [K_("cvar")], bias=epsc[:, 0:1])
                for c in range(2):
                    self.tt("dve", cw(co, c), cw(co, c), mean, ALU.subtract, [K_("co", c), K_("cmean")], [K_("co", c)])
                yield
                self.recip(var, var, [K_("cvar")], [K_("cvar")])
                yield
                for c in range(2):
                    self.tt("dve", cw(co, c), cw(co, c), var, ALU.mult, [K_("co", c), K_("cvar")], [K_("co", c)])
                yield
                for c in range(2):
                    self.act(self.mixed[:, 2 + c, gt:gt + n], cw(co, c), AF.Silu, [K_("co", c)], [("mixed", 2 + c, gt)],
                             scale=self.V(f"cln_g_{l}", c), bias=self.V(f"cln_b_{l}", c))

            def run_interleaved(gens):
                gens = list(gens)
                while gens:
                    for g in list(gens):
                        try:
                            next(g)
                        except StopIteration:
                            gens.remove(g)

            wins = list(SEQ_WINS)
            for a in range(0, len(wins), 2):
                run_interleaved([convwin(w_, u) for u, w_ in enumerate(wins[a:a + 2])])
            P.barrier()

    B.mixer_conv = mixer_conv


_install_mixers()


NA_TYPES = {0: 0, 1: 1, 14: 3, 15: 4}
NA_TYPE_REP = [0, 1, 2, 14, 15]


def na_bias_tiles(rpb):
    out = np.full((DEPTH, 5, 128, 4, 5, 128), -1e30, np.float32)
    qcol = np.arange(64)
    kcol = np.arange(64)
    cstart = np.clip(qcol - 8, 0, 48)
    col_ok = (kcol[:, None] >= cstart[None, :]) & (kcol[:, None] < cstart[None, :] + 16)
    dcol = np.clip(kcol[:, None] - qcol[None, :], -15, 15) + 15
    for ty, i in enumerate(NA_TYPE_REP):
        kr0 = int(np.clip(2 * i - 4, 0, 22))
        for qr2 in range(2):
            r = 2 * i + qr2
            start = int(np.clip(r - 4, 0, 24))
            for m in range(5):
                for kr2 in range(2):
                    kr = kr0 + 2 * m + kr2
                    if not (start <= kr < start + 8):
                        continue
                    drow = kr - r + 7
                    for l in range(DEPTH):
                        vals = rpb[l][:, drow, :][:, dcol]
                        vals = np.where(col_ok[None], vals, np.float32(-1e30))
                        out[l, ty, kr2 * 64:(kr2 + 1) * 64, :, m, qr2 * 64:(qr2 + 1) * 64] = vals.transpose(1, 0, 2)
    return out


def _install_attn():
    B = Builder

    def mixer_attn(self, l):
        nc, P = self.nc, self.P
        with self.sb("qf", [128, 2, T], BF16) as qf, self.sb("kf", [128, 2, T], BF16) as kf, \
                self.sb("vt", [128, 20, 256], BF16) as vt, self.sb("ckf", [128, 2, 256], BF16) as ckf, \
                self.sb("cvt", [128, 2, 256], BF16) as cvt, self.sb("cst32", [128, 2, 2, 256], F32) as c32, \
                self.sb("nabI", [128, 4, 5, 128], F32) as nabI, self.sb("nabE", [128, 4, 5, 128], F32) as nabE, \
                self.sb("sT", [128, 2, 640], F32) as sT_, self.sb("pT", [128, 2, 7, 128], BF16) as pT_, \
                self.sb("pTp", [128, 2, 256], BF16) as pTp, self.sb("rc", [128, 2, 256], F32) as rc_, \
                self.sb("ost", [128, 2, 256], F32) as ost:
            self.dma("sp", c32[:, 0], self.cnk_in[l].rearrange("(c p) n -> p c n", p=128), (), ["c32k"])
            self.dma("sp", c32[:, 1], self.cnv_in[l].rearrange("(c p) n -> p c n", p=128), (), ["c32v"])
            self.dma("sp", nabI[:].rearrange("p a b c -> p (a b c)"), self.nab_in[l, 2], (), ["nabI"])
            self.cp("dve", cvt[:], c32[:, 1], ["c32v"], ["cvt"])
            for kc in range(2):
                pst, pk = self.bank()
                for c in range(2):
                    self.tr(pst[:, c * 128:(c + 1) * 128], c32[:, 0, kc, c * 128:(c + 1) * 128], self.ident[:], ["c32k", "ident"], [pk])
                self.cp("dve", ckf[:, :, kc * 128:(kc + 1) * 128], pst[:, 0:256].rearrange("p (c n) -> p c n", c=2), [pk], [("ckf", kc)])
            self._wsched = [(C_NAQ, 128), (C_NAK, 128), (C_NAQ + 128, 128), (C_NAK + 128, 128), (C_NAV, 256), (C_NAK, 256)]
            self._wpref = None
            for c in range(2):
                def ev_q(win, pst, pk, c=c):
                    (s, toff, n, hc, gt) = win
                    self.act(qf[:, c, gt:gt + n], pst[:, 0:n], AF.Copy, [pk], [("qf", c, gt)], scale=0.125)
                self.proj_fm(l, C_NAQ + 128 * c, ev_q)

                def ev_k(win, pst, pk, c=c):
                    (s, toff, n, hc, gt) = win
                    self.cp("dve", kf[:, c, gt:gt + n], pst[:, 0:n], [pk], [("kf", c, gt)])
                self.proj_fm(l, C_NAK + 128 * c, ev_k)

            def ev_v(sl, pst, pk):
                (i, hc, gt) = sl
                if i < 4:
                    b = i % 2
                    self.cp("dve", ost[:, b, :], pst[:, 0:256], [pk], [("ost", b)])
                    self.out_toks.append(self.dma("sp", self.nav_out[i // 2, l, (i % 2) * 128:(i % 2 + 1) * 128, :], ost[:, b, :],
                                                  [("ost", b)], ()))
                    self.cp("act", vt[:, i, :], ost[:, b, :], [("ost", b)], [("vt", i)])
                else:
                    self.cp("act", vt[:, i, :], pst[:, 0:256], [pk], [("vt", i)])
            self.proj_tm(l, C_NAV, ev_v)

            def ev_kt(sl, pst, pk):
                (i, hc, gt) = sl
                b = i % 2
                self.cp("dve", ost[:, b, :], pst[:, 0:256], [pk], [("ost", b)])
                self.out_toks.append(self.dma("sp", self.nak_out[i // 2, l, (i % 2) * 128:(i % 2 + 1) * 128, :], ost[:, b, :],
                                              [("ost", b)], ()))
            self.proj_tm(l, C_NAK, ev_kt, slices=TM_SLICES[:4])
            P.barrier()
            for s in range(2):
                for hd in range(4):
                    c, pb = hd // 2, (hd % 2) * 64
                    pS, pSk = self.bank()
                    for kc in range(2):
                        self.mm(pS[:, kc * 256:(kc + 1) * 256], kf[pb:pb + 64, c, s * 256 + kc * 128:s * 256 + (kc + 1) * 128],
                                qf[pb:pb + 64, c, s * 256:(s + 1) * 256], True, True, [], [pSk])
                    self.act(pTp[:].rearrange("p a b -> p (a b)"), pS[:, 0:512], AF.Exp, [pSk], ["pTp"])
                    pO, pOk = self.bank()
                    for kc in range(2):
                        self.mm(pO[pb:pb + 64, 0:256], vt[:, s * 2 + kc, hd * 64:(hd + 1) * 64], pTp[:, kc, :], kc == 0, kc == 1,
                                ["pTp"], [pOk])
                    for kc in range(2):
                        self.mm(pO[pb:pb + 64, 256:512], self.onesb[:, 0:64], pTp[:, kc, :], kc == 0, kc == 1, ["pTp"], [pOk])
                    self.recip(rc_[pb:pb + 64, 0, 0:256], pO[pb:pb + 64, 256:512], [pOk], ["rc"])
                    self.tt("dve", self.mixed[pb:pb + 64, c, s * 256:(s + 1) * 256], pO[pb:pb + 64, 0:256], rc_[pb:pb + 64, 0, 0:256],
                            ALU.mult, [pOk, "rc"], [("mixed", c, s, hd)])
            for i in range(16):
                ty = NA_TYPES.get(i, 2)
                if ty == 2:
                    nab, nabk = nabI, "nabI"
                else:
                    self.dma("sp", nabE[:].rearrange("p a b c -> p (a b c)"), self.nab_in[l, ty], (), ["nabE"])
                    nab, nabk = nabE, "nabE"
                kr0 = min(max(2 * i - 4, 0), 22)
                qtok = 512 + i * 128
                def na_unit(hd, u):
                    c, pb = hd // 2, (hd % 2) * 64
                    sT, pT, rc = sT_[:, u], pT_[:, u], rc_[:, u]
                    q_ap = qf[pb:pb + 64, c, qtok:qtok + 128]
                    pA, pAk = self.bank()
                    pB, pBk = self.bank()
                    for m in range(5):
                        kt = 512 + (kr0 // 2 + m) * 128
                        dst = pA[:, m * 128:(m + 1) * 128] if m < 4 else pB[:, 0:128]
                        self.mm(dst, kf[pb:pb + 64, c, kt:kt + 128], q_ap, True, True, [], [pAk if m < 4 else pBk])
                    for kc in range(2):
                        self.mm(pB[:, 128 + kc * 128:256 + kc * 128], ckf[pb:pb + 64, c, kc * 128:(kc + 1) * 128], q_ap, True, True,
                                [], [pBk])
                    yield
                    self.tt("dve", sT[:, 0:512], pA[:, 0:512], nab[:, hd, 0:4, :].rearrange("p a b -> p (a b)"), ALU.add,
                            [pAk, nabk], [("sT0", u)])
                    yield
                    self.tt("dve", sT[:, 512:640], pB[:, 0:128], nab[:, hd, 4, :], ALU.add, [pBk, nabk], [("sT1", u)])
                    yield
                    pTf = pT.rearrange("p a b -> p (a b)")
                    self.act(pTf[:, 0:640], sT[:, 0:640], AF.Exp, [("sT0", u), ("sT1", u)], [("pT0", u)])
                    self.act(pTf[:, 640:896], pB[:, 128:384], AF.Exp, [pBk], [("pT1", u)])
                    yield
                    pO, pOk = self.bank()
                    for half in range(2):
                        for m in range(7):
                            if half == 0:
                                lhs = vt[:, 4 + kr0 // 2 + m, hd * 64:(hd + 1) * 64] if m < 5 else cvt[:, m - 5, hd * 64:(hd + 1) * 64]
                            else:
                                lhs = self.onesb[:, 0:64]
                            self.mm(pO[pb:pb + 64, half * 128:(half + 1) * 128], lhs, pT[:, m, :], m == 0, m == 6,
                                    [("pT0", u), ("pT1", u)], [pOk])
                    yield
                    self.recip(rc[pb:pb + 64, 0:128], pO[pb:pb + 64, 128:256], [pOk], [("rc", u)])
                    yield
                    self.tt("dve", self.mixed[pb:pb + 64, c, qtok:qtok + 128], pO[pb:pb + 64, 0:128], rc[pb:pb + 64, 0:128],
                            ALU.mult, [pOk, ("rc", u)], [("mixed", c, i, hd)])

                for h0 in (0, 2):
                    gens = [na_unit(h0 + u, u) for u in range(2)]
                    while gens:
                        for g in list(gens):
                            try:
                                next(g)
                            except StopIteration:
                                gens.remove(g)
            P.barrier()

    B.mixer_attn = mixer_attn

    def phase_mixers(self, l):
        import os
        which = os.environ.get("MIX", "abcd")
        with self.sb("winb", [128, 2, KC, 256], BF16) as winb:
            self.winb = winb
            self._wn = 0
            self.memset("pool", self.mixed[:], 0.0, ["mixed_all"])
            self.P.barrier()
            with ExitStack() as esb:
                if l == 0:
                    adab = esb.enter_context(self.sb("adab1", [128, 2, KC, 512], F32))
                    self._bg.append(self.gen_mod(self.cvt, adab, [1]))
                if "b" in which:
                    self.mixer_conv(l)
                if "a" in which:
                    self.mixer_attn(l)
                while self._bg:
                    self.bg_step()
                self.P.barrier()
            if "c" in which:
                self.mixer_ret(l)
            self.P.barrier()
        if "d" in which:
            self.mixer_rwkv(l)
        self.P.barrier()

    B.phase_mixers = phase_mixers


_install_attn()


CST = {}
_o = 0
for _n, _w in (("ident", 128), ("perm", 128), ("pdf", 128), ("pdb", 128), ("mf", 128), ("mb", 128), ("blk", 128),
               ("ef", 128), ("eb", 128), ("ecf", 1), ("ecb", 1), ("tri_i_f", 128), ("tri_e_f", 128), ("tri_i_b", 128),
               ("tri_e_b", 128), ("mstrict_f", 128), ("mstrict_b", 128)):
    CST[_n] = (_o, _w)
    _o += _w
NCST = _o


def make_cst():
    c = np.zeros((128, NCST), np.float32)
    j = np.arange(128)[:, None].astype(np.float32)
    t = np.arange(128)[None, :].astype(np.float32)

    def put(n, a):
        o, w = CST[n]
        c[:, o:o + w] = a

    put("ident", np.eye(128))
    pm = np.zeros((128, 128))
    for m in range(128):
        pm[m ^ 32, m] = 1.0
    put("perm", pm)
    put("pdf", np.maximum(t - j, 0))
    put("pdb", np.maximum(j - t, 0))
    put("mf", (t >= j))
    put("mb", (j >= t))
    blk = np.zeros((128, 128))
    blk[:64, :64] = 1
    blk[64:, 64:] = 1
    put("blk", blk)
    put("ef", np.broadcast_to(t + 1, (128, 128)))
    put("eb", np.broadcast_to(128 - t, (128, 128)))
    put("ecf", 127 - j)
    put("ecb", j)
    put("tri_i_f", (j <= t))
    put("tri_e_f", (j < t))
    put("tri_i_b", (j >= t))
    put("tri_e_b", (j > t))
    put("mstrict_f", (j < t))
    put("mstrict_b", (j > t))
    return c


def make_rope():
    tt = np.arange(2048)
    nfreq = 16
    inv = (10000.0 ** (-np.arange(nfreq, dtype=np.float32) / nfreq)).astype(np.float32)
    ang = np.concatenate([(tt // 64).astype(np.float32)[:, None] * inv, (tt % 64).astype(np.float32)[:, None] * inv], axis=-1)
    cos, sin = np.cos(ang).astype(np.float32), np.sin(ang).astype(np.float32)
    r = np.zeros((128, 2, 2048), np.float32)
    for p in range(128):
        d = p % 64
        r[p, 0] = cos[:, d % 32]
        r[p, 1] = (-sin[:, d % 32]) if d < 32 else sin[:, d % 32]
    return r


def _install_ret():
    B = Builder

    def C(self, name):
        o, w = CST[name]
        return self.cstt[:, o:o + w]

    B.C = C

    def mixer_ret(self, l):
        nc, P = self.nc, self.P
        with ExitStack() as es:
            A = lambda n, sh, dt: es.enter_context(self.sb(n, sh, dt))
            rq = A("rq", [128, 2, T], BF16)
            rk = A("rk", [128, 2, T], BF16)
            rg = A("rg", [128, 2, T], BF16)
            rv = A("rv", [128, 20, 256], BF16)
            qd = A("qd", [128, 2, 2, 2, 128], BF16)
            kT = A("kT", [128, 2, 256], BF16)
            rope = A("rope", [128, 2, 2048], F32)
            rdb = A("rdb", [128, 16], F32)
            lgc = A("lgc", [128, 2, 2], F32)
            gC = A("gC", [128, 2, 2], F32)
            DQ = A("DQ", [128, 2, 2, 128], F32)
            Dall = A("Dall", [128, 4, 128], F32)
            DK = A("DK", [128, 2, 4], F32)
            dtmp = A("dtmp", [128, 2, 128], F32)
            xr = A("xr", [128, 512], BF16)
            rt = A("rt", [128, 2, 512], F32)
            AT_ = A("AT", [128, 2, 4, 128], BF16)
            Srun = A("Srun", [128, 2, 2, 64], F32)
            Sall = A("Sall", [128, 20, 2, 2, 64], BF16)
            of_ = A("of", [128, 2, 2, 128], F32)
            ob_ = A("ob", [128, 2, 2, 2, 128], BF16)
            hst_ = A("hst", [128, 2, 3, 256], F32)
            epsr = A("epsr", [128, 1], F32)
            permb = A("permb", [128, 128], BF16)
            blkb = A("blkb", [128, 128], BF16)
            self.memset("pool", epsr[:], EPS, ["epsr"])
            self.cp("dve", permb[:], self.C("perm"), [], ["permb"])
            self.cp("dve", blkb[:], self.C("blk"), [], ["blkb"])
            self.dma("sp", rope[:], self.rope_in, (), ["rope"])
            self.dma("sp", rdb[:], self.retd_in.partition_broadcast(128), (), ["rdb"])
            self.act(rdb[:], rdb[:], AF.Exp, ["rdb"], ["rdb"], scale=-1.0)
            self.ts("dve", rdb[:], rdb[:], 1.0, None, ALU.add, None, ["rdb"], ["rdb"])
            self.act(rdb[:], rdb[:], AF.Ln, ["rdb"], ["rdb"])
            self.ts("dve", rdb[:], rdb[:], -1.0, None, ALU.mult, None, ["rdb"], ["rdb"])
            base = l * 8
            for c in range(2):
                for d in range(2):
                    for h2 in range(2):
                        col = base + d * 4 + 2 * c + h2
                        self.cp("dve", lgc[h2 * 64:(h2 + 1) * 64, c, d:d + 1], rdb[h2 * 64:(h2 + 1) * 64, col:col + 1], ["rdb"], [("lgc", c, d, h2)])
            P.barrier()
            for c in range(2):
                for d in range(2):
                    self.act(gC[:, c, d:d + 1], lgc[:, c, d:d + 1], AF.Exp, [], [("gC", c, d)], scale=128.0)
                    self.act(DQ[:, c, d, :], self.C("ef" if d == 0 else "eb"), AF.Exp, [], [("DQ", c, d)], scale=lgc[:, c, d:d + 1])
            for h in range(4):
                for d in range(2):
                    col = base + d * 4 + h
                    self.act(dtmp[:, d, :], self.C("pdf" if d == 0 else "pdb"), AF.Exp, [], [("dtmp", d)], scale=rdb[:, col:col + 1])
                    self.tt("dve", dtmp[:, d, :], dtmp[:, d, :], self.C("mf" if d == 0 else "mb"), ALU.mult, [("dtmp", d)], [("dtmp", d)])
                self.tt("dve", Dall[:, h, :], dtmp[:, 0, :], dtmp[:, 1, :], ALU.add, [("dtmp", 0), ("dtmp", 1)], [("Dall", h)])
            for d in range(2):
                self.ts("dve", DK[:, d, :], rdb[:, base + d * 4:base + d * 4 + 4], self.C("ecf" if d == 0 else "ecb"), None, ALU.mult, None,
                        [], [("DK", d)])
                self.act(DK[:, d, :], DK[:, d, :], AF.Exp, [("DK", d)], [("DK", d)])
            P.barrier()
            import os
            rcut = int(os.environ.get("RCUT", "9"))
            if rcut < 1:
                return
            self._wsched = [(C_RTQ, 128), (C_RTQ + 128, 128), (C_RTK, 128), (C_RTK + 128, 128), (C_RTG, 128), (C_RTG + 128, 128), (C_RTV, 256)]
            self._wpref = None
            for which, col0, dst, scl in (("q", C_RTQ, rq, 1.0), ("k", C_RTK, rk, 0.125)):
                for c in range(2):
                    def ev(win, pst, pk, c=c, dst=dst, scl=scl):
                        (s, toff, n, hc, gt) = win
                        if s < 2:
                            self.act(dst[:, c, gt:gt + n], pst[:, 0:n], AF.Copy, [pk], [("rqk", gt)], scale=scl)
                            return
                        self.act(xr[:, 0:n], pst[:, 0:n], AF.Copy, [pk], ["xr"], scale=scl)
                        psw, pswk = self.bank()
                        self.mm(psw[:, 0:n], permb[:], xr[:, 0:n], True, True, ["xr", "permb"], [pswk])
                        self.stt("dve", rt[:, 0, 0:n], pst[:, 0:n], scl, rope[:, 0, toff:toff + n], ALU.mult, ALU.mult, [pk, "rope"], ["rt0"])
                        self.tt("dve", rt[:, 1, 0:n], psw[:, 0:n], rope[:, 1, toff:toff + n], ALU.mult, [pswk, "rope"], ["rt1"])
                        self.tt("pool", dst[:, c, gt:gt + n], rt[:, 0, 0:n], rt[:, 1, 0:n], ALU.add, ["rt0", "rt1"], [("rqk", gt)])
                    self.proj_fm(l, col0 + 128 * c, ev)
            for c in range(2):
                def ev_g(win, pst, pk, c=c):
                    (s, toff, n, hc, gt) = win
                    self.act(rg[:, c, gt:gt + n], pst[:, 0:n], AF.Silu, [pk], [("rg", gt)])
                self.proj_fm(l, C_RTG + 128 * c, ev_g)

            def ev_v(sl, pst, pk):
                (i, hc, gt) = sl
                self.cp("act", rv[:, i, :], pst[:, 0:256], [pk], [("rv", i)])
            self.proj_tm(l, C_RTV, ev_v)
            P.barrier()
            if rcut < 2:
                return
            def run_interleaved(gens):
                gens = list(gens)
                while gens:
                    for g in list(gens):
                        try:
                            next(g)
                        except StopIteration:
                            gens.remove(g)

            def chain(s, i0, nch, d):
                if s == 2:
                    for c in range(2):
                        self.dma("sp", Srun[:, d, c, :], self.sret_in[l, d, 2 * c:2 * c + 2].rearrange("h e v -> (h e) v"), (), [("Srun", d, c)])
                else:
                    self.memset("pool", Srun[:, d], 0.0, [("Srun", d, 0), ("Srun", d, 1)])
                yield
                order = list(range(i0, i0 + nch)) if d == 0 else list(range(i0 + nch - 1, i0 - 1, -1))
                for i in order:
                    self.cp("act", Sall[:, i, :, d, :], Srun[:, d], [("Srun", d, 0), ("Srun", d, 1)], [("Sall", i, d)])
                    ptr, ptrk = self.bank()
                    pb16 = ptr[:].bitcast(BF16)
                    for c in range(2):
                        self.tr(pb16[:, c * 128:(c + 1) * 128], rk[:, c, i * 128:(i + 1) * 128], self.identb[:], ["identb"], [ptrk])
                    yield
                    self.tt("dve", kT[:, d, :].rearrange("p (h e) -> p h e", h=4), pb16[:, 0:256].rearrange("p (h e) -> p h e", h=4),
                            DK[:, d, :].unsqueeze(2).to_broadcast([128, 4, 64]), ALU.mult, [ptrk], [("kT", d)])
                    yield
                    pkv, pkvk = self.bank()
                    for c in range(2):
                        self.mm(pkv[:, c * 128:(c + 1) * 128], kT[:, d, c * 128:(c + 1) * 128], rv[:, i, c * 128:(c + 1) * 128],
                                True, True, [("kT", d)], [pkvk])
                    yield
                    for c in range(2):
                        for h2 in range(2):
                            pr = slice(h2 * 64, (h2 + 1) * 64)
                            self.stt("dve", Srun[pr, d, c, :], Srun[pr, d, c, :], gC[pr, c, d:d + 1],
                                     pkv[pr, c * 128 + h2 * 64:c * 128 + (h2 + 1) * 64], ALU.mult, ALU.add,
                                     [pkvk, ("Srun", d, c), ("Sall", i, d)], [("Srun", d, c)])
                    yield
                if s < 2:
                    for c in range(2):
                        self.out_toks.append(self.dma("sp", self.oret_out[s, l, d, 2 * c:2 * c + 2].rearrange("h e v -> (h e) v"),
                                                      Srun[:, d, c, :], [("Srun", d, c)], ()))

            def outunit(i, u):
                tk = i * 128
                AT, of, ob, hst = AT_[:, u], of_[:, u], ob_[:, u], hst_[:, u]
                pkqs = []
                for par in range(2):
                    pkq, pkqk = self.bank()
                    pkqs.append((pkq, pkqk))
                    for c in range(2):
                        pb = par * 64
                        self.mm(pkq[:, c * 128:(c + 1) * 128], rk[pb:pb + 64, c, tk:tk + 128], rq[pb:pb + 64, c, tk:tk + 128], True, True, [], [pkqk])
                for d in range(2):
                    self.tt("dve", qd[:, u, d], rq[:, :, tk:tk + 128], DQ[:, :, d, :], ALU.mult, [], [("qd", u, d)])
                yield
                for par in range(2):
                    pkq, pkqk = pkqs[par]
                    self.tt("dve", AT[:, par::2, :], pkq[:, 0:256].rearrange("p (c t) -> p c t", c=2), Dall[:, par::2, :], ALU.mult,
                            [pkqk], [("AT", u, par), ("AT", u, par + 2)])
                yield
                po, pok = self.bank()
                for h in (0, 2, 1, 3):
                    c, pb = h // 2, (h % 2) * 64
                    dst = po[pb:pb + 64, c * 128:(c + 1) * 128]
                    self.mm(dst, rv[:, i, h * 64:(h + 1) * 64], AT[:, h, :], True, False, [("AT", u, h)], [pok])
                    for d in range(2):
                        self.mm(dst, Sall[pb:pb + 64, i, c, d, :], qd[pb:pb + 64, u, d, c, :], False, d == 1,
                                [("Sall", i, 0), ("Sall", i, 1), ("qd", u, d)], [pok])
                yield
                yield from self.head_norm_gen(l, po, pok, of, ob, hst, blkb, epsr, f"ret_gn_{l}", rg, 4, tk, None, None, u)

            for (s, i0, nch) in ((0, 0, 2), (1, 2, 2), (2, 4, 16)):
                run_interleaved([chain(s, i0, nch, d) for d in range(2)])
                for a in range(i0, i0 + nch, 2):
                    run_interleaved([outunit(i, u) for u, i in enumerate(range(a, min(a + 2, i0 + nch)))])
            P.barrier()

    B.mixer_ret = mixer_ret

    def head_norm_out(self, l, po, pok, of, ob, hst, blkb, epsr, eps, gname, gate, mix0, tk, bonus):
        if po is not None:
            self.cp("dve", of[:].rearrange("p c t -> p (c t)"), po[:, 0:256], [pok], ["of"])
        self.act(ob[:, 0].rearrange("p c t -> p (c t)"), of[:].rearrange("p c t -> p (c t)"), AF.Copy, ["of"], ["ob0"])
        self.act(ob[:, 1].rearrange("p c t -> p (c t)"), of[:].rearrange("p c t -> p (c t)"), AF.Square, ["of"], ["ob1"])
        ps, psk = self.bank()
        self.mm(ps[:, 0:256], blkb[:], ob[:, 0].rearrange("p c t -> p (c t)"), True, True, ["ob0", "blkb"], [psk])
        self.mm(ps[:, 256:512], blkb[:], ob[:, 1].rearrange("p c t -> p (c t)"), True, True, ["ob1", "blkb"], [psk])
        mean, msq, var = hst[:, 0, :], hst[:, 1, :], hst[:, 2, :]
        self.ts("dve", mean, ps[:, 0:256], 1.0 / 64, None, ALU.mult, None, [psk], ["hmean"])
        self.tt("dve", msq, mean, mean, ALU.mult, ["hmean"], ["hmsq"])
        self.stt("dve", var, ps[:, 256:512], 1.0 / 64, msq, ALU.mult, ALU.subtract, [psk, "hmsq"], ["hvar"])
        self.act(var, var, AF.Sqrt, ["hvar"], ["hvar"], bias=epsr[:, 0:1])
        self.recip(var, var, ["hvar"], ["hvar"])
        ofl = of[:].rearrange("p c t -> p (c t)")
        self.tt("dve", ofl, ofl, mean, ALU.subtract, ["of", "hmean"], ["of"])
        self.tt("dve", ofl, ofl, var, ALU.mult, ["of", "hvar"], ["of"])
        for c in range(2):
            if bonus is None:
                self.stt("dve", self.mixed[:, mix0 + c, tk:tk + 128], of[:, c, :], self.V(gname, c), gate[:, c, tk:tk + 128],
                         ALU.mult, ALU.mult, ["of"], [("mixed", mix0 + c, tk)])
            else:
                self.stt("dve", of[:, c, :], of[:, c, :], self.V(gname, c), bonus[:, c, :], ALU.mult, ALU.add, ["of"], ["of"])
                self.tt("dve", self.mixed[:, mix0 + c, tk:tk + 128], of[:, c, :], gate[:, c, tk:tk + 128], ALU.mult, ["of"],
                        [("mixed", mix0 + c, tk)])

    B.head_norm_out = head_norm_out

    def head_norm_gen(self, l, po, pok, of, ob, hst, blkb, epsr, gname, gate, mix0, tk, bonus, add, u):
        fl = lambda a: a.rearrange("p c t -> p (c t)")
        K_ = lambda n: (n, u)
        if add is None:
            self.cp("dve", fl(of), po[:, 0:256], [pok], [K_("of")])
        else:
            self.tt("dve", of, po[:, 0:256].rearrange("p (c t) -> p c t", c=2), add, ALU.add, [pok], [K_("of")])
        yield
        self.act(fl(ob[:, 0]), fl(of), AF.Copy, [K_("of")], [K_("ob0")])
        self.act(fl(ob[:, 1]), fl(of), AF.Square, [K_("of")], [K_("ob1")])
        yield
        ps, psk = self.bank()
        self.mm(ps[:, 0:256], blkb[:], fl(ob[:, 0]), True, True, [K_("ob0")], [psk])
        self.mm(ps[:, 256:512], blkb[:], fl(ob[:, 1]), True, True, [K_("ob1")], [psk])
        yield
        mean, msq, var = hst[:, 0, :], hst[:, 1, :], hst[:, 2, :]
        self.act(mean, ps[:, 0:256], AF.Copy, [psk], [K_("hmean")], scale=1.0 / 64)
        yield
        self.tt("dve", msq, mean, mean, ALU.mult, [K_("hmean")], [K_("hmsq")])
        yield
        self.stt("dve", var, ps[:, 256:512], 1.0 / 64, msq, ALU.mult, ALU.subtract, [psk, K_("hmsq")], [K_("hvar")])
        yield
        self.act(var, var, AF.Sqrt, [K_("hvar")], [K_("hvar")], bias=epsr[:, 0:1])
        self.tt("dve", fl(of), fl(of), mean, ALU.subtract, [K_("of"), K_("hmean")], [K_("of")])
        yield
        self.recip(var, var, [K_("hvar")], [K_("hvar")])
        yield
        self.tt("dve", fl(of), fl(of), var, ALU.mult, [K_("of"), K_("hvar")], [K_("of")])
        yield
        for c in range(2):
            if bonus is None:
                self.stt("dve", self.mixed[:, mix0 + c, tk:tk + 128], of[:, c, :], self.V(gname, c), gate[:, c, tk:tk + 128],
                         ALU.mult, ALU.mult, [K_("of")], [("mixed", mix0 + c, tk)])
            else:
                self.stt("dve", of[:, c, :], of[:, c, :], self.V(gname, c), bonus[:, c, :], ALU.mult, ALU.add, [K_("of")], [K_("of")])
        if bonus is not None:
            yield
            for c in range(2):
                self.tt("dve", self.mixed[:, mix0 + c, tk:tk + 128], of[:, c, :], gate[:, c, tk:tk + 128], ALU.mult, [K_("of")],
                        [("mixed", mix0 + c, tk)])

    B.head_norm_gen = head_norm_gen


_install_ret()


def _install_rwkv():
    B = Builder
    NEG_E = -float(np.exp(-0.5))

    def mixer_rwkv(self, l):
        nc, P = self.nc, self.P
        with ExitStack() as es:
            A = lambda n, sh, dt: es.enter_context(self.sb(n, sh, dt))
            dr = A("dr", [128, 2, T], BF16)
            dk = A("dk", [128, 2, T], BF16)
            wlt = A("wlt", [128, T], BF16)
            alt = A("alt", [128, T], BF16)
            sg = self.mixed[:, 7, :]
            ball = self.mixed[:, 6:8, :]
            sc = A("rsc", [128, 8], F32)
            eps12 = A("eps12", [128, 1], F32)
            epsg = A("epsg", [128, 1], F32)
            blkb = A("blkb2", [128, 128], BF16)
            HB = [self.hbuf[:, 2 * n:2 * n + 2, 0:T] for n in range(4)]
            self.memset("pool", eps12[:], 1e-12, ["e12"])
            self.memset("pool", epsg[:], 64e-5, ["epsg"])
            self.cp("dve", blkb[:], self.C("blk"), [], ["blkb2"])
            ka, rkv = self.V(f"ka_{l}"), self.V(f"rk_{l}")
            self.tt("dve", sc[:, 0:2], rkv, ka, ALU.mult, [], ["sc0"])
            self.ts("dve", sc[:, 4:6], ka, -1.0, 1.0, ALU.mult, ALU.add, [], ["sc4"])
            self.ts("dve", sc[:, 2:4], sc[:, 4:6], 2.0, None, ALU.mult, None, ["sc4"], ["sc2"])
            self.tt("dve", sc[:, 2:4], sc[:, 2:4], rkv, ALU.mult, ["sc2"], ["sc2"])
            kk = A("kk", [128, 2, T], BF16)
            vtm = A("vtm", [128, 20, 256], BF16)
            lorab = A("lorab", [128, 1280], BF16)
            w0b = A("w0b", [128, 512], F32)
            m4 = A("m4", [128, 2, 512], F32)
            mT = A("mT", [128, 2, 128], F32)
            es_dv = es.enter_context(ExitStack())
            dv = es_dv.enter_context(self.sb("dv", [128, 2, T], BF16))
            with ExitStack() as es2:
                self.winb = es2.enter_context(self.sb("winb", [128, 2, KC, 256], BF16))
                self._wn = 0
                raw = es2.enter_context(self.sb("raw", [128, 6, TP], BF16))
                dsh = es2.enter_context(self.sb("dsh", [128, 3, 6, 128], BF16))
                self.memset("pool", raw[:], 0.0, ["raw_all"])
                for tp in range(3):
                    for ch in range(6):
                        self.ts(("dve", "pool")[(tp + ch) % 2], dsh[:, tp, ch, :], self.identb[:], self.V(f"shift{tp}_{l}", ch), None,
                                ALU.mult, None, [], [("dsh", tp, ch)])
                P.barrier()
                self._wsched = [(C_RWR + 128 * ch, 128) for ch in range(6)] + [(C_WLOW, 128), (C_GLOW, 128)]
                self._wpref = None
                for ch in range(6):
                    def ev(win, pst, pk, ch=ch):
                        (s, toff, n, hc, gt) = win
                        self.cp(("act", "dve")[ch % 2], raw[:, ch, hc:hc + n], pst[:, 0:n], [pk], [("raw", ch, gt)])
                    self.proj_fm(l, C_RWR + 128 * ch, ev)

                def ev_wa(win, pst, pk):
                    (s, toff, n, hc, gt) = win
                    self.act(wlt[0:64, gt:gt + n], pst[0:64, 0:n], AF.Tanh, [pk], [("wlt", gt)])
                    self.cp("dve", alt[64:128, gt:gt + n], pst[64:128, 0:n], [pk], [("alt", gt)])
                self.proj_fm(l, C_WLOW, ev_wa)

                def ev_g(win, pst, pk):
                    (s, toff, n, hc, gt) = win
                    self.act(sg[:, gt:gt + n], pst[:, 0:n], AF.Sigmoid, [pk], [("sg", gt)])
                self.proj_fm(l, C_GLOW, ev_g)
                P.barrier()
                for win in SEQ_WINS:
                    (s, toff, n, hc, gt) = win
                    for ch in range(6):
                        dst = (dr, dk, dv)[ch // 2]
                        pc, pck = self.bank()
                        for tp in range(3):
                            self.mm(pc[:, 0:n], dsh[:, tp, ch, :], raw[:, ch, hc - 1 + tp:hc - 1 + tp + n], tp == 0, tp == 2, [], [pck])
                        self.cp(("act", "dve")[ch % 2], dst[:, ch % 2, gt:gt + n], pc[:, 0:n], [pck], [("d", ch, gt)])
                P.barrier()
            for d in range(2):
                st_, in_ = ("mstrict_f", "mf") if d == 0 else ("mstrict_b", "mb")
                stT = "mstrict_b" if d == 0 else "mstrict_f"
                self.ts("dve", m4[:, d, 0:128], self.C(st_), -1.0, None, ALU.mult, None, [], [("m4", d, 0)])
                self.cp("dve", m4[:, d, 128:256], self.C(in_), [], [("m4", d, 1)])
                self.cp("dve", m4[:, d, 256:384], self.C(st_), [], [("m4", d, 2)])
                self.cp("dve", m4[:, d, 384:512], self.C(in_), [], [("m4", d, 3)])
                self.ts("dve", mT[:, d, :], self.C(stT), -1.0, None, ALU.mult, None, [], [("mT", d)])
            self.dma("pool", lorab[:], self.lora_in[l], (), ["lorab"])
            self.dma("sp", w0b[:], self.w0row_in[:, l * 512:(l + 1) * 512].partition_broadcast(128), (), ["w0b"])
            P.barrier()
            with ExitStack() as es3:
                tf = es3.enter_context(self.sb("ktf", [128, 2, 512], F32))
                tb = es3.enter_context(self.sb("ktb", [128, 2, 512], BF16))
                rn = es3.enter_context(self.sb("krn", [128, 2, 512], F32))
                asg = es3.enter_context(self.sb("asg", [128, 2, 2, 512], F32))
                for win in SEQ_WINS:
                    (s, toff, n, hc, gt) = win
                    for c in range(2):
                        self.ts("dve", tf[:, c, 0:n], dk[:, c, gt:gt + n], self.V(f"kk_{l}", c), None, ALU.mult, None, [], [("tf", c)])
                        self.act(tb[:, c, 0:n], tf[:, c, 0:n], AF.Square, [("tf", c)], [("tb", c)])
                        pq, pqk = self.bank()
                        self.mm(pq[:, 0:n], blkb[:], tb[:, c, 0:n], True, True, [("tb", c), "blkb2"], [pqk])
                        self.act(rn[:, c, 0:n], pq[:, 0:n], AF.Sqrt, [pqk, "e12"], [("rn", c)], bias=eps12[:, 0:1])
                        self.recip(rn[:, c, 0:n], rn[:, c, 0:n], [("rn", c)], [("rn", c)])
                        self.tt("dve", kk[:, c, gt:gt + n], tf[:, c, 0:n], rn[:, c, 0:n], ALU.mult, [("tf", c), ("rn", c)], [("kk", c, gt)])
                    for c in range(2):
                        pg, pgk = self.bank()
                        self.mm(pg[:, 0:n], lorab[:, 1024 + c * 128:1024 + (c + 1) * 128], sg[:, gt:gt + n], True, True, ["lorab"], [pgk])
                        self.cp("act", HB[2][:, c, gt:gt + n], pg[:, 0:n], [pgk], [("gate", c, gt)])
                    for d in range(2):
                        for c in range(2):
                            pa, pak = self.bank()
                            self.mm(pa[:, 0:n], lorab[64:128, 512 + d * 256 + c * 128:512 + d * 256 + (c + 1) * 128], alt[64:128, gt:gt + n],
                                    True, True, ["lorab"], [pak])
                            self.act(asg[:, d, c, 0:n], pa[:, 0:n], AF.Sigmoid, [pak], [("asg", d, c)], bias=self.V(f"a0_{d}_{l}", c))
                    for c in range(2):
                        self.tt("dve", asg[:, 0, c, 0:n], asg[:, 0, c, 0:n], asg[:, 1, c, 0:n], ALU.add, [("asg", 0, c), ("asg", 1, c)], [("asg", 0, c)])
                        self.ts("dve", asg[:, 0, c, 0:n], asg[:, 0, c, 0:n], sc[:, c:c + 1], sc[:, 2 + c:3 + c], ALU.mult, ALU.add,
                                [("asg", 0, c), "sc0", "sc2"], [("asg", 0, c)])
                        self.tt("dve", tf[:, c, 0:n], dr[:, c, gt:gt + n], dk[:, c, gt:gt + n], ALU.mult, [("tf", c)], [("tf", c)])
                        self.tt("dve", tb[:, c, 0:n], tf[:, c, 0:n], asg[:, 0, c, 0:n], ALU.mult, [("tf", c), ("asg", 0, c)], [("tb", c)])
                        pq, pqk = self.bank()
                        self.mm(pq[:, 0:n], blkb[:], tb[:, c, 0:n], True, True, [("tb", c), "blkb2"], [pqk])
                        self.tt("dve", HB[3][:, c, gt:gt + n], pq[:, 0:n], dv[:, c, gt:gt + n], ALU.mult, [pqk], [("bonus", c, gt)])
                for i in range(20):
                    pt, ptk = self.bank()
                    p16 = pt[:].bitcast(BF16)
                    for c in range(2):
                        self.tr(p16[:, c * 128:(c + 1) * 128], dv[:, c, i * 128:(i + 1) * 128], self.identb[:], [], [ptk])
                    self.cp("act", vtm[:, i, :], p16[:, 0:256], [ptk], [("vtm", i)])
                P.barrier()
            es_dv.close()
            with ExitStack() as es4:
                A4 = lambda n, sh, dt: es4.enter_context(self.sb(n, sh, dt))
                zt_ = A4("zt", [128, 2, 256], F32)
                lw_ = zt_
                cl2_ = A4("cl2", [128, 2, 2], F32)
                gam = A4("gam", [128, 2, 2], F32)
                E_ = A4("E", [128, 2, 4, 2, 128], BF16)
                BK_ = A4("BK", [128, 2, 4, 2, 128], BF16)
                KR = A4("KR", [128, 2, 2, 2, 128], BF16)
                BKt = A4("BKt", [128, 2, 2, 256], BF16)
                MN = A4("MN", [128, 2, 4, 4, 128], BF16)
                Nb = A4("Nb", [128, 2, 1, 4, 128], BF16)
                NTb = A4("NTb", [128, 2, 1, 4, 128], BF16)
                WK = A4("WK", [128, 2, 6, 4, 128], BF16)
                bm = A4("bm", [128, 5, 128], BF16)
                self.dma("pool", bm[:].rearrange("p a b -> p (a b)"), self.bmask_in, (), ["bm"])
                Xn = A4("Xn", [128, 256], BF16)
                UT = A4("UT", [128, 256], BF16)
                Zrun = A4("Zrun", [128, 2, 64], F32)
                Zbd = A4("Zbd", [128, 2, 128], BF16)
                zin = A4("zin", [128, 2, 64], F32)
                zo = A4("zo", [64, 4, 64], F32)
                of_ = A4("of2", [128, 2, 2, 128], F32)
                ob_ = A4("ob2", [128, 2, 2, 2, 128], BF16)
                hst_ = A4("hst2", [128, 2, 3, 256], F32)
                self.memset("pool", Zbd[:], 0.0, ["Zbd_all"])
                P.barrier()

                def early(i, q, d, kd, tri, last):
                    tk = i * 128
                    sl = slice(tk, tk + 128)
                    zt, lw, cl2, E, BK = zt_[:, q], lw_[:, q], cl2_[:, q], E_[:, q], BK_[:, q]
                    pz, pzk = self.bank()
                    self.mm(pz[:, 0:256], wlt[0:64, tk:tk + 128], lorab[0:64, d * 256:(d + 1) * 256], True, True, [], [pzk])
                    yield
                    self.tt("dve", zt, pz[:, 0:256], w0b[:, d * 256:(d + 1) * 256], ALU.add, [pzk], [("zt", q)])
                    yield
                    self.act(lw, zt, AF.Sigmoid, [("zt", q)], [("lw", q)])
                    yield
                    self.ts("dve", lw, lw, NEG_E, None, ALU.mult, None, [("lw", q)], [("lw", q)])
                    yield
                    pcl, pclk = self.bank()
                    for c in range(2):
                        self.mm(pcl[:, c * 256:(c + 1) * 256], lw[:, c * 128:(c + 1) * 128], tri, True, True, [("lw", q)], [pclk])
                    yield
                    V3 = pcl[:, 0:512].rearrange("p (c x) -> p c x", c=2)
                    incl, excl = V3[:, :, 0:128], V3[:, :, 128:256]
                    self.cp("dve", cl2.unsqueeze(2), V3[:, :, last:last + 1], [pclk], [("cl2", q)])
                    self.act(E[:, 0], excl, AF.Exp, [pclk], [("E", q, 0)])
                    yield
                    self.act(E[:, 1], incl, AF.Exp, [pclk], [("E", q, 1)])
                    self.tt("dve", KR[:, q, :, 0, :], kk[:, :, sl], E[:, 0], ALU.mult, [("E", q, 0)], [("KR", q, 0)])
                    yield
                    self.act(E[:, 2], incl, AF.Exp, [pclk], [("E", q, 2)], scale=-1.0)
                    self.tt("pool", KR[:, q, :, 1, :], dr[:, :, sl], E[:, 1], ALU.mult, [("E", q, 1)], [("KR", q, 1)])
                    yield
                    for c in range(2):
                        self.act(E[:, 3, c, :], V3[:, c, 0:128], AF.Exp, [pclk, ("cl2", q)], [("E", q, 3, c)], scale=-1.0, bias=cl2[:, c:c + 1])
                    self.tt("dve", BK[:, 0], ball[:, :, sl], E[:, 2], ALU.mult, [("E", q, 2)], [("BK", q, 0)])
                    self.tt("pool", BK[:, 1], kd[:, :, sl], E[:, 2], ALU.mult, [("E", q, 2)], [("BK", q, 1)])
                    yield
                    self.act(gam[:, q, :], cl2, AF.Exp, [("cl2", q)], [("gam", q)])
                    self.tt("dve", BK[:, 2], ball[:, :, sl], E[:, 3], ALU.mult, [("E", q, 3, 0), ("E", q, 3, 1)], [("BK", q, 2)])
                    self.tt("pool", BK[:, 3], kd[:, :, sl], E[:, 3], ALU.mult, [("E", q, 3, 0), ("E", q, 3, 1)], [("BK", q, 3)])
                    for h in (0, 2, 1, 3):
                        c, pb = h // 2, (h % 2) * 64
                        pmn, pmnk = self.bank()
                        rhs = KR[pb:pb + 64, q, c].rearrange("p a t -> p (a t)")
                        self.mm(pmn[:, 0:256], BK[pb:pb + 64, 0, c, :], rhs, True, True, [("BK", q, 0), ("KR", q, 0), ("KR", q, 1)], [pmnk])
                        self.mm(pmn[:, 256:512], BK[pb:pb + 64, 1, c, :], rhs, True, True, [("BK", q, 1), ("KR", q, 0), ("KR", q, 1)], [pmnk])
                        yield
                        self.tt("dve", MN[:, q, h].rearrange("p a t -> p (a t)"), pmn[:, 0:512], m4[:, d, :], ALU.mult, [pmnk], [("MN", q, h)])
                    for par in range(2):
                        pmt, pmtk = self.bank()
                        pb = par * 64
                        for c in range(2):
                            self.mm(pmt[:, c * 128:(c + 1) * 128], KR[pb:pb + 64, q, c, 0, :], BK[pb:pb + 64, 0, c, :], True, True,
                                    [("BK", q, 0), ("KR", q, 0)], [pmtk])
                        yield
                        self.tt("dve", NTb[:, q, 0, par::2, :], pmt[:, 0:256].rearrange("p (c t) -> p c t", c=2),
                                mT[:, d, :].unsqueeze(1).to_broadcast([128, 2, 128]), ALU.mult, [pmtk],
                                [("NT", q, 0, par), ("NT", q, 0, par + 2)])
                    ptb, ptbk = self.bank()
                    p16 = ptb[:].bitcast(BF16)
                    for w_ in range(2):
                        for c in range(2):
                            self.tr(p16[:, w_ * 256 + c * 128:w_ * 256 + (c + 1) * 128], BK[:, 2 + w_, c, :], self.identb[:],
                                    [("BK", q, 2 + w_)], [ptbk])
                    yield
                    self.cp("act", BKt[:, q].rearrange("p w x -> p (w x)"), p16[:, 0:512], [ptbk], [("BKt", q)])
                    H4 = range(4)
                    self.cp("act", Nb[:, q, 0], MN[:, q, :, 0, :], [("MN", q, h) for h in H4], [("N", q, 0, h) for h in H4])
                    yield
                    yield from stable_inverse(q)

                def stable_inverse(q):
                    H = range(4)
                    fl = lambda a: a.rearrange("p h t -> p (h t)")
                    N0, N0T = Nb[:, q, 0], NTb[:, q, 0]
                    W = lambda k: WK[:, q, k]
                    kN0 = [("N", q, 0, h) for h in H]
                    kNT0 = [("NT", q, 0, h) for h in H]
                    kW = lambda k: [("WK", q, k)]
                    mb = lambda m: bm[:, m, :].unsqueeze(1).to_broadcast([128, 4, 128])
                    idb = self.identb[:].unsqueeze(1).to_broadcast([128, 4, 128])
                    self.tt("dve", W(0), N0, mb(0), ALU.mult, kN0 + ["bm"], kW(0))
                    self.tt("pool", W(1), N0T, mb(0), ALU.mult, kNT0 + ["bm"], kW(1))
                    yield
                    self.tt("dve", W(4), W(0), idb, ALU.add, kW(0), kW(4))
                    self.tt("pool", W(5), W(1), idb, ALU.add, kW(1), kW(5))
                    yield
                    ca, cb = (0, 1), (2, 3)
                    for lvl in range(2):
                        cur, nxt = (ca, cb) if lvl % 2 == 0 else (cb, ca)
                        pn, pnk = self.bank()
                        for h in H:
                            self.mm(pn[:, h * 128:(h + 1) * 128], W(cur[1])[:, h, :], W(cur[0])[:, h, :], True, True, kW(cur[0]) + kW(cur[1]), [pnk])
                        pnt, pntk = self.bank()
                        for h in H:
                            self.mm(pnt[:, h * 128:(h + 1) * 128], W(cur[0])[:, h, :], W(cur[1])[:, h, :], True, True, kW(cur[0]) + kW(cur[1]), [pntk])
                        yield
                        self.cp("act", fl(W(nxt[0])), pn[:, 0:512], [pnk], kW(nxt[0]))
                        self.cp("dve", fl(W(nxt[1])), pnt[:, 0:512], [pntk], kW(nxt[1]))
                        yield
                        pp, ppk = self.bank()
                        for h in H:
                            self.mm(pp[:, h * 128:(h + 1) * 128], self.identb[:], W(4)[:, h, :], True, False, kW(4), [ppk])
                            self.mm(pp[:, h * 128:(h + 1) * 128], W(nxt[1])[:, h, :], W(4)[:, h, :], False, True, kW(nxt[1]) + kW(4), [ppk])
                        ppt, pptk = self.bank()
                        for h in H:
                            self.mm(ppt[:, h * 128:(h + 1) * 128], self.identb[:], W(5)[:, h, :], True, False, kW(5), [pptk])
                            self.mm(ppt[:, h * 128:(h + 1) * 128], W(nxt[0])[:, h, :], W(5)[:, h, :], False, True, kW(nxt[0]) + kW(5), [pptk])
                        yield
                        self.cp("act", fl(W(4)), pp[:, 0:512], [ppk], kW(4))
                        self.cp("act", fl(W(5)), ppt[:, 0:512], [pptk], kW(5))
                        yield
                    for m in (1, 2, 3, 4):
                        lastm = m == 4
                        p1, p1k = self.bank()
                        for h in H:
                            self.mm(p1[:, h * 128:(h + 1) * 128], N0T[:, h, :], W(4)[:, h, :], True, True, kNT0 + kW(4), [p1k])
                        if not lastm:
                            q1, q1k = self.bank()
                            for h in H:
                                self.mm(q1[:, h * 128:(h + 1) * 128], N0[:, h, :], W(5)[:, h, :], True, True, kN0 + kW(5), [q1k])
                        yield
                        m4b = bm[:, m, :].unsqueeze(1).to_broadcast([128, 4, 128])
                        self.tt("dve", W(2), p1[:, 0:512].rearrange("p (h t) -> p h t", h=4), m4b, ALU.mult, [p1k, "bm"], kW(2))
                        if not lastm:
                            self.tt("dve", W(3), q1[:, 0:512].rearrange("p (h t) -> p h t", h=4), m4b, ALU.mult, [q1k, "bm"], kW(3))
                        yield
                        pa, pak = self.bank()
                        for h in H:
                            self.mm(pa[:, h * 128:(h + 1) * 128], self.identb[:], W(4)[:, h, :], True, False, kW(4), [pak])
                            self.mm(pa[:, h * 128:(h + 1) * 128], W(5)[:, h, :], W(2)[:, h, :], False, True, kW(5) + kW(2), [pak])
                        if not lastm:
                            pb_, pbk = self.bank()
                            for h in H:
                                self.mm(pb_[:, h * 128:(h + 1) * 128], self.identb[:], W(5)[:, h, :], True, False, kW(5), [pbk])
                                self.mm(pb_[:, h * 128:(h + 1) * 128], W(4)[:, h, :], W(3)[:, h, :], False, True, kW(4) + kW(3), [pbk])
                        yield
                        self.cp("act", fl(W(4)), pa[:, 0:512], [pak], kW(4))
                        if not lastm:
                            self.cp("act", fl(W(5)), pb_[:, 0:512], [pbk], kW(5))
                        yield

                def chain(i, q, d):
                    tk = i * 128
                    sl = slice(tk, tk + 128)
                    pX, pXk = self.bank()
                    for h in range(4):
                        c, h2 = h // 2, h % 2
                        self.mm(pX[:, h * 64:(h + 1) * 64], KR[:, q, c, 0, :], Zbd[:, c, h2 * 64:(h2 + 1) * 64], True, False,
                                [("KR", q, 0), ("Zbd", 0), ("Zbd", 1)], [pXk])
                        self.mm(pX[:, h * 64:(h + 1) * 64], MN[:, q, h, 2, :], vtm[:, i, h * 64:(h + 1) * 64], False, True, [("MN", q, h)], [pXk])
                    yield None
                    self.act(Xn[:], pX[:, 0:256], AF.Copy, [pXk], ["Xn"], scale=-1.0)
                    yield None
                    pU, pUk = self.bank()
                    for h in range(4):
                        self.mm(pU[:, h * 64:(h + 1) * 64], WK[:, q, 4, h, :], Xn[:, h * 64:(h + 1) * 64], True, True, [("WK", q, 4), "Xn"], [pUk])
                    yield None
                    self.cp("act", UT[:], pU[:, 0:256], [pUk], ["UT"])
                    yield None
                    pO, pOk = self.bank()
                    for h in range(4):
                        c, h2 = h // 2, h % 2
                        pb = h2 * 64
                        dst = pO[pb:pb + 64, c * 128:(c + 1) * 128]
                        self.mm(dst, Zbd[:, c, pb:pb + 64], KR[:, q, c, 1, :], True, False, [("KR", q, 1), ("Zbd", 0), ("Zbd", 1)], [pOk])
                        self.mm(dst, UT[:, h * 64:(h + 1) * 64], MN[:, q, h, 1, :], False, False, ["UT", ("MN", q, h)], [pOk])
                        self.mm(dst, vtm[:, i, h * 64:(h + 1) * 64], MN[:, q, h, 3, :], False, True, [("MN", q, h)], [pOk])
                    pZ, pZk = self.bank()
                    for h in range(4):
                        c, pb = h // 2, (h % 2) * 64
                        dst = pZ[pb:pb + 64, c * 64:(c + 1) * 64]
                        self.mm(dst, BKt[:, q, 0, h * 64:(h + 1) * 64], UT[:, h * 64:(h + 1) * 64], True, False, [("BKt", q), "UT"], [pZk])
                        self.mm(dst, BKt[:, q, 1, h * 64:(h + 1) * 64], vtm[:, i, h * 64:(h + 1) * 64], False, True, [("BKt", q)], [pZk])
                    yield None
                    for c in range(2):
                        self.stt("dve", Zrun[:, c, :], Zrun[:, c, :], gam[:, q, c:c + 1], pZ[:, c * 64:(c + 1) * 64], ALU.mult, ALU.add,
                                 [pZk, ("gam", q), ("Zrun", 0), ("Zrun", 1)], [("Zrun", 0), ("Zrun", 1)])
                    yield None
                    for par in range(2):
                        pb = par * 64
                        self.cp("act", Zbd[pb:pb + 64, :, pb:pb + 64], Zrun[pb:pb + 64, :, :], [("Zrun", par)], [("Zbd", par)])
                    yield (pO, pOk)

                def tail(i, q, d, pO, pOk):
                    tk = i * 128
                    sl = slice(tk, tk + 128)
                    if d == 0:
                        self.cp("act", HB[0][:, :, sl], pO[:, 0:256].rearrange("p (c t) -> p c t", c=2), [pOk], [("of_", i)])
                        yield
                    else:
                        yield from self.head_norm_gen(l, pO, pOk, of_[:, q], ob_[:, q], hst_[:, q], blkb, epsg, f"rw_gn_{l}", HB[2], 6, tk,
                                                      HB[3][:, :, sl], HB[0][:, :, sl], ("rw", q))

                def run_interleaved(gens):
                    gens = list(gens)
                    while gens:
                        for g in list(gens):
                            try:
                                next(g)
                            except StopIteration:
                                gens.remove(g)

                for d in range(2):
                    for win in SEQ_WINS:
                        (s, toff, n, hc, gt) = win
                        for c in range(2):
                            pa, pak = self.bank()
                            self.mm(pa[:, 0:n], lorab[64:128, 512 + d * 256 + c * 128:512 + d * 256 + (c + 1) * 128], alt[64:128, gt:gt + n],
                                    True, True, [], [pak])
                            self.act(HB[1][:, c, gt:gt + n], pa[:, 0:n], AF.Sigmoid, [pak], [("ad", c, gt)], bias=self.V(f"a0_{d}_{l}", c))
                    P.barrier()
                    self.tt("dve", ball, kk[:], HB[1], ALU.mult, [], ["ball"])
                    for c in range(2):
                        self.ts("dve", HB[1][:, c, :], HB[1][:, c, :], self.V(f"ka_{l}", c), sc[:, 4 + c:5 + c], ALU.mult, ALU.add, ["ball"], [("kd", c)])
                    self.tt("dve", HB[1], HB[1], dk[:], ALU.mult, [("kd", 0), ("kd", 1)], ["kd_all"])
                    P.barrier()
                    kd = HB[1]
                    tri = self.cstt[:, CST["tri_i_f"][0]:CST["tri_i_f"][0] + 256] if d == 0 else \
                        self.cstt[:, CST["tri_i_b"][0]:CST["tri_i_b"][0] + 256]
                    last = 127 if d == 0 else 0
                    for (s, i0, nch) in ((0, 0, 2), (1, 2, 2), (2, 4, 16)):
                        if s == 2:
                            for c in range(2):
                                self.dma("sp", zin[:, c, :], self.srwkv_in[l, d, 2 * c:2 * c + 2].rearrange("h v k -> (h v) k"), (), [("zin", c)])
                            for par in range(2):
                                pt, ptk = self.bank()
                                pb = par * 64
                                for c in range(2):
                                    self.tr(pt[0:64, c * 64:(c + 1) * 64], zin[pb:pb + 64, c, :], self.ident[pb:pb + 64, pb:pb + 64],
                                            [("zin", c)], [ptk])
                                self.cp("dve", Zrun[pb:pb + 64, :, :], pt[0:64, 0:128].rearrange("p (c v) -> p c v", c=2), [ptk], [("Zrun", par)])
                        else:
                            self.memset("pool", Zrun[:], 0.0, [("Zrun", 0), ("Zrun", 1)])
                        for par in range(2):
                            pb = par * 64
                            self.cp("act", Zbd[pb:pb + 64, :, pb:pb + 64], Zrun[pb:pb + 64, :, :], [("Zrun", par)], [("Zbd", par)])
                        order = list(range(i0, i0 + nch)) if d == 0 else list(range(i0 + nch - 1, i0 - 1, -1))
                        LOOK = 6
                        pend = None
                        for a in range(0, nch, 2):
                            pair = order[a:a + 2]
                            gens = pend if pend is not None else [early(i, q, d, kd, tri, last) for q, i in enumerate(pair)]
                            run_interleaved(gens)
                            nxt_pair = order[a + 2:a + 4]
                            pend = [early(i, q, d, kd, tri, last) for q, i in enumerate(nxt_pair)] if nxt_pair else None
                            ahead = [LOOK] * len(pend) if pend else []
                            pos = []
                            for q, i in enumerate(pair):
                                cg_ = chain(i, q, d)
                                while True:
                                    r_ = next(cg_)
                                    for gi_ in range(len(ahead)):
                                        if ahead[gi_] > 0:
                                            next(pend[gi_])
                                            ahead[gi_] -= 1
                                    if r_ is not None:
                                        pos.append(r_)
                                        break
                            for gi_ in range(len(ahead)):
                                while ahead[gi_] > 0:
                                    next(pend[gi_])
                                    ahead[gi_] -= 1
                            run_interleaved([tail(i, q, d, *pos[q]) for q, i in enumerate(pair)])
                        if s < 2:
                            for par in range(2):
                                pt, ptk = self.bank()
                                pb = par * 64
                                for c in range(2):
                                    self.tr(pt[0:64, c * 64:(c + 1) * 64], Zrun[pb:pb + 64, c, :], self.ident[pb:pb + 64, pb:pb + 64],
                                            [("Zrun", par)], [ptk])
                                for c in range(2):
                                    self.cp("dve", zo[:, 2 * c + par, :], pt[0:64, c * 64:(c + 1) * 64], [ptk], [("zo", 2 * c + par)])
                            self.out_toks.append(self.dma("sp", self.orwkv_out[s, l, d].rearrange("h v k -> v h k"), zo[:],
                                                          [("zo", h) for h in range(4)], ()))
                    P.barrier()
            for pc in (0, 257, 514, 2563):
                self.memset("pool", self.hbuf[:, :, pc:pc + 1], 0.0, [("hpad", pc)])
            P.barrier()

    B.mixer_rwkv = mixer_rwkv

    _old = B.head_norm_out

    def head_norm_out(self, l, po, pok, of, ob, hst, blkb, epsr, eps, gname, gate, mix0, tk, bonus, add=None):
        if add is None:
            return _old(self, l, po, pok, of, ob, hst, blkb, epsr, eps, gname, gate, mix0, tk, bonus)
        self.tt("dve", of[:], po[:, 0:256].rearrange("p (c t) -> p c t", c=2), add, ALU.add, [pok], ["of"])
        return _old(self, l, None, None, of, ob, hst, blkb, epsr, eps, gname, gate, mix0, tk, bonus)

    B.head_norm_out = head_norm_out


_install_rwkv()


_NC_CACHE = {}


def kernel(**inputs):
    inp = {k: np.asarray(v) for k, v in inputs.items()}
    if "nc" not in _NC_CACHE:
        _NC_CACHE["nc"] = Builder().build()
    nc = _NC_CACHE["nc"]
    shared = host_shared(inp)
    in_maps = []
    for core in range(8):
        m = dict(shared)
        m.update(host_inputs(inp, core))
        in_maps.append(m)
    res = run_bass_kernel_spmd(nc, in_maps, core_ids=list(range(8)))
    r = res.results
    y_prompt = np.concatenate([np.asarray(r[c]["y"])[0:512].reshape(2, 256, D) for c in range(8)], axis=0)
    y_sample = np.stack([np.asarray(r[2 * b]["y"])[512:2560] for b in range(4)], axis=0)
    nak = np.concatenate([np.asarray(r[c]["nak"]).reshape(2, DEPTH, 256, 4, 64) for c in range(8)], axis=0)
    nav = np.concatenate([np.asarray(r[c]["nav"]).reshape(2, DEPTH, 256, 4, 64) for c in range(8)], axis=0)
    oret = np.concatenate([np.asarray(r[c]["oret"]) for c in range(8)], axis=0)
    orwkv = np.concatenate([np.asarray(r[c]["orwkv"]) for c in range(8)], axis=0)
    return (y_prompt.astype(np.float32), y_sample.astype(np.float32), nak.astype(np.float32), nav.astype(np.float32),
            oret.astype(np.float32), orwkv.astype(np.float32))
```

```python
import numpy as np
from contextlib import ExitStack
import concourse.bass as bass
import concourse.mybir as mybir
from concourse.bass_utils import run_bass_kernel_spmd

F32 = mybir.dt.float32
BF16 = mybir.dt.bfloat16
ALU = mybir.AluOpType
AF = mybir.ActivationFunctionType

D = 1024
KC = 8
T = 2560
DFF = 2816
NJ = 22
DIN = 3328
DEPTH = 2
EPS = 1e-6
SEQS = [(0, 1, 256), (256, 258, 256), (512, 515, 2048)]
TP = 2564
import os as _os
SAME_ENG_SYNC = _os.environ.get('SES', '1') == '1'
DEBUG = False


def tok2col(t):
    for t0, c0, ln in SEQS:
        if t0 <= t < t0 + ln:
            return c0 + (t - t0)
    raise ValueError(t)


def vec_layout():
    lay = {}
    off = 0

    def add(name, n):
        nonlocal off
        lay[name] = (off, n)
        off += n

    for l in range(DEPTH):
        add(f"ada_b{l}", 48)
        for i in range(4):
            add(f"ng{i}_{l}", 8)
        for nm in ("cln_g", "cln_b", "ret_gn", "kk", "ka", "rk", "rw_gn", "w0_0", "w0_1", "a0_0", "a0_1"):
            add(f"{nm}_{l}", 2)
        for tp in range(3):
            add(f"shift{tp}_{l}", 6)
        for tp in range(3):
            add(f"fc{tp}_{l}", 44)
        for tp in range(31):
            add(f"dw{tp}_{l}", 2)
    return lay, off


VLAY, NV = vec_layout()


def pack_vecs(inp):
    v = np.zeros((128, NV), np.float32)

    def put(name, arr):
        off, n = VLAY[name]
        v[:, off:off + n] = np.asarray(arr, np.float32).reshape(n, 128).T

    for l in range(DEPTH):
        put(f"ada_b{l}", inp["ada_b"][l])
        for i in range(4):
            put(f"ng{i}_{l}", inp["norm_g"][l, i])
        put(f"cln_g_{l}", inp["conv_ln_g"][l])
        put(f"cln_b_{l}", inp["conv_ln_b"][l])
        put(f"ret_gn_{l}", inp["ret_gn"][l])
        put(f"kk_{l}", inp["rwkv_kk"][l])
        put(f"ka_{l}", inp["rwkv_ka"][l])
        put(f"rk_{l}", inp["rwkv_rk"][l])
        put(f"rw_gn_{l}", inp["rwkv_gn"][l])
        for d in range(2):
            put(f"w0_{d}_{l}", inp["rwkv_w0"][l, d])
            put(f"a0_{d}_{l}", inp["rwkv_a0"][l, d])
        for tp in range(3):
            put(f"shift{tp}_{l}", inp["rwkv_shift"][l, tp])
            put(f"fc{tp}_{l}", inp["ffn_conv"][l, tp])
        for tp in range(31):
            put(f"dw{tp}_{l}", inp["conv_dw"][l, tp])
    return v


class _Eng:
    def __init__(self, name, sem):
        self.name = name
        self.sem = sem
        self.count = 0
        self.waited = {}
        self.items = []


class Prog:
    def __init__(self, nc, n_dma_sems=(40, 30, 4)):
        self.nc = nc
        self.engs = {}
        for name in ("pe", "act", "dve", "pool", "sp"):
            self.engs[name] = _Eng(name, nc.alloc_semaphore("s_" + name))
        self.last_write = {}
        self.reads = {}
        self.dma_pools = {}
        for q, n in zip(("sp", "pool", "act"), n_dma_sems):
            self.dma_pools[q] = dict(sems=[nc.alloc_semaphore(f"d_{q}{i}") for i in range(n)],
                                     tot=[0] * n, nxt=0)
        self.n_ops = 0
        self.marks = []

    def _need(self, e, tok):
        if tok is None:
            return
        sem, val, src = tok
        if src == e.name and (e.name == "pe" or not SAME_ENG_SYNC):
            return
        k = id(sem)
        if e.waited.get(k, 0) >= val:
            return
        e.waited[k] = val
        e.items.append(("wait", sem, val))

    def _deps(self, e, reads, writes):
        for k in reads:
            self._need(e, self.last_write.get(k))
        for k in writes:
            self._need(e, self.last_write.get(k))
            for tok in self.reads.get(k, {}).values():
                self._need(e, tok)

    def _commit(self, tok, reads, writes):
        for k in writes:
            self.last_write[k] = tok
            self.reads[k] = {}
        for k in reads:
            self.reads.setdefault(k, {})[tok[2]] = tok

    def op(self, eng, fn, reads=(), writes=()):
        psk = [k for k in reads if isinstance(k, tuple) and k and k[0] == "ps"]
        if psk:
            writes = list(writes) + [k for k in psk if k not in writes]
            reads = [k for k in reads if k not in psk]
        e = self.engs[eng]
        self._deps(e, reads, writes)
        e.count += 1
        e.items.append(("op", fn, e.sem))
        self._commit((e.sem, e.count, eng), reads, writes)
        self.n_ops += 1

    def dma(self, q, fn, reads=(), writes=()):
        e = self.engs[q]
        self._deps(e, reads, writes)
        p = self.dma_pools[q]
        i = p["nxt"]
        p["nxt"] = (i + 1) % len(p["sems"])
        sem = p["sems"][i]
        if p["tot"][i] > 0:
            self._need(e, (sem, p["tot"][i], "dma"))
        p["tot"][i] += 16
        e.items.append(("dma", fn, sem))
        tok = (sem, p["tot"][i], "dma:" + q + str(i))
        self._commit(tok, reads, writes)
        self.n_ops += 1
        return tok

    def wait_all(self, eng, toks):
        e = self.engs[eng]
        for t in toks:
            self._need(e, t)

    def barrier(self):
        self.marks.append((self.engs["pe"].count, self.engs["act"].count, self.engs["dve"].count, self.engs["pool"].count))
        toks = [(e.sem, e.count, name) for name, e in self.engs.items() if e.count > 0]
        for q, p in self.dma_pools.items():
            for s, t in zip(p["sems"], p["tot"]):
                if t > 0:
                    toks.append((s, t, "dma"))
        for name, e in self.engs.items():
            for t in toks:
                self._need(e, t)
        self.last_write = {}
        self.reads = {}

    def emit(self):
        nc = self.nc
        handles = {"pe": "tensor", "act": "scalar", "dve": "vector", "pool": "gpsimd", "sp": "sync"}
        with nc.Block() as block:
            for name, attr in handles.items():
                e = self.engs[name]

                def body(h, e=e):
                    for it in e.items:
                        if it[0] == "wait":
                            h.wait_ge(it[1], it[2])
                        elif it[0] == "op":
                            it[1](h).then_inc(it[2], 1)
                        else:
                            it[1](h).then_inc(it[2], 16)

                getattr(block, attr)(body)


class Builder:
    def __init__(self, dbg_names=()):
        self.nc = bass.Bass("TRN2", target_bir_lowering=False)
        self.P = Prog(self.nc)
        self.dbg_names = set(dbg_names)
        self.dbg_out = {}
        self.out_toks = []
        self._bank = 0

    def mm(self, out, lhsT, rhs, start, stop, reads, writes):
        self.P.op("pe", lambda h: h.matmul(out, lhsT=lhsT, rhs=rhs, start=start, stop=stop), reads, writes)

    def tr(self, out, in_, ident, reads, writes):
        self.P.op("pe", lambda h: h.transpose(out=out, in_=in_, identity=ident), reads, writes)

    def act(self, out, in_, func, reads, writes, scale=1.0, bias=None, accum=None):
        def f(h):
            kw = {}
            if bias is not None:
                kw["bias"] = bias
            if accum is not None:
                kw["accum_out"] = accum
            return h.activation(out=out, in_=in_, func=func, scale=scale, **kw)
        self.P.op("act", f, reads, writes)

    def ts(self, eng, out, in0, s1, s2, op0, op1, reads, writes):
        if s2 is None:
            self.P.op(eng, lambda h: h.tensor_scalar(out=out, in0=in0, scalar1=s1, scalar2=None, op0=op0), reads, writes)
        else:
            self.P.op(eng, lambda h: h.tensor_scalar(out=out, in0=in0, scalar1=s1, scalar2=s2, op0=op0, op1=op1), reads, writes)

    def tt(self, eng, out, in0, in1, op, reads, writes):
        self.P.op(eng, lambda h: h.tensor_tensor(out=out, in0=in0, in1=in1, op=op), reads, writes)

    def stt(self, eng, out, in0, scalar, in1, op0, op1, reads, writes):
        self.P.op(eng, lambda h: h.scalar_tensor_tensor(out=out, in0=in0, scalar=scalar, in1=in1, op0=op0, op1=op1), reads, writes)

    def cp(self, eng, out, in_, reads, writes):
        if eng == "act":
            self.act(out, in_, AF.Copy, reads, writes)
        else:
            self.P.op(eng, lambda h: h.tensor_copy(out=out, in_=in_), reads, writes)

    def memset(self, eng, ap, val, writes):
        self.P.op(eng, lambda h: h.memset(ap, val), (), writes)

    def recip(self, out, in_, reads, writes):
        self.P.op("dve", lambda h: h.reciprocal(out=out, in_=in_), reads, writes)

    def dma(self, q, out, in_, reads, writes):
        return self.P.dma(q, lambda h: h.dma_start(out=out, in_=in_), reads, writes)

    def sb(self, name, shape, dt):
        self._uid = getattr(self, "_uid", 0) + 1
        return self.nc.sbuf_tensor(f"{name}_{self._uid}", shape, dt)

    def bank(self):
        res = getattr(self, "_reserved", ())
        i = self._bank
        while i in res:
            i = (i + 1) % 8
        self._bank = (i + 1) % 8
        return self.ps[i], ("ps", i)

    def dbg(self, name, ap, shape, key):
        if name not in self.dbg_names:
            return
        dt = ap.dtype
        o = self.nc.dram_tensor("dbg_" + name, list(shape), dt, kind="ExternalOutput").ap()
        self.dbg_out[name] = o
        self.out_toks.append(self.dma("sp", o, ap, reads=key, writes=()))

    def build(self, skip_mixers=False, phases=None):
        nc, P = self.nc, self.P
        self.x_in = nc.dram_tensor("x_in", [T, D], F32, kind="ExternalInput").ap()
        self.cv_in = nc.dram_tensor("cv", [128, KC, 2], F32, kind="ExternalInput").ap()
        self.vec_in = nc.dram_tensor("vec", [128, NV], F32, kind="ExternalInput").ap()
        self.cst_in = nc.dram_tensor("cst", [128, NCST], F32, kind="ExternalInput").ap()
        self.rope_in = nc.dram_tensor("rope", [128, 2, 2048], F32, kind="ExternalInput").ap()
        self.ada_w = nc.dram_tensor("ada_w", [DEPTH, D, 6 * D], F32, kind="ExternalInput").ap()
        self.w_in = nc.dram_tensor("w_in", [DEPTH, D, DIN], F32, kind="ExternalInput").ap()
        self.w_out = nc.dram_tensor("w_out", [DEPTH, D, D], F32, kind="ExternalInput").ap()
        self.ffn_up = nc.dram_tensor("ffn_up", [DEPTH, D, 2 * DFF], F32, kind="ExternalInput").ap()
        self.ffn_down = nc.dram_tensor("ffn_down", [DEPTH, DFF, D], F32, kind="ExternalInput").ap()
        self.y_out = nc.dram_tensor("y", [T, D], F32, kind="ExternalOutput").ap()
        self.xs = nc.dram_tensor("xs", [128, KC, T], F32).ap()
        if skip_mixers:
            self.mix_in = nc.dram_tensor("mix_in", [DEPTH, 128, KC, T], F32, kind="ExternalInput").ap()
        self.cnk_in = nc.dram_tensor("cnk", [DEPTH, 256, 256], F32, kind="ExternalInput").ap()
        self.cnv_in = nc.dram_tensor("cnv", [DEPTH, 256, 256], F32, kind="ExternalInput").ap()
        self.nab_in = nc.dram_tensor("nab", [DEPTH, 5, 128, 2560], F32, kind="ExternalInput").ap()
        self.sret_in = nc.dram_tensor("sret", [DEPTH, 2, 4, 64, 64], F32, kind="ExternalInput").ap()
        self.srwkv_in = nc.dram_tensor("srwkv", [DEPTH, 2, 4, 64, 64], F32, kind="ExternalInput").ap()
        self.retd_in = nc.dram_tensor("retd", [1, DEPTH * 8], F32, kind="ExternalInput").ap()
        self.lora_in = nc.dram_tensor("lora", [DEPTH, 128, 1280], F32, kind="ExternalInput").ap()
        self.w0row_in = nc.dram_tensor("w0row", [1, DEPTH * 2 * 256], F32, kind="ExternalInput").ap()
        self.bmask_in = nc.dram_tensor("bmask", [128, 640], F32, kind="ExternalInput").ap()
        self.nak_out = nc.dram_tensor("nak", [2, DEPTH, 256, 256], F32, kind="ExternalOutput").ap()
        self.nav_out = nc.dram_tensor("nav", [2, DEPTH, 256, 256], F32, kind="ExternalOutput").ap()
        self.oret_out = nc.dram_tensor("oret", [2, DEPTH, 2, 4, 64, 64], F32, kind="ExternalOutput").ap()
        self.orwkv_out = nc.dram_tensor("orwkv", [2, DEPTH, 2, 4, 64, 64], F32, kind="ExternalOutput").ap()

        self.ps = [nc.alloc_psum_tensor(f"ps{i}", [128, 512], F32) for i in range(8)]

        self.ident = nc.alloc_sbuf_tensor("ident", [128, 128], F32)
        self.identb = nc.alloc_sbuf_tensor("identb", [128, 128], BF16)
        self.onesb = nc.alloc_sbuf_tensor("onesb", [128, 128], BF16)
        self.epsD = nc.alloc_sbuf_tensor("epsD", [128, 1], F32)
        self.vec = nc.alloc_sbuf_tensor("vecs", [128, NV], F32)
        self.modv = nc.alloc_sbuf_tensor("modv", [128, DEPTH, 48, 2], F32)
        self.msc = nc.alloc_sbuf_tensor("msc", [128, DEPTH, 4, KC, 2], F32)
        self.hbuf = nc.alloc_sbuf_tensor("hbuf", [128, KC, TP], BF16)
        self.cvt = nc.alloc_sbuf_tensor("cvt", [128, KC, 2], F32)
        self._bg = []

        self.cstt = nc.alloc_sbuf_tensor("cstt", [128, NCST], F32)
        self.dma("sp", self.cstt[:], self.cst_in, (), ["cstt"])
        self.cp("act", self.ident[:], self.cstt[:, 0:128], ["cstt"], ["ident"])
        self.dma("sp", self.vec[:], self.vec_in, (), ["vec"])
        self.cp("dve", self.identb[:], self.ident[:], ["ident"], ["identb"])
        self.memset("pool", self.onesb[:], 1.0, ["onesb"])
        self.memset("pool", self.epsD[:], D * EPS, ["epsD"])
        self.memset("pool", self.hbuf[:], 0.0, ["hbuf_all"])
        P.barrier()

        self.phase_load_x()
        if phases is None or "mod" in phases:
            self.phase_mod()
        for l in range(DEPTH if (phases is None or "layers" in phases) else 0):
            self.phase_norm1(l)
            P.barrier()
            if phases is not None and "norm1only" in phases:
                continue
            with self.sb("mixed", [128, KC, T], BF16) as mixed:
                self.mixed = mixed
                if skip_mixers:
                    self.phase_fake_mixers(l)
                else:
                    self.phase_mixers(l)
                P.barrier()
                self.dbg(f"mixed{l}", self.mixed[:], [128, KC, T], [])
                if phases is not None and "mixonly" in phases:
                    P.barrier()
                    break
                if phases is None or "wout" in phases:
                    self.phase_wout(l)
                P.barrier()
            if phases is None or "ffn" in phases:
                self.phase_ffn(l)
            P.barrier()
        self.phase_store_y()
        P.wait_all("sp", self.out_toks)
        P.emit()
        return nc

    def V(self, name, j=None):
        off, n = VLAY[name]
        if j is None:
            return self.vec[:, off:off + n]
        return self.vec[:, off + j:off + j + 1]

    def phase_load_x(self):
        nc, P = self.nc, self.P
        with self.sb("xrow", [128, 4, D], F32) as xrow, self.sb("xT", [128, 2, KC, 512], F32) as xT, \
                self.sb("adab", [128, 2, KC, 512], F32) as adab:
            cvt = self.cvt
            def gen_x():
                for r in range(T // 128):
                    b = r % 4
                    t4, r4 = (r // 4) % 2, r % 4
                    self.dma("sp", xrow[:, b, :], self.x_in[r * 128:(r + 1) * 128, :], (), [("xrow", b)])
                    for half in range(2):
                        pst, pk = self.bank()
                        for i in range(4):
                            k = half * 4 + i
                            self.tr(pst[:, i * 128:(i + 1) * 128], xrow[:, b, k * 128:(k + 1) * 128], self.ident[:],
                                    [("xrow", b), "ident"], [pk])
                        eng = "act" if half == 0 else "dve"
                        self.cp(eng, xT[:, t4, half * 4:(half + 1) * 4, r4 * 128:(r4 + 1) * 128],
                                pst[:].rearrange("p (i n) -> p i n", i=4), [pk], [("xT", t4, r4, half)])
                    if r4 == 3:
                        tt = r // 4
                        self.dma("sp", self.xs[:, :, tt * 512:(tt + 1) * 512], xT[:, t4],
                                 [("xT", t4, q_, h_) for q_ in range(4) for h_ in range(2)], [("xs", tt)])
                    yield

            gx, gm = gen_x(), self.gen_mod(cvt, adab, [0])
            alive = [gx, gm]
            while alive:
                for g in list(alive):
                    try:
                        next(g)
                    except StopIteration:
                        alive.remove(g)
            P.barrier()

    def phase_mod(self):
        pass

    def bg_step(self):
        for g in list(self._bg):
            try:
                next(g)
            except StopIteration:
                self._bg.remove(g)

    def gen_mod(self, cvt, adab, layers):
        nc, P = self.nc, self.P
        if 0 in layers:
            self.dma("sp", cvt[:], self.cv_in, (), ["cvt"])
            self.act(cvt[:], cvt[:], AF.Silu, ["cvt"], ["cvt"])
        n = 0
        for l in layers:
            wv = self.ada_w[l].rearrange("(k p) n -> p k n", p=128)
            pst, pk = self.ps[7], ("ps", 7)
            self._reserved = {7}
            for nb in range(12):
                b = n % 2
                n += 1
                self.dma("sp", adab[:, b], wv[:, :, nb * 512:(nb + 1) * 512], (), [("adab", b)])
                for m in range(4):
                    cc = nb * 4 + m
                    for k in range(KC):
                        self.mm(pst[:, cc * 2:cc * 2 + 2], adab[:, b, k, m * 128:(m + 1) * 128], cvt[:, k, :],
                                k == 0, k == KC - 1, [("adab", b), "cvt"], [pk])
                yield
            off, _ = VLAY[f"ada_b{l}"]
            self.tt("dve", self.modv[:, l], pst[:, 0:96].rearrange("p (c g) -> p c g", g=2),
                    self.vec[:, off:off + 48].unsqueeze(2).to_broadcast([128, 48, 2]), ALU.add,
                    [pk, "vec"], [("modv", l)])
            self._reserved = set()
            for which, (src, ng) in enumerate(((8, 0), (16, 1), (32, 2), (40, 3))):
                o = self.msc[:, l, which]
                g = self.V(f"ng{ng}_{l}").unsqueeze(2).to_broadcast([128, KC, 2])
                m_ = self.modv[:, l, src:src + 8, :]
                if which in (0, 2):
                    self.ts("dve", o, m_, 1.0, 32.0, ALU.add, ALU.mult, [("modv", l)], [("msc", l, which)])
                else:
                    self.ts("dve", o, m_, 32.0, None, ALU.mult, None, [("modv", l)], [("msc", l, which)])
                self.tt("dve", o, o, g, ALU.mult, [("msc", l, which), "vec"], [("msc", l, which)])
            yield

    def norm_mod_tile(self, l, xt, xkey, sq, sqkeys, rs, rskey, pieces, grp, which_scale, shift_chunk0, xn=None, xnkey=None):
        n = xt.shape[2]
        if xn is None:
            xn, xnkey = xt, xkey
        self.act(sq, xt, AF.Square, [xkey], list(sqkeys))
        pst, pk = self.bank()
        for k in range(KC):
            self.mm(pst[:, 0:n], self.onesb[:], sq[:, k, :], k == 0, k == KC - 1, [sqkeys[k], "onesb"], [pk])
        self.act(rs, pst[:, 0:n], AF.Sqrt, [pk, "epsD"], [rskey], bias=self.epsD[:, 0:1])
        self.recip(rs, rs, [rskey], [rskey])
        xnkeys = xnkey if isinstance(xnkey, list) else [xnkey]
        self.tt("dve", xn, xt, rs.unsqueeze(1).to_broadcast([128, KC, n]), ALU.mult, [xkey, rskey], xnkeys)
        for k in range(KC):
            sc = self.msc[:, l, which_scale, k, grp:grp + 1]
            sh = self.modv[:, l, shift_chunk0 + k, grp:grp + 1]
            xk = xnkeys[k] if len(xnkeys) > 1 else xnkeys[0]
            for (c0, ln, hc) in pieces:
                eng = ("act", "pool")[k % 2]
                o = self.hbuf[:, k, hc:hc + ln]
                i = xn[:, k, c0:c0 + ln]
                rd = [xk, ("msc", l, which_scale), ("modv", l)]
                if eng == "act":
                    self.act(o, i, AF.Identity, rd, [("h", k, hc)], scale=sc, bias=sh)
                else:
                    self.ts(eng, o, i, sc, sh, ALU.mult, ALU.add, rd, [("h", k, hc)])

    @staticmethod
    def tile_pieces(tt):
        if tt == 0:
            return 0, [(0, 256, 1), (256, 256, 258)]
        return 1, [(0, 512, 515 + (tt - 1) * 512)]

    def phase_norm1(self, l):
        nc, P = self.nc, self.P
        with self.sb("n1x", [128, 2, KC, 512], F32) as xt, self.sb("n1sq", [128, 2, KC, 512], BF16) as sq, \
                self.sb("n1rs", [128, 2, 512], F32) as rs:
            def tile(tt, b):
                self.dma("sp", xt[:, b], self.xs[:, :, tt * 512:(tt + 1) * 512], (), [("n1x", b)])
                grp, pieces = self.tile_pieces(tt)
                yield
                yield from self.norm_mod_gen(l, xt[:, b], ("n1x", b), sq[:, b], [("n1sq", b, k) for k in range(KC)], rs[:, b],
                                             ("n1rs", b), pieces, grp, 0, 0, xt[:, b], [("n1x", b)] * KC)

            for a in range(0, 5, 2):
                gens = [tile(tt, u) for u, tt in enumerate(range(a, min(a + 2, 5)))]
                while gens:
                    for g in list(gens):
                        try:
                            next(g)
                        except StopIteration:
                            gens.remove(g)
            P.barrier()

    def phase_fake_mixers(self, l):
        nc = self.nc
        with self.sb("mixf", [128, KC, 512], F32) as mf:
            for tt in range(5):
                self.dma("sp", mf[:], self.mix_in[l, :, :, tt * 512:(tt + 1) * 512], (), ["mixf"])
                self.cp("dve", self.mixed[:, :, tt * 512:(tt + 1) * 512], mf[:], ["mixf"], [("mixed", tt)])
            self.P.barrier()

    def phase_mixers(self, l):
        raise NotImplementedError

    def phase_wout(self, l):
        nc, P = self.nc, self.P
        with self.sb("wo", [128, KC, D], BF16) as wo, self.sb("wx", [128, 2, KC, 512], F32) as xt, \
                self.sb("wm", [128, 2, KC, 512], F32) as mt_, self.sb("wsq", [128, 2, KC, 512], BF16) as sq_, \
                self.sb("wrs", [128, 2, 512], F32) as rs_:
            self.dma("pool", wo[:], self.w_out[l].rearrange("(k p) n -> p k n", p=128), (), ["wo"])

            def tile(tt, u):
                b = u
                mt, sq, rs = mt_[:, u], sq_[:, u], rs_[:, u]
                grp, pieces = self.tile_pieces(tt)
                mk = [("wm", u, oc) for oc in range(KC)]
                sk = [("wsq", u, oc) for oc in range(KC)]
                self.dma("sp", xt[:, b], self.xs[:, :, tt * 512:(tt + 1) * 512], (), [("wx", b)])
                for oc in range(KC):
                    pst, pk = self.bank()
                    for k in range(KC):
                        self.mm(pst[:], wo[:, k, oc * 128:(oc + 1) * 128], self.mixed[:, k, tt * 512:(tt + 1) * 512],
                                k == 0, k == KC - 1, ["wo", ("mixed", tt)], [pk])
                    self.act(sq[:, oc, :], pst[:], AF.Square, [pk], [sk[oc]])
                    self.cp("dve", mt[:, oc, :], pst[:], [pk], [mk[oc]])
                    if oc % 2 == 1:
                        yield
                yield from self.post_norm_gen(l, mt, mk, sq, sk, rs, ("wrs", u), xt[:, b], ("wx", b), grp, 1)
                self.dma("sp", self.xs[:, :, tt * 512:(tt + 1) * 512], xt[:, b], [("wx", b)], [("xs2", tt)])
                yield
                yield from self.norm_mod_gen(l, xt[:, b], ("wx", b), sq, sk, rs, ("wrs", u), pieces, grp, 2, 24, mt, mk)

            for a in range(0, 5, 2):
                gens = [tile(tt, u) for u, tt in enumerate(range(a, min(a + 2, 5)))]
                while gens:
                    for g in list(gens):
                        try:
                            next(g)
                        except StopIteration:
                            gens.remove(g)
            P.barrier()

    def post_norm_gen(self, l, mt, mkeys, sq, sqkeys, rs, rskey, xt, xkey, grp, which_gate):
        n = mt.shape[2]
        pst, pk = self.bank()
        for k in range(KC):
            self.mm(pst[:, 0:n], self.onesb[:], sq[:, k, :], k == 0, k == KC - 1, [sqkeys[k], "onesb"], [pk])
        yield
        self.act(rs, pst[:, 0:n], AF.Sqrt, [pk, "epsD"], [rskey], bias=self.epsD[:, 0:1])
        yield
        self.recip(rs, rs, [rskey], [rskey])
        yield
        self.tt("dve", mt, mt, rs.unsqueeze(1).to_broadcast([128, KC, n]), ALU.mult, list(mkeys) + [rskey], list(mkeys))
        yield
        for k in range(KC):
            gg = self.msc[:, l, which_gate, k, grp:grp + 1]
            self.stt("dve", xt[:, k, :], mt[:, k, :], gg, xt[:, k, :], ALU.mult, ALU.add,
                     [mkeys[k], xkey, ("msc", l, which_gate)], [xkey])
            if k % 4 == 3:
                yield

    def norm_mod_gen(self, l, xt, xkey, sq, sqkeys, rs, rskey, pieces, grp, which_scale, shift_chunk0, xn, xnkeys):
        n = xt.shape[2]
        self.act(sq, xt, AF.Square, [xkey], list(sqkeys))
        yield
        pst, pk = self.bank()
        for k in range(KC):
            self.mm(pst[:, 0:n], self.onesb[:], sq[:, k, :], k == 0, k == KC - 1, [sqkeys[k], "onesb"], [pk])
        yield
        self.act(rs, pst[:, 0:n], AF.Sqrt, [pk, "epsD"], [rskey], bias=self.epsD[:, 0:1])
        yield
        self.recip(rs, rs, [rskey], [rskey])
        yield
        self.tt("dve", xn, xt, rs.unsqueeze(1).to_broadcast([128, KC, n]), ALU.mult, [xkey, rskey], list(xnkeys))
        yield
        for k in range(KC):
            sc = self.msc[:, l, which_scale, k, grp:grp + 1]
            sh = self.modv[:, l, shift_chunk0 + k, grp:grp + 1]
            for (c0, ln, hc) in pieces:
                eng = ("act", "pool")[k % 2]
                o = self.hbuf[:, k, hc:hc + ln]
                i = xn[:, k, c0:c0 + ln]
                rd = [xnkeys[k], ("msc", l, which_scale), ("modv", l)]
                if eng == "act":
                    self.act(o, i, AF.Identity, rd, [("h", k, hc)], scale=sc, bias=sh)
                else:
                    self.ts(eng, o, i, sc, sh, ALU.mult, ALU.add, rd, [("h", k, hc)])
            if k % 2 == 1:
                yield

    def post_norm_residual(self, l, mt, mkeys, sq, sqkeys, rs, rskey, xt, xkey, grp, which_gate):
        n = mt.shape[2]
        pst, pk = self.bank()
        for k in range(KC):
            self.mm(pst[:, 0:n], self.onesb[:], sq[:, k, :], k == 0, k == KC - 1, [sqkeys[k], "onesb"], [pk])
        self.act(rs, pst[:, 0:n], AF.Sqrt, [pk, "epsD"], [rskey], bias=self.epsD[:, 0:1])
        self.recip(rs, rs, [rskey], [rskey])
        self.tt("dve", mt, mt, rs.unsqueeze(1).to_broadcast([128, KC, n]), ALU.mult, list(mkeys) + [rskey], list(mkeys))
        for k in range(KC):
            gg = self.msc[:, l, which_gate, k, grp:grp + 1]
            self.stt("dve", xt[:, k, :], mt[:, k, :], gg, xt[:, k, :], ALU.mult, ALU.add,
                     [mkeys[k], xkey, ("msc", l, which_gate)], [xkey])

    def ffn_windows(self):
        w = [(0, 258, 0, 256), (257, 258, 256, 256)]
        sizes = [410, 410, 410, 410, 408]
        t = 512
        for s in sizes:
            w.append((tok2col(t) - 1, s + 2, t, s))
            t += s
        return w

    def phase_ffn(self, l):
        nc, P = self.nc, self.P
        wins = self.ffn_windows()
        passes = [wins[0:3], wins[3:5], wins[5:7]]
        upv = self.ffn_up[l].rearrange("(k p) (two j c) -> p k two j c", p=128, two=2, j=NJ)
        dnv = self.ffn_down[l].rearrange("(j p) n -> p j n", p=128)
        MAXT = 1024

        def run_interleaved(gens):
            gens = list(gens)
            while gens:
                for g in list(gens):
                    try:
                        next(g)
                    except StopIteration:
                        gens.remove(g)

        with self.sb("fact", [128, NJ, MAXT], BF16) as fact:
            for pw in passes:
                p_tok0 = pw[0][2]
                p_ntok = sum(w[3] for w in pw)
                with self.sb("wu", [128, 2, KC, 2, 512], BF16) as wu, self.sb("cg", [128, 2, 512], F32) as cg, \
                        self.sb("vc", [128, 2, 3, 512], F32) as vc, self.sb("cvv", [128, 2, 512], F32) as cvv:

                    upw = self.ffn_up[l].rearrange("(k p) (two n) -> p k two n", p=128, two=2)
                    groups = [(j0, min(4, NJ - j0)) for j0 in range(0, NJ, 4)]

                    def load_wu(gi):
                        j0, gs = groups[gi]
                        wb = gi % 2
                        for half in range(2):
                            self.dma("pool", wu[:, wb, :, half, 0:gs * 128], upw[:, :, half, j0 * 128:(j0 + gs) * 128], (), [("wu", wb, half)])

                    def unit(j, wb, win, cb):
                        (c0, ncol, tok0, nout) = win
                        jo = (j % 4) * 128
                        fcg = [self.V(f"fc{tp}_{l}", j) for tp in range(3)]
                        fcv = [self.V(f"fc{tp}_{l}", NJ + j) for tp in range(3)]
                        pg, pgk = self.bank()
                        pv, pvk = self.bank()
                        for k in range(KC):
                            self.mm(pg[:, 0:ncol], wu[:, wb, k, 0, jo:jo + 128], self.hbuf[:, k, c0:c0 + ncol], k == 0, k == KC - 1, [("wu", wb, 0)], [pgk])
                        for k in range(KC):
                            self.mm(pv[:, 0:ncol], wu[:, wb, k, 1, jo:jo + 128], self.hbuf[:, k, c0:c0 + ncol], k == 0, k == KC - 1, [("wu", wb, 1)], [pvk])
                        yield
                        g_o = cg[:, cb, 0:nout]
                        v_o = cvv[:, cb, 0:nout]
                        self.ts("dve", g_o, pg[:, 1:1 + nout], fcg[1], None, ALU.mult, None, [pgk], [("cg", cb)])
                        self.act(vc[:, cb, 0, 0:nout], pv[:, 0:nout], AF.Identity, [pvk], [("vc", cb, 0)], scale=fcv[0])
                        yield
                        self.stt("dve", g_o, pg[:, 0:nout], fcg[0], g_o, ALU.mult, ALU.add, [pgk, ("cg", cb)], [("cg", cb)])
                        self.act(vc[:, cb, 1, 0:nout], pv[:, 1:1 + nout], AF.Identity, [pvk], [("vc", cb, 1)], scale=fcv[1])
                        yield
                        self.stt("dve", g_o, pg[:, 2:2 + nout], fcg[2], g_o, ALU.mult, ALU.add, [pgk, ("cg", cb)], [("cg", cb)])
                        self.act(vc[:, cb, 2, 0:nout], pv[:, 2:2 + nout], AF.Identity, [pvk], [("vc", cb, 2)], scale=fcv[2])
                        yield
                        self.act(g_o, g_o, AF.Silu, [("cg", cb)], [("cg", cb)])
                        self.tt("pool", v_o, vc[:, cb, 0, 0:nout], vc[:, cb, 1, 0:nout], ALU.add, [("vc", cb, 0), ("vc", cb, 1)], [("cvv", cb)])
                        yield
                        self.tt("pool", v_o, v_o, vc[:, cb, 2, 0:nout], ALU.add, [("vc", cb, 2), ("cvv", cb)], [("cvv", cb)])
                        yield
                        a0 = tok0 - p_tok0
                        self.tt("dve", fact[:, j, a0:a0 + nout], g_o, v_o, ALU.mult, [("cg", cb), ("cvv", cb)], [("fact", j)])

                    load_wu(0)
                    for j in range(NJ):
                        gi = j // 4
                        if j % 4 == 0 and gi + 1 < len(groups):
                            load_wu(gi + 1)
                        wl = list(pw)
                        for a in range(0, len(wl), 2):
                            run_interleaved([unit(j, gi % 2, w_, ci) for ci, w_ in enumerate(wl[a:a + 2])])
                    P.barrier()
                tiles = []
                t = 0
                while t < p_ntok:
                    n = min(512, p_ntok - t)
                    tiles.append((t, n))
                    t += n
                nt = len(tiles)
                with self.sb("wd", [128, 2, NJ, 128], BF16) as wd, self.sb("ff", [128, 2, KC, 512], F32) as ff, \
                        self.sb("fsq", [128, 2, KC, 512], BF16) as fsq, self.sb("frs", [128, 2, 512], F32) as frs, \
                        self.sb("fx", [128, 2, KC, 512], F32) as fx:
                    for ti, (t0, n) in enumerate(tiles):
                        g_tok0 = p_tok0 + t0
                        assert (g_tok0 + n <= 512) or g_tok0 >= 512
                        self.dma("sp", fx[:, ti, :, 0:n], self.xs[:, :, g_tok0:g_tok0 + n], (), [("fx", ti)])
                    self.dma("pool", wd[:, 0], dnv[:, :, 0:128], (), [("wd", 0)])
                    for oc in range(KC):
                        db = oc % 2
                        if oc + 1 < KC:
                            self.dma("pool", wd[:, 1 - db], dnv[:, :, (oc + 1) * 128:(oc + 2) * 128], (), [("wd", 1 - db)])
                        for ti, (t0, n) in enumerate(tiles):
                            pst, pk = self.bank()
                            for j in range(NJ):
                                self.mm(pst[:, 0:n], wd[:, db, j, :], fact[:, j, t0:t0 + n], j == 0, j == NJ - 1,
                                        [("wd", db), ("fact", j)], [pk])
                            self.act(fsq[:, ti, oc, 0:n], pst[:, 0:n], AF.Square, [pk], [("fsq", ti, oc)])
                            self.cp("dve", ff[:, ti, oc, 0:n], pst[:, 0:n], [pk], [("ff", ti, oc)])
                    for ti, (t0, n) in enumerate(tiles):
                        g_tok0 = p_tok0 + t0
                        grp = 0 if g_tok0 < 512 else 1
                        self.post_norm_residual(l, ff[:, ti, :, 0:n], [("ff", ti, oc) for oc in range(KC)], fsq[:, ti, :, 0:n],
                                                [("fsq", ti, oc) for oc in range(KC)], frs[:, ti, 0:n], ("frs", ti), fx[:, ti, :, 0:n],
                                                ("fx", ti), grp, 3)
                        self.dma("sp", self.xs[:, :, g_tok0:g_tok0 + n], fx[:, ti, :, 0:n], [("fx", ti)], [("xs3", g_tok0)])
                    P.barrier()
            P.barrier()

    def phase_store_y(self):
        nc, P = self.nc, self.P
        with self.sb("yx", [128, 2, KC, 512], F32) as yx, self.sb("yrow", [128, 2, D], F32) as yrow:
            for tt in range(5):
                b = tt % 2
                self.dma("sp", yx[:, b], self.xs[:, :, tt * 512:(tt + 1) * 512], (), [("yx", b)])
                for r4 in range(4):
                    rb = (tt * 4 + r4) % 2
                    for half in range(2):
                        pst, pk = self.bank()
                        for i in range(4):
                            k = half * 4 + i
                            self.tr(pst[:, i * 128:(i + 1) * 128], yx[:, b, k, r4 * 128:(r4 + 1) * 128], self.ident[:],
                                    [("yx", b), "ident"], [pk])
                        eng = "act" if half == 0 else "dve"
                        self.cp(eng, yrow[:, rb, half * 512:(half + 1) * 512], pst[:], [pk], [("yrow", rb, half)])
                    r = tt * 4 + r4
                    self.out_toks.append(self.dma("sp", self.y_out[r * 128:(r + 1) * 128, :], yrow[:, rb, :],
                                                  [("yrow", rb, 0), ("yrow", rb, 1)], ()))


def host_inputs(inp, core):
    b = core // 2
    x = np.concatenate([inp["x_prompt"][2 * core].reshape(256, D), inp["x_prompt"][2 * core + 1].reshape(256, D),
                        inp["x_sample"][b].reshape(2048, D)], axis=0)
    cvec = np.stack([inp["c_ctx"], inp["c"][b]], axis=-1)
    cv = cvec.reshape(KC, 128, 2).transpose(1, 0, 2)
    return {"x_in": np.ascontiguousarray(x, np.float32), "cv": np.ascontiguousarray(cv, np.float32),
            "cnk": np.ascontiguousarray(inp["cache_na_k"][b].reshape(DEPTH, 256, 256)),
            "cnv": np.ascontiguousarray(inp["cache_na_v"][b].reshape(DEPTH, 256, 256)),
            "sret": np.ascontiguousarray(inp["state_retention"][b]),
            "srwkv": np.ascontiguousarray(inp["state_rwkv"][b])}


def host_shared(inp):
    lora = np.zeros((DEPTH, 128, 1280), np.float32)
    for l in range(DEPTH):
        for d in range(2):
            lora[l, 0:64, d * 256:(d + 1) * 256] = inp["rwkv_w2"][l, d]
            lora[l, 64:128, 512 + d * 256:512 + (d + 1) * 256] = inp["rwkv_a2"][l, d]
        lora[l, :, 1024:1280] = inp["rwkv_g2"][l]
    return {"vec": pack_vecs(inp), "cst": make_cst(), "rope": make_rope(), "ada_w": inp["ada_w"], "w_in": inp["w_in"],
            "w_out": inp["w_out"], "ffn_up": inp["ffn_up"], "ffn_down": inp["ffn_down"],
            "nab": na_bias_tiles(np.asarray(inp["na_rpb"], np.float32)).reshape(DEPTH, 5, 128, 2560),
            "retd": np.ascontiguousarray(inp["ret_decay"].reshape(1, DEPTH * 8)), "lora": lora,
            "w0row": np.ascontiguousarray(inp["rwkv_w0"].reshape(1, DEPTH * 2 * 256)), "bmask": make_bmask()}


def make_bmask():
    j = np.arange(128)[:, None]
    t = np.arange(128)[None, :]
    same = lambda b: (j // b == t // b).astype(np.float32)
    m = [same(8), same(16) - same(8), same(32) - same(16), same(64) - same(32), 1.0 - same(64)]
    return np.ascontiguousarray(np.concatenate(m, axis=1), np.float32)


C_NAQ, C_NAK, C_NAV, C_CVA, C_CVB, C_RTQ, C_RTK, C_RTV, C_RTG, C_RWR, C_RWK, C_RWV = [256 * i for i in range(12)]
C_WLOW, C_ALOW, C_GLOW = 3072, 3136, 3200
SEQ_WINS = [(0, 0, 256, 1, 0), (1, 0, 256, 258, 256)] + [(2, 512 * i, 512, 515 + 512 * i, 512 + 512 * i) for i in range(4)]
TM_SLICES = [(i, tok2col(128 * i), 128 * i) for i in range(20)]


def _install_mixers():
    B = Builder

    def _issue_win(self, l, col0, ncols):
        i = self._wn % 2
        self._wn += 1
        key = ("winb", i)
        self.dma("pool", self.winb[:, i, :, 0:ncols], self.w_in[l].rearrange("(k p) n -> p k n", p=128)[:, :, col0:col0 + ncols],
                 (), [key])
        return self.winb[:, i], key

    def load_win(self, l, col0, ncols):
        pref = getattr(self, "_wpref", None)
        if pref is not None and pref[0] == (l, col0, ncols):
            cur = pref[1]
        else:
            cur = self._issue_win(l, col0, ncols)
        self._wpref = None
        sched = getattr(self, "_wsched", None)
        if sched:
            if sched and sched[0] == (col0, ncols):
                sched.pop(0)
            if sched:
                nc0, nn = sched[0]
                self._wpref = ((l, nc0, nn), self._issue_win(l, nc0, nn))
        return cur

    def proj_fm(self, l, col0, evac, wins=None, m=128):
        wv, wk = self.load_win(l, col0, m)
        for win in (wins or SEQ_WINS):
            (s, toff, n, hc, gt) = win
            pst, pk = self.bank()
            for k in range(KC):
                self.mm(pst[0:m, 0:n], wv[:, k, 0:m], self.hbuf[:, k, hc:hc + n], k == 0, k == KC - 1, [wk, "hbuf"], [pk])
            evac(win, pst, pk)
            self.bg_step()

    def proj_tm(self, l, col0, evac, slices=None):
        wv, wk = self.load_win(l, col0, 256)
        for sl in (slices or TM_SLICES):
            (i, hc, gt) = sl
            pst, pk = self.bank()
            for k in range(KC):
                self.mm(pst[:, 0:256], self.hbuf[:, k, hc:hc + 128], wv[:, k, 0:256], k == 0, k == KC - 1, [wk, "hbuf"], [pk])
            evac(sl, pst, pk)

    B._issue_win, B.load_win, B.proj_fm, B.proj_tm = _issue_win, load_win, proj_fm, proj_tm

    def mixer_conv(self, l):
        nc, P = self.nc, self.P
        GP = 2620
        gcol = {0: 15, 1: 286, 2: 557}
        with self.sb("glu", [128, 2, GP], BF16) as glu, self.sb("dg", [128, 31, 2, 128], BF16) as dg, \
                self.sb("sig", [128, 2, 512], F32) as sig, self.sb("co", [128, 2, 2, 512], F32) as co, \
                self.sb("cob", [128, 2, 2, 512], BF16) as cob, self.sb("cosq", [128, 2, 2, 512], BF16) as cosq, \
                self.sb("cst", [128, 2, 3, 512], F32) as st, self.sb("epsc", [128, 1], F32) as epsc:
            self.memset("pool", glu[:], 0.0, ["glu_all"])
            self.memset("pool", epsc[:], EPS, ["epsc"])
            P.barrier()
            for j in range(31):
                for c in range(2):
                    eng = ("dve", "pool")[(j + c) % 2]
                    self.ts(eng, dg[:, j, c, :], self.identb[:], self.V(f"dw{j}_{l}", c), None, ALU.mult, None,
                            ["identb", "vec"], [("dg", j, c)])
            P.barrier()
            self._wsched = None
            self._wpref = None
            for c in range(2):
                hold = {}

                def evac_b(win, pst, pk, c=c, hold=hold):
                    (s, toff, n, hc, gt) = win
                    b = (gt // 512) % 2 if False else 0
                    self.act(sig[:, c, 0:n], pst[:, 0:n], AF.Sigmoid, [pk], [("sig", c, gt)])

                wa, wak = self.load_win(l, C_CVA + 128 * c, 128)
                wb, wbk = self.load_win(l, C_CVB + 128 * c, 128)
                for win in SEQ_WINS:
                    (s, toff, n, hc, gt) = win
                    pb, pbk = self.bank()
                    for k in range(KC):
                        self.mm(pb[:, 0:n], wb[:, k, 0:128], self.hbuf[:, k, hc:hc + n], k == 0, k == KC - 1, [wbk, "hbuf"], [pbk])
                    pa, pak = self.bank()
                    for k in range(KC):
                        self.mm(pa[:, 0:n], wa[:, k, 0:128], self.hbuf[:, k, hc:hc + n], k == 0, k == KC - 1, [wak, "hbuf"], [pak])
                    self.act(sig[:, c, 0:n], pb[:, 0:n], AF.Sigmoid, [pbk], [("sig", c)])
                    g0 = gcol[s] + toff
                    self.tt("dve", glu[:, c, g0:g0 + n], pa[:, 0:n], sig[:, c, 0:n], ALU.mult, [pak, ("sig", c)], [("glu", c, gt)])
            P.barrier()
            def convwin(win, u):
                (s, toff, n, hc, gt) = win
                g0 = gcol[s] + toff - 15
                K_ = lambda nm, *a: (nm, u) + a
                cw = lambda t, c: t[:, u, c, 0:n]
                for c in range(2):
                    pc, pck = self.bank()
                    for j in range(31):
                        self.mm(pc[:, 0:n], dg[:, j, c, :], glu[:, c, g0 + j:g0 + j + n], j == 0, j == 30, [], [pck])
                    yield
                    self.cp("dve", cw(co, c), pc[:, 0:n], [pck], [K_("co", c)])
                    yield
                    self.act(cw(cosq, c), cw(co, c), AF.Square, [K_("co", c)], [K_("cosq", c)])
                    self.act(cw(cob, c), cw(co, c), AF.Copy, [K_("co", c)], [K_("cob", c)])
                    yield
                p1, p1k = self.bank()
                for c in range(2):
                    self.mm(p1[:, 0:n], self.onesb[:], cw(cob, c), c == 0, c == 1, [K_("cob", c)], [p1k])
                p2, p2k = self.bank()
                for c in range(2):
                    self.mm(p2[:, 0:n], self.onesb[:], cw(cosq, c), c == 0, c == 1, [K_("cosq", c)], [p2k])
                yield
                mean, msq, var = st[:, u, 0, 0:n], st[:, u, 1, 0:n], st[:, u, 2, 0:n]
                self.ts("dve", mean, p1[:, 0:n], 1.0 / 256, None, ALU.mult, None, [p1k], [K_("cmean")])
                yield
                self.tt("dve", msq, mean, mean, ALU.mult, [K_("cmean")], [K_("cmsq")])
                yield
                self.stt("dve", var, p2[:, 0:n], 1.0 / 256, msq, ALU.mult, ALU.subtract, [p2k, K_("cmsq")], [K_("cvar")])
                yield
                self.act(var, var, AF.Sqrt, [K_("cvar")], [K_("cvar")], bias=epsc[:, 0:1])
                for c in range(2):
                    self.tt("dve", cw(co, c), cw(co, c), mean, ALU.subtract, [K_("co", c), K_("cmean")], [K_("co", c)])
                yield
                self.recip(var, var, [K_("cvar")], [K_("cvar")])
                yield
                for c in range(2):
                    self.tt("dve", cw(co, c), cw(co, c), var, ALU.mult, [K_("co", c), K_("cvar")], [K_("co", c)])
                yield
                for c in range(2):
                    self.act(self.mixed[:, 2 + c, gt:gt + n], cw(co, c), AF.Silu, [K_("co", c)], [("mixed", 2 + c, gt)],
                             scale=self.V(f"cln_g_{l}", c), bias=self.V(f"cln_b_{l}", c))

            def run_interleaved(gens):
                gens = list(gens)
                while gens:
                    for g in list(gens):
                        try:
                            next(g)
                        except StopIteration:
                            gens.remove(g)

            wins = list(SEQ_WINS)
            for a in range(0, len(wins), 2):
                run_interleaved([convwin(w_, u) for u, w_ in enumerate(wins[a:a + 2])])
            P.barrier()

    B.mixer_conv = mixer_conv


_install_mixers()


NA_TYPES = {0: 0, 1: 1, 14: 3, 15: 4}
NA_TYPE_REP = [0, 1, 2, 14, 15]


def na_bias_tiles(rpb):
    out = np.full((DEPTH, 5, 128, 4, 5, 128), -1e30, np.float32)
    qcol = np.arange(64)
    kcol = np.arange(64)
    cstart = np.clip(qcol - 8, 0, 48)
    col_ok = (kcol[:, None] >= cstart[None, :]) & (kcol[:, None] < cstart[None, :] + 16)
    dcol = np.clip(kcol[:, None] - qcol[None, :], -15, 15) + 15
    for ty, i in enumerate(NA_TYPE_REP):
        kr0 = int(np.clip(2 * i - 4, 0, 22))
        for qr2 in range(2):
            r = 2 * i + qr2
            start = int(np.clip(r - 4, 0, 24))
            for m in range(5):
                for kr2 in range(2):
                    kr = kr0 + 2 * m + kr2
                    if not (start <= kr < start + 8):
                        continue
                    drow = kr - r + 7
                    for l in range(DEPTH):
                        vals = rpb[l][:, drow, :][:, dcol]
                        vals = np.where(col_ok[None], vals, np.float32(-1e30))
                        out[l, ty, kr2 * 64:(kr2 + 1) * 64, :, m, qr2 * 64:(qr2 + 1) * 64] = vals.transpose(1, 0, 2)
    return out


def _install_attn():
    B = Builder

    def mixer_attn(self, l):
        nc, P = self.nc, self.P
        with self.sb("qf", [128, 2, T], BF16) as qf, self.sb("kf", [128, 2, T], BF16) as kf, \
                self.sb("vt", [128, 20, 256], BF16) as vt, self.sb("ckf", [128, 2, 256], BF16) as ckf, \
                self.sb("cvt", [128, 2, 256], BF16) as cvt, self.sb("cst32", [128, 2, 2, 256], F32) as c32, \
                self.sb("nabI", [128, 4, 5, 128], F32) as nabI, self.sb("nabE", [128, 4, 5, 128], F32) as nabE, \
                self.sb("sT", [128, 2, 640], F32) as sT_, self.sb("pT", [128, 2, 7, 128], BF16) as pT_, \
                self.sb("pTp", [128, 2, 256], BF16) as pTp, self.sb("rc", [128, 2, 256], F32) as rc_, \
                self.sb("ost", [128, 2, 256], F32) as ost:
            self.dma("sp", c32[:, 0], self.cnk_in[l].rearrange("(c p) n -> p c n", p=128), (), ["c32k"])
            self.dma("sp", c32[:, 1], self.cnv_in[l].rearrange("(c p) n -> p c n", p=128), (), ["c32v"])
            self.dma("sp", nabI[:].rearrange("p a b c -> p (a b c)"), self.nab_in[l, 2], (), ["nabI"])
            self.cp("dve", cvt[:], c32[:, 1], ["c32v"], ["cvt"])
            for kc in range(2):
                pst, pk = self.bank()
                for c in range(2):
                    self.tr(pst[:, c * 128:(c + 1) * 128], c32[:, 0, kc, c * 128:(c + 1) * 128], self.ident[:], ["c32k", "ident"], [pk])
                self.cp("dve", ckf[:, :, kc * 128:(kc + 1) * 128], pst[:, 0:256].rearrange("p (c n) -> p c n", c=2), [pk], [("ckf", kc)])
            self._wsched = [(C_NAQ, 128), (C_NAK, 128), (C_NAQ + 128, 128), (C_NAK + 128, 128), (C_NAV, 256), (C_NAK, 256)]
            self._wpref = None
            for c in range(2):
                def ev_q(win, pst, pk, c=c):
                    (s, toff, n, hc, gt) = win
                    self.act(qf[:, c, gt:gt + n], pst[:, 0:n], AF.Copy, [pk], [("qf", c, gt)], scale=0.125)
                self.proj_fm(l, C_NAQ + 128 * c, ev_q)

                def ev_k(win, pst, pk, c=c):
                    (s, toff, n, hc, gt) = win
                    self.cp("dve", kf[:, c, gt:gt + n], pst[:, 0:n], [pk], [("kf", c, gt)])
                self.proj_fm(l, C_NAK + 128 * c, ev_k)

            def ev_v(sl, pst, pk):
                (i, hc, gt) = sl
                if i < 4:
                    b = i % 2
                    self.cp("dve", ost[:, b, :], pst[:, 0:256], [pk], [("ost", b)])
                    self.out_toks.append(self.dma("sp", self.nav_out[i // 2, l, (i % 2) * 128:(i % 2 + 1) * 128, :], ost[:, b, :],
                                                  [("ost", b)], ()))
                    self.cp("act", vt[:, i, :], ost[:, b, :], [("ost", b)], [("vt", i)])
                else:
                    self.cp("act", vt[:, i, :], pst[:, 0:256], [pk], [("vt", i)])
            self.proj_tm(l, C_NAV, ev_v)

            def ev_kt(sl, pst, pk):
                (i, hc, gt) = sl
                b = i % 2
                self.cp("dve", ost[:, b, :], pst[:, 0:256], [pk], [("ost", b)])
                self.out_toks.append(self.dma("sp", self.nak_out[i // 2, l, (i % 2) * 128:(i % 2 + 1) * 128, :], ost[:, b, :],
                                              [("ost", b)], ()))
            self.proj_tm(l, C_NAK, ev_kt, slices=TM_SLICES[:4])
            P.barrier()
            for s in range(2):
                for hd in range(4):
                    c, pb = hd // 2, (hd % 2) * 64
                    pS, pSk = self.bank()
                    for kc in range(2):
                        self.mm(pS[:, kc * 256:(kc + 1) * 256], kf[pb:pb + 64, c, s * 256 + kc * 128:s * 256 + (kc + 1) * 128],
                                qf[pb:pb + 64, c, s * 256:(s + 1) * 256], True, True, [], [pSk])
                    self.act(pTp[:].rearrange("p a b -> p (a b)"), pS[:, 0:512], AF.Exp, [pSk], ["pTp"])
                    pO, pOk = self.bank()
                    for kc in range(2):
                        self.mm(pO[pb:pb + 64, 0:256], vt[:, s * 2 + kc, hd * 64:(hd + 1) * 64], pTp[:, kc, :], kc == 0, kc == 1,
                                ["pTp"], [pOk])
                    for kc in range(2):
                        self.mm(pO[pb:pb + 64, 256:512], self.onesb[:, 0:64], pTp[:, kc, :], kc == 0, kc == 1, ["pTp"], [pOk])
                    self.recip(rc_[pb:pb + 64, 0, 0:256], pO[pb:pb + 64, 256:512], [pOk], ["rc"])
                    self.tt("dve", self.mixed[pb:pb + 64, c, s * 256:(s + 1) * 256], pO[pb:pb + 64, 0:256], rc_[pb:pb + 64, 0, 0:256],
                            ALU.mult, [pOk, "rc"], [("mixed", c, s, hd)])
            for i in range(16):
                ty = NA_TYPES.get(i, 2)
                if ty == 2:
                    nab, nabk = nabI, "nabI"
                else:
                    self.dma("sp", nabE[:].rearrange("p a b c -> p (a b c)"), self.nab_in[l, ty], (), ["nabE"])
                    nab, nabk = nabE, "nabE"
                kr0 = min(max(2 * i - 4, 0), 22)
                qtok = 512 + i * 128
                def na_unit(hd, u):
                    c, pb = hd // 2, (hd % 2) * 64
                    sT, pT, rc = sT_[:, u], pT_[:, u], rc_[:, u]
                    q_ap = qf[pb:pb + 64, c, qtok:qtok + 128]
                    pA, pAk = self.bank()
                    pB, pBk = self.bank()
                    for m in range(5):
                        kt = 512 + (kr0 // 2 + m) * 128
                        dst = pA[:, m * 128:(m + 1) * 128] if m < 4 else pB[:, 0:128]
                        self.mm(dst, kf[pb:pb + 64, c, kt:kt + 128], q_ap, True, True, [], [pAk if m < 4 else pBk])
                    for kc in range(2):
                        self.mm(pB[:, 128 + kc * 128:256 + kc * 128], ckf[pb:pb + 64, c, kc * 128:(kc + 1) * 128], q_ap, True, True,
                                [], [pBk])
                    yield
                    self.tt("dve", sT[:, 0:512], pA[:, 0:512], nab[:, hd, 0:4, :].rearrange("p a b -> p (a b)"), ALU.add,
                            [pAk, nabk], [("sT0", u)])
                    yield
                    self.tt("dve", sT[:, 512:640], pB[:, 0:128], nab[:, hd, 4, :], ALU.add, [pBk, nabk], [("sT1", u)])
                    yield
                    pTf = pT.rearrange("p a b -> p (a b)")
                    self.act(pTf[:, 0:640], sT[:, 0:640], AF.Exp, [("sT0", u), ("sT1", u)], [("pT0", u)])
                    self.act(pTf[:, 640:896], pB[:, 128:384], AF.Exp, [pBk], [("pT1", u)])
                    yield
                    pO, pOk = self.bank()
                    for half in range(2):
                        for m in range(7):
                            if half == 0:
                                lhs = vt[:, 4 + kr0 // 2 + m, hd * 64:(hd + 1) * 64] if m < 5 else cvt[:, m - 5, hd * 64:(hd + 1) * 64]
                            else:
                                lhs = self.onesb[:, 0:64]
                            self.mm(pO[pb:pb + 64, half * 128:(half + 1) * 128], lhs, pT[:, m, :], m == 0, m == 6,
                                    [("pT0", u), ("pT1", u)], [pOk])
                    yield
                    self.recip(rc[pb:pb + 64, 0:128], pO[pb:pb + 64, 128:256], [pOk], [("rc", u)])
                    yield
                    self.tt("dve", self.mixed[pb:pb + 64, c, qtok:qtok + 128], pO[pb:pb + 64, 0:128], rc[pb:pb + 64, 0:128],
                            ALU.mult, [pOk, ("rc", u)], [("mixed", c, i, hd)])

                for h0 in (0, 2):
                    gens = [na_unit(h0 + u, u) for u in range(2)]
                    while gens:
                        for g in list(gens):
                            try:
                                next(g)
                            except StopIteration:
                                gens.remove(g)
            P.barrier()

    B.mixer_attn = mixer_attn

    def phase_mixers(self, l):
        import os
        which = os.environ.get("MIX", "abcd")
        with self.sb("winb", [128, 2, KC, 256], BF16) as winb:
            self.winb = winb
            self._wn = 0
            self.memset("pool", self.mixed[:], 0.0, ["mixed_all"])
            self.P.barrier()
            with ExitStack() as esb:
                if l == 0:
                    adab = esb.enter_context(self.sb("adab1", [128, 2, KC, 512], F32))
                    self._bg.append(self.gen_mod(self.cvt, adab, [1]))
                if "b" in which:
                    self.mixer_conv(l)
                if "a" in which:
                    self.mixer_attn(l)
                while self._bg:
                    self.bg_step()
                self.P.barrier()
            if "c" in which:
                self.mixer_ret(l)
            self.P.barrier()
        if "d" in which:
            self.mixer_rwkv(l)
        self.P.barrier()

    B.phase_mixers = phase_mixers


_install_attn()


CST = {}
_o = 0
for _n, _w in (("ident", 128), ("perm", 128), ("pdf", 128), ("pdb", 128), ("mf", 128), ("mb", 128), ("blk", 128),
               ("ef", 128), ("eb", 128), ("ecf", 1), ("ecb", 1), ("tri_i_f", 128), ("tri_e_f", 128), ("tri_i_b", 128),
               ("tri_e_b", 128), ("mstrict_f", 128), ("mstrict_b", 128)):
    CST[_n] = (_o, _w)
    _o += _w
NCST = _o


def make_cst():
    c = np.zeros((128, NCST), np.float32)
    j = np.arange(128)[:, None].astype(np.float32)
    t = np.arange(128)[None, :].astype(np.float32)

    def put(n, a):
        o, w = CST[n]
        c[:, o:o + w] = a

    put("ident", np.eye(128))
    pm = np.zeros((128, 128))
    for m in range(128):
        pm[m ^ 32, m] = 1.0
    put("perm", pm)
    put("pdf", np.maximum(t - j, 0))
    put("pdb", np.maximum(j - t, 0))
    put("mf", (t >= j))
    put("mb", (j >= t))
    blk = np.zeros((128, 128))
    blk[:64, :64] = 1
    blk[64:, 64:] = 1
    put("blk", blk)
    put("ef", np.broadcast_to(t + 1, (128, 128)))
    put("eb", np.broadcast_to(128 - t, (128, 128)))
    put("ecf", 127 - j)
    put("ecb", j)
    put("tri_i_f", (j <= t))
    put("tri_e_f", (j < t))
    put("tri_i_b", (j >= t))
    put("tri_e_b", (j > t))
    put("mstrict_f", (j < t))
    put("mstrict_b", (j > t))
    return c


def make_rope():
    tt = np.arange(2048)
    nfreq = 16
    inv = (10000.0 ** (-np.arange(nfreq, dtype=np.float32) / nfreq)).astype(np.float32)
    ang = np.concatenate([(tt // 64).astype(np.float32)[:, None] * inv, (tt % 64).astype(np.float32)[:, None] * inv], axis=-1)
    cos, sin = np.cos(ang).astype(np.float32), np.sin(ang).astype(np.float32)
    r = np.zeros((128, 2, 2048), np.float32)
    for p in range(128):
        d = p % 64
        r[p, 0] = cos[:, d % 32]
        r[p, 1] = (-sin[:, d % 32]) if d < 32 else sin[:, d % 32]
    return r


def _install_ret():
    B = Builder

    def C(self, name):
        o, w = CST[name]
        return self.cstt[:, o:o + w]

    B.C = C

    def mixer_ret(self, l):
        nc, P = self.nc, self.P
        with ExitStack() as es:
            A = lambda n, sh, dt: es.enter_context(self.sb(n, sh, dt))
            rq = A("rq", [128, 2, T], BF16)
            rk = A("rk", [128, 2, T], BF16)
            rg = A("rg", [128, 2, T], BF16)
            rv = A("rv", [128, 20, 256], BF16)
            qd = A("qd", [128, 2, 2, 2, 128], BF16)
            kT = A("kT", [128, 2, 256], BF16)
            rope = A("rope", [128, 2, 2048], F32)
            rdb = A("rdb", [128, 16], F32)
            lgc = A("lgc", [128, 2, 2], F32)
            gC = A("gC", [128, 2, 2], F32)
            DQ = A("DQ", [128, 2, 2, 128], F32)
            Dall = A("Dall", [128, 4, 128], F32)
            DK = A("DK", [128, 2, 4], F32)
            dtmp = A("dtmp", [128, 2, 128], F32)
            xr = A("xr", [128, 512], BF16)
            rt = A("rt", [128, 2, 512], F32)
            AT_ = A("AT", [128, 2, 4, 128], BF16)
            Srun = A("Srun", [128, 2, 2, 64], F32)
            Sall = A("Sall", [128, 20, 2, 2, 64], BF16)
            of_ = A("of", [128, 2, 2, 128], F32)
            ob_ = A("ob", [128, 2, 2, 2, 128], BF16)
            hst_ = A("hst", [128, 2, 3, 256], F32)
            epsr = A("epsr", [128, 1], F32)
            permb = A("permb", [128, 128], BF16)
            blkb = A("blkb", [128, 128], BF16)
            self.memset("pool", epsr[:], EPS, ["epsr"])
            self.cp("dve", permb[:], self.C("perm"), [], ["permb"])
            self.cp("dve", blkb[:], self.C("blk"), [], ["blkb"])
            self.dma("sp", rope[:], self.rope_in, (), ["rope"])
            self.dma("sp", rdb[:], self.retd_in.partition_broadcast(128), (), ["rdb"])
            self.act(rdb[:], rdb[:], AF.Exp, ["rdb"], ["rdb"], scale=-1.0)
            self.ts("dve", rdb[:], rdb[:], 1.0, None, ALU.add, None, ["rdb"], ["rdb"])
            self.act(rdb[:], rdb[:], AF.Ln, ["rdb"], ["rdb"])
            self.ts("dve", rdb[:], rdb[:], -1.0, None, ALU.mult, None, ["rdb"], ["rdb"])
            base = l * 8
            for c in range(2):
                for d in range(2):
                    for h2 in range(2):
                        col = base + d * 4 + 2 * c + h2
                        self.cp("dve", lgc[h2 * 64:(h2 + 1) * 64, c, d:d + 1], rdb[h2 * 64:(h2 + 1) * 64, col:col + 1], ["rdb"], [("lgc", c, d, h2)])
            P.barrier()
            for c in range(2):
                for d in range(2):
                    self.act(gC[:, c, d:d + 1], lgc[:, c, d:d + 1], AF.Exp, [], [("gC", c, d)], scale=128.0)
                    self.act(DQ[:, c, d, :], self.C("ef" if d == 0 else "eb"), AF.Exp, [], [("DQ", c, d)], scale=lgc[:, c, d:d + 1])
            for h in range(4):
                for d in range(2):
                    col = base + d * 4 + h
                    self.act(dtmp[:, d, :], self.C("pdf" if d == 0 else "pdb"), AF.Exp, [], [("dtmp", d)], scale=rdb[:, col:col + 1])
                    self.tt("dve", dtmp[:, d, :], dtmp[:, d, :], self.C("mf" if d == 0 else "mb"), ALU.mult, [("dtmp", d)], [("dtmp", d)])
                self.tt("dve", Dall[:, h, :], dtmp[:, 0, :], dtmp[:, 1, :], ALU.add, [("dtmp", 0), ("dtmp", 1)], [("Dall", h)])
            for d in range(2):
                self.ts("dve", DK[:, d, :], rdb[:, base + d * 4:base + d * 4 + 4], self.C("ecf" if d == 0 else "ecb"), None, ALU.mult, None,
                        [], [("DK", d)])
                self.act(DK[:, d, :], DK[:, d, :], AF.Exp, [("DK", d)], [("DK", d)])
            P.barrier()
            import os
            rcut = int(os.environ.get("RCUT", "9"))
            if rcut < 1:
                return
            self._wsched = [(C_RTQ, 128), (C_RTQ + 128, 128), (C_RTK, 128), (C_RTK + 128, 128), (C_RTG, 128), (C_RTG + 128, 128), (C_RTV, 256)]
            self._wpref = None
            for which, col0, dst, scl in (("q", C_RTQ, rq, 1.0), ("k", C_RTK, rk, 0.125)):
                for c in range(2):
                    def ev(win, pst, pk, c=c, dst=dst, scl=scl):
                        (s, toff, n, hc, gt) = win
                        if s < 2:
                            self.act(dst[:, c, gt:gt + n], pst[:, 0:n], AF.Copy, [pk], [("rqk", gt)], scale=scl)
                            return
                        self.act(xr[:, 0:n], pst[:, 0:n], AF.Copy, [pk], ["xr"], scale=scl)
                        psw, pswk = self.bank()
                        self.mm(psw[:, 0:n], permb[:], xr[:, 0:n], True, True, ["xr", "permb"], [pswk])
                        self.stt("dve", rt[:, 0, 0:n], pst[:, 0:n], scl, rope[:, 0, toff:toff + n], ALU.mult, ALU.mult, [pk, "rope"], ["rt0"])
                        self.tt("dve", rt[:, 1, 0:n], psw[:, 0:n], rope[:, 1, toff:toff + n], ALU.mult, [pswk, "rope"], ["rt1"])
                        self.tt("pool", dst[:, c, gt:gt + n], rt[:, 0, 0:n], rt[:, 1, 0:n], ALU.add, ["rt0", "rt1"], [("rqk", gt)])
                    self.proj_fm(l, col0 + 128 * c, ev)
            for c in range(2):
                def ev_g(win, pst, pk, c=c):
                    (s, toff, n, hc, gt) = win
                    self.act(rg[:, c, gt:gt + n], pst[:, 0:n], AF.Silu, [pk], [("rg", gt)])
                self.proj_fm(l, C_RTG + 128 * c, ev_g)

            def ev_v(sl, pst, pk):
                (i, hc, gt) = sl
                self.cp("act", rv[:, i, :], pst[:, 0:256], [pk], [("rv", i)])
            self.proj_tm(l, C_RTV, ev_v)
            P.barrier()
            if rcut < 2:
                return
            def run_interleaved(gens):
                gens = list(gens)
                while gens:
                    for g in list(gens):
                        try:
                            next(g)
                        except StopIteration:
                            gens.remove(g)

            def chain(s, i0, nch, d):
                if s == 2:
                    for c in range(2):
                        self.dma("sp", Srun[:, d, c, :], self.sret_in[l, d, 2 * c:2 * c + 2].rearrange("h e v -> (h e) v"), (), [("Srun", d, c)])
                else:
                    self.memset("pool", Srun[:, d], 0.0, [("Srun", d, 0), ("Srun", d, 1)])
                yield
                order = list(range(i0, i0 + nch)) if d == 0 else list(range(i0 + nch - 1, i0 - 1, -1))
                for i in order:
                    self.cp("act", Sall[:, i, :, d, :], Srun[:, d], [("Srun", d, 0), ("Srun", d, 1)], [("Sall", i, d)])
                    ptr, ptrk = self.bank()
                    pb16 = ptr[:].bitcast(BF16)
                    for c in range(2):
                        self.tr(pb16[:, c * 128:(c + 1) * 128], rk[:, c, i * 128:(i + 1) * 128], self.identb[:], ["identb"], [ptrk])
                    yield
                    self.tt("dve", kT[:, d, :].rearrange("p (h e) -> p h e", h=4), pb16[:, 0:256].rearrange("p (h e) -> p h e", h=4),
                            DK[:, d, :].unsqueeze(2).to_broadcast([128, 4, 64]), ALU.mult, [ptrk], [("kT", d)])
                    yield
                    pkv, pkvk = self.bank()
                    for c in range(2):
                        self.mm(pkv[:, c * 128:(c + 1) * 128], kT[:, d, c * 128:(c + 1) * 128], rv[:, i, c * 128:(c + 1) * 128],
                                True, True, [("kT", d)], [pkvk])
                    yield
                    for c in range(2):
                        for h2 in range(2):
                            pr = slice(h2 * 64, (h2 + 1) * 64)
                            self.stt("dve", Srun[pr, d, c, :], Srun[pr, d, c, :], gC[pr, c, d:d + 1],
                                     pkv[pr, c * 128 + h2 * 64:c * 128 + (h2 + 1) * 64], ALU.mult, ALU.add,
                                     [pkvk, ("Srun", d, c), ("Sall", i, d)], [("Srun", d, c)])
                    yield
                if s < 2:
                    for c in range(2):
                        self.out_toks.append(self.dma("sp", self.oret_out[s, l, d, 2 * c:2 * c + 2].rearrange("h e v -> (h e) v"),
                                                      Srun[:, d, c, :], [("Srun", d, c)], ()))

            def outunit(i, u):
                tk = i * 128
                AT, of, ob, hst = AT_[:, u], of_[:, u], ob_[:, u], hst_[:, u]
                pkqs = []
                for par in range(2):
                    pkq, pkqk = self.bank()
                    pkqs.append((pkq, pkqk))
                    for c in range(2):
                        pb = par * 64
                        self.mm(pkq[:, c * 128:(c + 1) * 128], rk[pb:pb + 64, c, tk:tk + 128], rq[pb:pb + 64, c, tk:tk + 128], True, True, [], [pkqk])
                for d in range(2):
                    self.tt("dve", qd[:, u, d], rq[:, :, tk:tk + 128], DQ[:, :, d, :], ALU.mult, [], [("qd", u, d)])
                yield
                for par in range(2):
                    pkq, pkqk = pkqs[par]
                    self.tt("dve", AT[:, par::2, :], pkq[:, 0:256].rearrange("p (c t) -> p c t", c=2), Dall[:, par::2, :], ALU.mult,
                            [pkqk], [("AT", u, par), ("AT", u, par + 2)])
                yield
                po, pok = self.bank()
                for h in (0, 2, 1, 3):
                    c, pb = h // 2, (h % 2) * 64
                    dst = po[pb:pb + 64, c * 128:(c + 1) * 128]
                    self.mm(dst, rv[:, i, h * 64:(h + 1) * 64], AT[:, h, :], True, False, [("AT", u, h)], [pok])
                    for d in range(2):
                        self.mm(dst, Sall[pb:pb + 64, i, c, d, :], qd[pb:pb + 64, u, d, c, :], False, d == 1,
                                [("Sall", i, 0), ("Sall", i, 1), ("qd", u, d)], [pok])
                yield
                yield from self.head_norm_gen(l, po, pok, of, ob, hst, blkb, epsr, f"ret_gn_{l}", rg, 4, tk, None, None, u)

            for (s, i0, nch) in ((0, 0, 2), (1, 2, 2), (2, 4, 16)):
                run_interleaved([chain(s, i0, nch, d) for d in range(2)])
                for a in range(i0, i0 + nch, 2):
                    run_interleaved([outunit(i, u) for u, i in enumerate(range(a, min(a + 2, i0 + nch)))])
            P.barrier()

    B.mixer_ret = mixer_ret

    def head_norm_out(self, l, po, pok, of, ob, hst, blkb, epsr, eps, gname, gate, mix0, tk, bonus):
        if po is not None:
            self.cp("dve", of[:].rearrange("p c t -> p (c t)"), po[:, 0:256], [pok], ["of"])
        self.act(ob[:, 0].rearrange("p c t -> p (c t)"), of[:].rearrange("p c t -> p (c t)"), AF.Copy, ["of"], ["ob0"])
        self.act(ob[:, 1].rearrange("p c t -> p (c t)"), of[:].rearrange("p c t -> p (c t)"), AF.Square, ["of"], ["ob1"])
        ps, psk = self.bank()
        self.mm(ps[:, 0:256], blkb[:], ob[:, 0].rearrange("p c t -> p (c t)"), True, True, ["ob0", "blkb"], [psk])
        self.mm(ps[:, 256:512], blkb[:], ob[:, 1].rearrange("p c t -> p (c t)"), True, True, ["ob1", "blkb"], [psk])
        mean, msq, var = hst[:, 0, :], hst[:, 1, :], hst[:, 2, :]
        self.ts("dve", mean, ps[:, 0:256], 1.0 / 64, None, ALU.mult, None, [psk], ["hmean"])
        self.tt("dve", msq, mean, mean, ALU.mult, ["hmean"], ["hmsq"])
        self.stt("dve", var, ps[:, 256:512], 1.0 / 64, msq, ALU.mult, ALU.subtract, [psk, "hmsq"], ["hvar"])
        self.act(var, var, AF.Sqrt, ["hvar"], ["hvar"], bias=epsr[:, 0:1])
        self.recip(var, var, ["hvar"], ["hvar"])
        ofl = of[:].rearrange("p c t -> p (c t)")
        self.tt("dve", ofl, ofl, mean, ALU.subtract, ["of", "hmean"], ["of"])
        self.tt("dve", ofl, ofl, var, ALU.mult, ["of", "hvar"], ["of"])
        for c in range(2):
            if bonus is None:
                self.stt("dve", self.mixed[:, mix0 + c, tk:tk + 128], of[:, c, :], self.V(gname, c), gate[:, c, tk:tk + 128],
                         ALU.mult, ALU.mult, ["of"], [("mixed", mix0 + c, tk)])
            else:
                self.stt("dve", of[:, c, :], of[:, c, :], self.V(gname, c), bonus[:, c, :], ALU.mult, ALU.add, ["of"], ["of"])
                self.tt("dve", self.mixed[:, mix0 + c, tk:tk + 128], of[:, c, :], gate[:, c, tk:tk + 128], ALU.mult, ["of"],
                        [("mixed", mix0 + c, tk)])

    B.head_norm_out = head_norm_out

    def head_norm_gen(self, l, po, pok, of, ob, hst, blkb, epsr, gname, gate, mix0, tk, bonus, add, u):
        fl = lambda a: a.rearrange("p c t -> p (c t)")
        K_ = lambda n: (n, u)
        if add is None:
            self.cp("dve", fl(of), po[:, 0:256], [pok], [K_("of")])
        else:
            self.tt("dve", of, po[:, 0:256].rearrange("p (c t) -> p c t", c=2), add, ALU.add, [pok], [K_("of")])
        yield
        self.act(fl(ob[:, 0]), fl(of), AF.Copy, [K_("of")], [K_("ob0")])
        self.act(fl(ob[:, 1]), fl(of), AF.Square, [K_("of")], [K_("ob1")])
        yield
        ps, psk = self.bank()
        self.mm(ps[:, 0:256], blkb[:], fl(ob[:, 0]), True, True, [K_("ob0")], [psk])
        self.mm(ps[:, 256:512], blkb[:], fl(ob[:, 1]), True, True, [K_("ob1")], [psk])
        yield
        mean, msq, var = hst[:, 0, :], hst[:, 1, :], hst[:, 2, :]
        self.act(mean, ps[:, 0:256], AF.Copy, [psk], [K_("hmean")], scale=1.0 / 64)
        yield
        self.tt("dve", msq, mean, mean, ALU.mult, [K_("hmean")], [K_("hmsq")])
        yield
        self.stt("dve", var, ps[:, 256:512], 1.0 / 64, msq, ALU.mult, ALU.subtract, [psk, K_("hmsq")], [K_("hvar")])
        yield
        self.act(var, var, AF.Sqrt, [K_("hvar")], [K_("hvar")], bias=epsr[:, 0:1])
        self.tt("dve", fl(of), fl(of), mean, ALU.subtract, [K_("of"), K_("hmean")], [K_("of")])
        yield
        self.recip(var, var, [K_("hvar")], [K_("hvar")])
        yield
        self.tt("dve", fl(of), fl(of), var, ALU.mult, [K_("of"), K_("hvar")], [K_("of")])
        yield
        for c in range(2):
            if bonus is None:
                self.stt("dve", self.mixed[:, mix0 + c, tk:tk + 128], of[:, c, :], self.V(gname, c), gate[:, c, tk:tk + 128],
                         ALU.mult, ALU.mult, [K_("of")], [("mixed", mix0 + c, tk)])
            else:
                self.stt("dve", of[:, c, :], of[:, c, :], self.V(gname, c), bonus[:, c, :], ALU.mult, ALU.add, [K_("of")], [K_("of")])
        if bonus is not None:
            yield
            for c in range(2):
                self.tt("dve", self.mixed[:, mix0 + c, tk:tk + 128], of[:, c, :], gate[:, c, tk:tk + 128], ALU.mult, [K_("of")],
                        [("mixed", mix0 + c, tk)])

    B.head_norm_gen = head_norm_gen


_install_ret()


def _install_rwkv():
    B = Builder
    NEG_E = -float(np.exp(-0.5))

    def mixer_rwkv(self, l):
        nc, P = self.nc, self.P
        with ExitStack() as es:
            A = lambda n, sh, dt: es.enter_context(self.sb(n, sh, dt))
            dr = A("dr", [128, 2, T], BF16)
            dk = A("dk", [128, 2, T], BF16)
            wlt = A("wlt", [128, T], BF16)
            alt = A("alt", [128, T], BF16)
            sg = self.mixed[:, 7, :]
            ball = self.mixed[:, 6:8, :]
            sc = A("rsc", [128, 8], F32)
            eps12 = A("eps12", [128, 1], F32)
            epsg = A("epsg", [128, 1], F32)
            blkb = A("blkb2", [128, 128], BF16)
            HB = [self.hbuf[:, 2 * n:2 * n + 2, 0:T] for n in range(4)]
            self.memset("pool", eps12[:], 1e-12, ["e12"])
            self.memset("pool", epsg[:], 64e-5, ["epsg"])
            self.cp("dve", blkb[:], self.C("blk"), [], ["blkb2"])
            ka, rkv = self.V(f"ka_{l}"), self.V(f"rk_{l}")
            self.tt("dve", sc[:, 0:2], rkv, ka, ALU.mult, [], ["sc0"])
            self.ts("dve", sc[:, 4:6], ka, -1.0, 1.0, ALU.mult, ALU.add, [], ["sc4"])
            self.ts("dve", sc[:, 2:4], sc[:, 4:6], 2.0, None, ALU.mult, None, ["sc4"], ["sc2"])
            self.tt("dve", sc[:, 2:4], sc[:, 2:4], rkv, ALU.mult, ["sc2"], ["sc2"])
            kk = A("kk", [128, 2, T], BF16)
            vtm = A("vtm", [128, 20, 256], BF16)
            lorab = A("lorab", [128, 1280], BF16)
            w0b = A("w0b", [128, 512], F32)
            m4 = A("m4", [128, 2, 512], F32)
            mT = A("mT", [128, 2, 128], F32)
            es_dv = es.enter_context(ExitStack())
            dv = es_dv.enter_context(self.sb("dv", [128, 2, T], BF16))
            with ExitStack() as es2:
                self.winb = es2.enter_context(self.sb("winb", [128, 2, KC, 256], BF16))
                self._wn = 0
                raw = es2.enter_context(self.sb("raw", [128, 6, TP], BF16))
                dsh = es2.enter_context(self.sb("dsh", [128, 3, 6, 128], BF16))
                self.memset("pool", raw[:], 0.0, ["raw_all"])
                for tp in range(3):
                    for ch in range(6):
                        self.ts(("dve", "pool")[(tp + ch) % 2], dsh[:, tp, ch, :], self.identb[:], self.V(f"shift{tp}_{l}", ch), None,
                                ALU.mult, None, [], [("dsh", tp, ch)])
                P.barrier()
                self._wsched = [(C_RWR + 128 * ch, 128) for ch in range(6)] + [(C_WLOW, 128), (C_GLOW, 128)]
                self._wpref = None
                for ch in range(6):
                    def ev(win, pst, pk, ch=ch):
                        (s, toff, n, hc, gt) = win
                        self.cp(("act", "dve")[ch % 2], raw[:, ch, hc:hc + n], pst[:, 0:n], [pk], [("raw", ch, gt)])
                    self.proj_fm(l, C_RWR + 128 * ch, ev)

                def ev_wa(win, pst, pk):
                    (s, toff, n, hc, gt) = win
                    self.act(wlt[0:64, gt:gt + n], pst[0:64, 0:n], AF.Tanh, [pk], [("wlt", gt)])
                    self.cp("dve", alt[64:128, gt:gt + n], pst[64:128, 0:n], [pk], [("alt", gt)])
                self.proj_fm(l, C_WLOW, ev_wa)

                def ev_g(win, pst, pk):
                    (s, toff, n, hc, gt) = win
                    self.act(sg[:, gt:gt + n], pst[:, 0:n], AF.Sigmoid, [pk], [("sg", gt)])
                self.proj_fm(l, C_GLOW, ev_g)
                P.barrier()
                for win in SEQ_WINS:
                    (s, toff, n, hc, gt) = win
                    for ch in range(6):
                        dst = (dr, dk, dv)[ch // 2]
                        pc, pck = self.bank()
                        for tp in range(3):
                            self.mm(pc[:, 0:n], dsh[:, tp, ch, :], raw[:, ch, hc - 1 + tp:hc - 1 + tp + n], tp == 0, tp == 2, [], [pck])
                        self.cp(("act", "dve")[ch % 2], dst[:, ch % 2, gt:gt + n], pc[:, 0:n], [pck], [("d", ch, gt)])
                P.barrier()
            for d in range(2):
                st_, in_ = ("mstrict_f", "mf") if d == 0 else ("mstrict_b", "mb")
                stT = "mstrict_b" if d == 0 else "mstrict_f"
                self.ts("dve", m4[:, d, 0:128], self.C(st_), -1.0, None, ALU.mult, None, [], [("m4", d, 0)])
                self.cp("dve", m4[:, d, 128:256], self.C(in_), [], [("m4", d, 1)])
                self.cp("dve", m4[:, d, 256:384], self.C(st_), [], [("m4", d, 2)])
                self.cp("dve", m4[:, d, 384:512], self.C(in_), [], [("m4", d, 3)])
                self.ts("dve", mT[:, d, :], self.C(stT), -1.0, None, ALU.mult, None, [], [("mT", d)])
            self.dma("pool", lorab[:], self.lora_in[l], (), ["lorab"])
            self.dma("sp", w0b[:], self.w0row_in[:, l * 512:(l + 1) * 512].partition_broadcast(128), (), ["w0b"])
            P.barrier()
            with ExitStack() as es3:
                tf = es3.enter_context(self.sb("ktf", [128, 2, 512], F32))
                tb = es3.enter_context(self.sb("ktb", [128, 2, 512], BF16))
                rn = es3.enter_context(self.sb("krn", [128, 2, 512], F32))
                asg = es3.enter_context(self.sb("asg", [128, 2, 2, 512], F32))
                for win in SEQ_WINS:
                    (s, toff, n, hc, gt) = win
                    for c in range(2):
                        self.ts("dve", tf[:, c, 0:n], dk[:, c, gt:gt + n], self.V(f"kk_{l}", c), None, ALU.mult, None, [], [("tf", c)])
                        self.act(tb[:, c, 0:n], tf[:, c, 0:n], AF.Square, [("tf", c)], [("tb", c)])
                        pq, pqk = self.bank()
                        self.mm(pq[:, 0:n], blkb[:], tb[:, c, 0:n], True, True, [("tb", c), "blkb2"], [pqk])
                        self.act(rn[:, c, 0:n], pq[:, 0:n], AF.Sqrt, [pqk, "e12"], [("rn", c)], bias=eps12[:, 0:1])
                        self.recip(rn[:, c, 0:n], rn[:, c, 0:n], [("rn", c)], [("rn", c)])
                        self.tt("dve", kk[:, c, gt:gt + n], tf[:, c, 0:n], rn[:, c, 0:n], ALU.mult, [("tf", c), ("rn", c)], [("kk", c, gt)])
                    for c in range(2):
                        pg, pgk = self.bank()
                        self.mm(pg[:, 0:n], lorab[:, 1024 + c * 128:1024 + (c + 1) * 128], sg[:, gt:gt + n], True, True, ["lorab"], [pgk])
                        self.cp("act", HB[2][:, c, gt:gt + n], pg[:, 0:n], [pgk], [("gate", c, gt)])
                    for d in range(2):
                        for c in range(2):
                            pa, pak = self.bank()
                            self.mm(pa[:, 0:n], lorab[64:128, 512 + d * 256 + c * 128:512 + d * 256 + (c + 1) * 128], alt[64:128, gt:gt + n],
                                    True, True, ["lorab"], [pak])
                            self.act(asg[:, d, c, 0:n], pa[:, 0:n], AF.Sigmoid, [pak], [("asg", d, c)], bias=self.V(f"a0_{d}_{l}", c))
                    for c in range(2):
                        self.tt("dve", asg[:, 0, c, 0:n], asg[:, 0, c, 0:n], asg[:, 1, c, 0:n], ALU.add, [("asg", 0, c), ("asg", 1, c)], [("asg", 0, c)])
                        self.ts("dve", asg[:, 0, c, 0:n], asg[:, 0, c, 0:n], sc[:, c:c + 1], sc[:, 2 + c:3 + c], ALU.mult, ALU.add,
                                [("asg", 0, c), "sc0", "sc2"], [("asg", 0, c)])
                        self.tt("dve", tf[:, c, 0:n], dr[:, c, gt:gt + n], dk[:, c, gt:gt + n], ALU.mult, [("tf", c)], [("tf", c)])
                        self.tt("dve", tb[:, c, 0:n], tf[:, c, 0:n], asg[:, 0, c, 0:n], ALU.mult, [("tf", c), ("asg", 0, c)], [("tb", c)])
                        pq, pqk = self.bank()
                        self.mm(pq[:, 0:n], blkb[:], tb[:, c, 0:n], True, True, [("tb", c), "blkb2"], [pqk])
                        self.tt("dve", HB[3][:, c, gt:gt + n], pq[:, 0:n], dv[:, c, gt:gt + n], ALU.mult, [pqk], [("bonus", c, gt)])
                for i in range(20):
                    pt, ptk = self.bank()
                    p16 = pt[:].bitcast(BF16)
                    for c in range(2):
                        self.tr(p16[:, c * 128:(c + 1) * 128], dv[:, c, i * 128:(i + 1) * 128], self.identb[:], [], [ptk])
                    self.cp("act", vtm[:, i, :], p16[:, 0:256], [ptk], [("vtm", i)])
                P.barrier()
            es_dv.close()
            with ExitStack() as es4:
                A4 = lambda n, sh, dt: es4.enter_context(self.sb(n, sh, dt))
                zt_ = A4("zt", [128, 2, 256], F32)
                lw_ = zt_
                cl2_ = A4("cl2", [128, 2, 2], F32)
                gam = A4("gam", [128, 2, 2], F32)
                E_ = A4("E", [128, 2, 4, 2, 128], BF16)
                BK_ = A4("BK", [128, 2, 4, 2, 128], BF16)
                KR = A4("KR", [128, 2, 2, 2, 128], BF16)
                BKt = A4("BKt", [128, 2, 2, 256], BF16)
                MN = A4("MN", [128, 2, 4, 4, 128], BF16)
                Nb = A4("Nb", [128, 2, 1, 4, 128], BF16)
                NTb = A4("NTb", [128, 2, 1, 4, 128], BF16)
                WK = A4("WK", [128, 2, 6, 4, 128], BF16)
                bm = A4("bm", [128, 5, 128], BF16)
                self.dma("pool", bm[:].rearrange("p a b -> p (a b)"), self.bmask_in, (), ["bm"])
                Xn = A4("Xn", [128, 256], BF16)
                UT = A4("UT", [128, 256], BF16)
                Zrun = A4("Zrun", [128, 2, 64], F32)
                Zbd = A4("Zbd", [128, 2, 128], BF16)
                zin = A4("zin", [128, 2, 64], F32)
                zo = A4("zo", [64, 4, 64], F32)
                of_ = A4("of2", [128, 2, 2, 128], F32)
                ob_ = A4("ob2", [128, 2, 2, 2, 128], BF16)
                hst_ = A4("hst2", [128, 2, 3, 256], F32)
                self.memset("pool", Zbd[:], 0.0, ["Zbd_all"])
                P.barrier()

                def early(i, q, d, kd, tri, last):
                    tk = i * 128
                    sl = slice(tk, tk + 128)
                    zt, lw, cl2, E, BK = zt_[:, q], lw_[:, q], cl2_[:, q], E_[:, q], BK_[:, q]
                    pz, pzk = self.bank()
                    self.mm(pz[:, 0:256], wlt[0:64, tk:tk + 128], lorab[0:64, d * 256:(d + 1) * 256], True, True, [], [pzk])
                    yield
                    self.tt("dve", zt, pz[:, 0:256], w0b[:, d * 256:(d + 1) * 256], ALU.add, [pzk], [("zt", q)])
                    yield
                    self.act(lw, zt, AF.Sigmoid, [("zt", q)], [("lw", q)])
                    yield
                    self.ts("dve", lw, lw, NEG_E, None, ALU.mult, None, [("lw", q)], [("lw", q)])
                    yield
                    pcl, pclk = self.bank()
                    for c in range(2):
                        self.mm(pcl[:, c * 256:(c + 1) * 256], lw[:, c * 128:(c + 1) * 128], tri, True, True, [("lw", q)], [pclk])
                    yield
                    V3 = pcl[:, 0:512].rearrange("p (c x) -> p c x", c=2)
                    incl, excl = V3[:, :, 0:128], V3[:, :, 128:256]
                    self.cp("dve", cl2.unsqueeze(2), V3[:, :, last:last + 1], [pclk], [("cl2", q)])
                    self.act(E[:, 0], excl, AF.Exp, [pclk], [("E", q, 0)])
                    yield
                    self.act(E[:, 1], incl, AF.Exp, [pclk], [("E", q, 1)])
                    self.tt("dve", KR[:, q, :, 0, :], kk[:, :, sl], E[:, 0], ALU.mult, [("E", q, 0)], [("KR", q, 0)])
                    yield
                    self.act(E[:, 2], incl, AF.Exp, [pclk], [("E", q, 2)], scale=-1.0)
                    self.tt("pool", KR[:, q, :, 1, :], dr[:, :, sl], E[:, 1], ALU.mult, [("E", q, 1)], [("KR", q, 1)])
                    yield
                    for c in range(2):
                        self.act(E[:, 3, c, :], V3[:, c, 0:128], AF.Exp, [pclk, ("cl2", q)], [("E", q, 3, c)], scale=-1.0, bias=cl2[:, c:c + 1])
                    self.tt("dve", BK[:, 0], ball[:, :, sl], E[:, 2], ALU.mult, [("E", q, 2)], [("BK", q, 0)])
                    self.tt("pool", BK[:, 1], kd[:, :, sl], E[:, 2], ALU.mult, [("E", q, 2)], [("BK", q, 1)])
                    yield
                    self.act(gam[:, q, :], cl2, AF.Exp, [("cl2", q)], [("gam", q)])
                    self.tt("dve", BK[:, 2], ball[:, :, sl], E[:, 3], ALU.mult, [("E", q, 3, 0), ("E", q, 3, 1)], [("BK", q, 2)])
                    self.tt("pool", BK[:, 3], kd[:, :, sl], E[:, 3], ALU.mult, [("E", q, 3, 0), ("E", q, 3, 1)], [("BK", q, 3)])
                    for h in (0, 2, 1, 3):
                        c, pb = h // 2, (h % 2) * 64
                        pmn, pmnk = self.bank()
                        rhs = KR[pb:pb + 64, q, c].rearrange("p a t -> p (a t)")
                        self.mm(pmn[:, 0:256], BK[pb:pb + 64, 0, c, :], rhs, True, True, [("BK", q, 0), ("KR", q, 0), ("KR", q, 1)], [pmnk])
                        self.mm(pmn[:, 256:512], BK[pb:pb + 64, 1, c, :], rhs, True, True, [("BK", q, 1), ("KR", q, 0), ("KR", q, 1)], [pmnk])
                        yield
                        self.tt("dve", MN[:, q, h].rearrange("p a t -> p (a t)"), pmn[:, 0:512], m4[:, d, :], ALU.mult, [pmnk], [("MN", q, h)])
                    for par in range(2):
                        pmt, pmtk = self.bank()
                        pb = par * 64
                        for c in range(2):
                            self.mm(pmt[:, c * 128:(c + 1) * 128], KR[pb:pb + 64, q, c, 0, :], BK[pb:pb + 64, 0, c, :], True, True,
                                    [("BK", q, 0), ("KR", q, 0)], [pmtk])
                        yield
                        self.tt("dve", NTb[:, q, 0, par::2, :], pmt[:, 0:256].rearrange("p (c t) -> p c t", c=2),
                                mT[:, d, :].unsqueeze(1).to_broadcast([128, 2, 128]), ALU.mult, [pmtk],
                                [("NT", q, 0, par), ("NT", q, 0, par + 2)])
                    ptb, ptbk = self.bank()
                    p16 = ptb[:].bitcast(BF16)
                    for w_ in range(2):
                        for c in range(2):
                            self.tr(p16[:, w_ * 256 + c * 128:w_ * 256 + (c + 1) * 128], BK[:, 2 + w_, c, :], self.identb[:],
                                    [("BK", q, 2 + w_)], [ptbk])
                    yield
                    self.cp("act", BKt[:, q].rearrange("p w x -> p (w x)"), p16[:, 0:512], [ptbk], [("BKt", q)])
                    H4 = range(4)
                    self.cp("act", Nb[:, q, 0], MN[:, q, :, 0, :], [("MN", q, h) for h in H4], [("N", q, 0, h) for h in H4])
                    yield
                    yield from stable_inverse(q)

                def stable_inverse(q):
                    H = range(4)
                    fl = lambda a: a.rearrange("p h t -> p (h t)")
                    N0, N0T = Nb[:, q, 0], NTb[:, q, 0]
                    W = lambda k: WK[:, q, k]
                    kN0 = [("N", q, 0, h) for h in H]
                    kNT0 = [("NT", q, 0, h) for h in H]
                    kW = lambda k: [("WK", q, k)]
                    mb = lambda m: bm[:, m, :].unsqueeze(1).to_broadcast([128, 4, 128])
                    idb = self.identb[:].unsqueeze(1).to_broadcast([128, 4, 128])
                    self.tt("dve", W(0), N0, mb(0), ALU.mult, kN0 + ["bm"], kW(0))
                    self.tt("pool", W(1), N0T, mb(0), ALU.mult, kNT0 + ["bm"], kW(1))
                    yield
                    self.tt("dve", W(4), W(0), idb, ALU.add, kW(0), kW(4))
                    self.tt("pool", W(5), W(1), idb, ALU.add, kW(1), kW(5))
                    yield
                    ca, cb = (0, 1), (2, 3)
                    for lvl in range(2):
                        cur, nxt = (ca, cb) if lvl % 2 == 0 else (cb, ca)
                        pn, pnk = self.bank()
                        for h in H:
                            self.mm(pn[:, h * 128:(h + 1) * 128], W(cur[1])[:, h, :], W(cur[0])[:, h, :], True, True, kW(cur[0]) + kW(cur[1]), [pnk])
                        pnt, pntk = self.bank()
                        for h in H:
                            self.mm(pnt[:, h * 128:(h + 1) * 128], W(cur[0])[:, h, :], W(cur[1])[:, h, :], True, True, kW(cur[0]) + kW(cur[1]), [pntk])
                        yield
                        self.cp("act", fl(W(nxt[0])), pn[:, 0:512], [pnk], kW(nxt[0]))
                        self.cp("dve", fl(W(nxt[1])), pnt[:, 0:512], [pntk], kW(nxt[1]))
                        yield
                        pp, ppk = self.bank()
                        for h in H:
                            self.mm(pp[:, h * 128:(h + 1) * 128], self.identb[:], W(4)[:, h, :], True, False, kW(4), [ppk])
                            self.mm(pp[:, h * 128:(h + 1) * 128], W(nxt[1])[:, h, :], W(4)[:, h, :], False, True, kW(nxt[1]) + kW(4), [ppk])
                        ppt, pptk = self.bank()
                        for h in H:
                            self.mm(ppt[:, h * 128:(h + 1) * 128], self.identb[:], W(5)[:, h, :], True, False, kW(5), [pptk])
                            self.mm(ppt[:, h * 128:(h + 1) * 128], W(nxt[0])[:, h, :], W(5)[:, h, :], False, True, kW(nxt[0]) + kW(5), [pptk])
                        yield
                        self.cp("act", fl(W(4)), pp[:, 0:512], [ppk], kW(4))
                        self.cp("act", fl(W(5)), ppt[:, 0:512], [pptk], kW(5))
                        yield
                    for m in (1, 2, 3, 4):
                        lastm = m == 4
                        p1, p1k = self.bank()
                        for h in H:
                            self.mm(p1[:, h * 128:(h + 1) * 128], N0T[:, h, :], W(4)[:, h, :], True, True, kNT0 + kW(4), [p1k])
                        if not lastm:
                            q1, q1k = self.bank()
                            for h in H:
                                self.mm(q1[:, h * 128:(h + 1) * 128], N0[:, h, :], W(5)[:, h, :], True, True, kN0 + kW(5), [q1k])
                        yield
                        m4b = bm[:, m, :].unsqueeze(1).to_broadcast([128, 4, 128])
                        self.tt("dve", W(2), p1[:, 0:512].rearrange("p (h t) -> p h t", h=4), m4b, ALU.mult, [p1k, "bm"], kW(2))
                        if not lastm:
                            self.tt("dve", W(3), q1[:, 0:512].rearrange("p (h t) -> p h t", h=4), m4b, ALU.mult, [q1k, "bm"], kW(3))
                        yield
                        pa, pak = self.bank()
                        for h in H:
                            self.mm(pa[:, h * 128:(h + 1) * 128], self.identb[:], W(4)[:, h, :], True, False, kW(4), [pak])
                            self.mm(pa[:, h * 128:(h + 1) * 128], W(5)[:, h, :], W(2)[:, h, :], False, True, kW(5) + kW(2), [pak])
                        if not lastm:
                            pb_, pbk = self.bank()
                            for h in H:
                                self.mm(pb_[:, h * 128:(h + 1) * 128], self.identb[:], W(5)[:, h, :], True, False, kW(5), [pbk])
                                self.mm(pb_[:, h * 128:(h + 1) * 128], W(4)[:, h, :], W(3)[:, h, :], False, True, kW(4) + kW(3), [pbk])
                        yield
                        self.cp("act", fl(W(4)), pa[:, 0:512], [pak], kW(4))
                        if not lastm:
                            self.cp("act", fl(W(5)), pb_[:, 0:512], [pbk], kW(5))
                        yield

                def chain(i, q, d):
                    tk = i * 128
                    sl = slice(tk, tk + 128)
                    pX, pXk = self.bank()
                    for h in range(4):
                        c, h2 = h // 2, h % 2
                        self.mm(pX[:, h * 64:(h + 1) * 64], KR[:, q, c, 0, :], Zbd[:, c, h2 * 64:(h2 + 1) * 64], True, False,
                                [("KR", q, 0), ("Zbd", 0), ("Zbd", 1)], [pXk])
                        self.mm(pX[:, h * 64:(h + 1) * 64], MN[:, q, h, 2, :], vtm[:, i, h * 64:(h + 1) * 64], False, True, [("MN", q, h)], [pXk])
                    yield None
                    self.act(Xn[:], pX[:, 0:256], AF.Copy, [pXk], ["Xn"], scale=-1.0)
                    yield None
                    pU, pUk = self.bank()
                    for h in range(4):
                        self.mm(pU[:, h * 64:(h + 1) * 64], WK[:, q, 4, h, :], Xn[:, h * 64:(h + 1) * 64], True, True, [("WK", q, 4), "Xn"], [pUk])
                    yield None
                    self.cp("act", UT[:], pU[:, 0:256], [pUk], ["UT"])
                    yield None
                    pO, pOk = self.bank()
                    for h in range(4):
                        c, h2 = h // 2, h % 2
                        pb = h2 * 64
                        dst = pO[pb:pb + 64, c * 128:(c + 1) * 128]
                        self.mm(dst, Zbd[:, c, pb:pb + 64], KR[:, q, c, 1, :], True, False, [("KR", q, 1), ("Zbd", 0), ("Zbd", 1)], [pOk])
                        self.mm(dst, UT[:, h * 64:(h + 1) * 64], MN[:, q, h, 1, :], False, False, ["UT", ("MN", q, h)], [pOk])
                        self.mm(dst, vtm[:, i, h * 64:(h + 1) * 64], MN[:, q, h, 3, :], False, True, [("MN", q, h)], [pOk])
                    pZ, pZk = self.bank()
                    for h in range(4):
                        c, pb = h // 2, (h % 2) * 64
                        dst = pZ[pb:pb + 64, c * 64:(c + 1) * 64]
                        self.mm(dst, BKt[:, q, 0, h * 64:(h + 1) * 64], UT[:, h * 64:(h + 1) * 64], True, False, [("BKt", q), "UT"], [pZk])
                        self.mm(dst, BKt[:, q, 1, h * 64:(h + 1) * 64], vtm[:, i, h * 64:(h + 1) * 64], False, True, [("BKt", q)], [pZk])
                    yield None
                    for c in range(2):
                        self.stt("dve", Zrun[:, c, :], Zrun[:, c, :], gam[:, q, c:c + 1], pZ[:, c * 64:(c + 1) * 64], ALU.mult, ALU.add,
                                 [pZk, ("gam", q), ("Zrun", 0), ("Zrun", 1)], [("Zrun", 0), ("Zrun", 1)])
                    yield None
                    for par in range(2):
                        pb = par * 64
                        self.cp("act", Zbd[pb:pb + 64, :, pb:pb + 64], Zrun[pb:pb + 64, :, :], [("Zrun", par)], [("Zbd", par)])
                    yield (pO, pOk)

                def tail(i, q, d, pO, pOk):
                    tk = i * 128
                    sl = slice(tk, tk + 128)
                    if d == 0:
                        self.cp("act", HB[0][:, :, sl], pO[:, 0:256].rearrange("p (c t) -> p c t", c=2), [pOk], [("of_", i)])
                        yield
                    else:
                        yield from self.head_norm_gen(l, pO, pOk, of_[:, q], ob_[:, q], hst_[:, q], blkb, epsg, f"rw_gn_{l}", HB[2], 6, tk,
                                                      HB[3][:, :, sl], HB[0][:, :, sl], ("rw", q))

                def run_interleaved(gens):
                    gens = list(gens)
                    while gens:
                        for g in list(gens):
                            try:
                                next(g)
                            except StopIteration:
                                gens.remove(g)

                for d in range(2):
                    for win in SEQ_WINS:
                        (s, toff, n, hc, gt) = win
                        for c in range(2):
                            pa, pak = self.bank()
                            self.mm(pa[:, 0:n], lorab[64:128, 512 + d * 256 + c * 128:512 + d * 256 + (c + 1) * 128], alt[64:128, gt:gt + n],
                                    True, True, [], [pak])
                            self.act(HB[1][:, c, gt:gt + n], pa[:, 0:n], AF.Sigmoid, [pak], [("ad", c, gt)], bias=self.V(f"a0_{d}_{l}", c))
                    P.barrier()
                    self.tt("dve", ball, kk[:], HB[1], ALU.mult, [], ["ball"])
                    for c in range(2):
                        self.ts("dve", HB[1][:, c, :], HB[1][:, c, :], self.V(f"ka_{l}", c), sc[:, 4 + c:5 + c], ALU.mult, ALU.add, ["ball"], [("kd", c)])
                    self.tt("dve", HB[1], HB[1], dk[:], ALU.mult, [("kd", 0), ("kd", 1)], ["kd_all"])
                    P.barrier()
                    kd = HB[1]
                    tri = self.cstt[:, CST["tri_i_f"][0]:CST["tri_i_f"][0] + 256] if d == 0 else \
                        self.cstt[:, CST["tri_i_b"][0]:CST["tri_i_b"][0] + 256]
                    last = 127 if d == 0 else 0
                    for (s, i0, nch) in ((0, 0, 2), (1, 2, 2), (2, 4, 16)):
                        if s == 2:
                            for c in range(2):
                                self.dma("sp", zin[:, c, :], self.srwkv_in[l, d, 2 * c:2 * c + 2].rearrange("h v k -> (h v) k"), (), [("zin", c)])
                            for par in range(2):
                                pt, ptk = self.bank()
                                pb = par * 64
                                for c in range(2):
                                    self.tr(pt[0:64, c * 64:(c + 1) * 64], zin[pb:pb + 64, c, :], self.ident[pb:pb + 64, pb:pb + 64],
                                            [("zin", c)], [ptk])
                                self.cp("dve", Zrun[pb:pb + 64, :, :], pt[0:64, 0:128].rearrange("p (c v) -> p c v", c=2), [ptk], [("Zrun", par)])
                        else:
                            self.memset("pool", Zrun[:], 0.0, [("Zrun", 0), ("Zrun", 1)])
                        for par in range(2):
                            pb = par * 64
                            self.cp("act", Zbd[pb:pb + 64, :, pb:pb + 64], Zrun[pb:pb + 64, :, :], [("Zrun", par)], [("Zbd", par)])
                        order = list(range(i0, i0 + nch)) if d == 0 else list(range(i0 + nch - 1, i0 - 1, -1))
                        LOOK = 6
                        pend = None
                        for a in range(0, nch, 2):
                            pair = order[a:a + 2]
                            gens = pend if pend is not None else [early(i, q, d, kd, tri, last) for q, i in enumerate(pair)]
                            run_interleaved(gens)
                            nxt_pair = order[a + 2:a + 4]
                            pend = [early(i, q, d, kd, tri, last) for q, i in enumerate(nxt_pair)] if nxt_pair else None
                            ahead = [LOOK] * len(pend) if pend else []
                            pos = []
                            for q, i in enumerate(pair):
                                cg_ = chain(i, q, d)
                                while True:
                                    r_ = next(cg_)
                                    for gi_ in range(len(ahead)):
                                        if ahead[gi_] > 0:
                                            next(pend[gi_])
                                            ahead[gi_] -= 1
                                    if r_ is not None:
                                        pos.append(r_)
                                        break
                            for gi_ in range(len(ahead)):
                                while ahead[gi_] > 0:
                                    next(pend[gi_])
                                    ahead[gi_] -= 1
                            run_interleaved([tail(i, q, d, *pos[q]) for q, i in enumerate(pair)])
                        if s < 2:
                            for par in range(2):
                                pt, ptk = self.bank()
                                pb = par * 64
                                for c in range(2):
                                    self.tr(pt[0:64, c * 64:(c + 1) * 64], Zrun[pb:pb + 64, c, :], self.ident[pb:pb + 64, pb:pb + 64],
                                            [("Zrun", par)], [ptk])
                                for c in range(2):
                                    self.cp("dve", zo[:, 2 * c + par, :], pt[0:64, c * 64:(c + 1) * 64], [ptk], [("zo", 2 * c + par)])
                            self.out_toks.append(self.dma("sp", self.orwkv_out[s, l, d].rearrange("h v k -> v h k"), zo[:],
                                                          [("zo", h) for h in range(4)], ()))
                    P.barrier()
            for pc in (0, 257, 514, 2563):
                self.memset("pool", self.hbuf[:, :, pc:pc + 1], 0.0, [("hpad", pc)])
            P.barrier()

    B.mixer_rwkv = mixer_rwkv

    _old = B.head_norm_out

    def head_norm_out(self, l, po, pok, of, ob, hst, blkb, epsr, eps, gname, gate, mix0, tk, bonus, add=None):
        if add is None:
            return _old(self, l, po, pok, of, ob, hst, blkb, epsr, eps, gname, gate, mix0, tk, bonus)
        self.tt("dve", of[:], po[:, 0:256].rearrange("p (c t) -> p c t", c=2), add, ALU.add, [pok], ["of"])
        return _old(self, l, None, None, of, ob, hst, blkb, epsr, eps, gname, gate, mix0, tk, bonus)

    B.head_norm_out = head_norm_out


_install_rwkv()


_NC_CACHE = {}


def kernel(**inputs):
    inp = {k: np.asarray(v) for k, v in inputs.items()}
    if "nc" not in _NC_CACHE:
        _NC_CACHE["nc"] = Builder().build()
    nc = _NC_CACHE["nc"]
    shared = host_shared(inp)
    in_maps = []
    for core in range(8):
        m = dict(shared)
        m.update(host_inputs(inp, core))
        in_maps.append(m)
    res = run_bass_kernel_spmd(nc, in_maps, core_ids=list(range(8)))
    r = res.results
    y_prompt = np.concatenate([np.asarray(r[c]["y"])[0:512].reshape(2, 256, D) for c in range(8)], axis=0)
    y_sample = np.stack([np.asarray(r[2 * b]["y"])[512:2560] for b in range(4)], axis=0)
    nak = np.concatenate([np.asarray(r[c]["nak"]).reshape(2, DEPTH, 256, 4, 64) for c in range(8)], axis=0)
    nav = np.concatenate([np.asarray(r[c]["nav"]).reshape(2, DEPTH, 256, 4, 64) for c in range(8)], axis=0)
    oret = np.concatenate([np.asarray(r[c]["oret"]) for c in range(8)], axis=0)
    orwkv = np.concatenate([np.asarray(r[c]["orwkv"]) for c in range(8)], axis=0)
    return (y_prompt.astype(np.float32), y_sample.astype(np.float32), nak.astype(np.float32), nav.astype(np.float32),
            oret.astype(np.float32), orwkv.astype(np.float32))
```
